# Optimizing a Trainium2 kernel written in Bass

```python
import jax, jax.numpy as jnp
from jax import lax
import numpy as np

D_MODEL = 1024
BATCH = 16
SEQ = 4096
DEPTH = 4

CTX_LEN = 256
GRID_W = 64
EPS = 1e-6
NEG_INF = -1e30

HEAD_DIM = 64
ATTN_HEADS = D_MODEL // 128
ATTN_KV_HEADS = ATTN_HEADS // 4
ATTN_GROUP = ATTN_HEADS // ATTN_KV_HEADS
ATTN_WIDTH = ATTN_HEADS * HEAD_DIM
KV_WIDTH = ATTN_KV_HEADS * HEAD_DIM
WINDOW = 128
BLOCK = 128
ROPE_BASE = 10000.0
ROPE_FREQS = HEAD_DIM // 4

SSM_WIDTH = D_MODEL // 2
SSM_GROUP = 16
SSM_GROUPS = SSM_WIDTH // SSM_GROUP
SSM_STATE = 64
DT_MIN = 1e-3
DT_MAX = 1e-1

EVEN_IN = 2 * ATTN_WIDTH + 2 * KV_WIDTH + 2 * SSM_WIDTH
EVEN_MIX = ATTN_WIDTH + SSM_WIDTH

POOL_WIDTH = D_MODEL
POOL_WINDOWS = (2, 4, 8, 16)
POOL_GROUP = POOL_WIDTH // len(POOL_WINDOWS)

kernel_name = 'hybrid_swa_s5_pool_prefix_dit'


def rmsnorm(x, g):
    xf = x.astype(jnp.float32)
    y = xf * lax.rsqrt(jnp.mean(xf * xf, axis=-1, keepdims=True) + EPS)
    return (y * g.astype(jnp.float32)).astype(x.dtype)


def axial_rope_tables(n_tokens):
    rows = n_tokens // GRID_W
    row = jnp.repeat(jnp.arange(rows, dtype=jnp.float32), GRID_W)
    col = jnp.tile(jnp.arange(GRID_W, dtype=jnp.float32), rows)
    inv_freq = ROPE_BASE ** (-jnp.arange(ROPE_FREQS, dtype=jnp.float32) / ROPE_FREQS)
    ang = jnp.stack([row[:, None] * inv_freq, col[:, None] * inv_freq], axis=1)
    ang = ang[:, None]
    return jnp.cos(ang), jnp.sin(ang)


def apply_axial_rope(x, cos, sin):
    b, l, h, _ = x.shape
    xr = x.astype(jnp.float32).reshape(b, l, h, 2, 2, ROPE_FREQS)
    x1, x2 = xr[..., 0, :], xr[..., 1, :]
    out = jnp.stack([x1 * cos - x2 * sin, x2 * cos + x1 * sin], axis=-2)
    return out.reshape(b, l, h, HEAD_DIM).astype(x.dtype)


def window_attention(q, k, v, kc, vc, sink):
    b, l, hkv, g, dh = q.shape
    nb = l // BLOCK
    scale = dh ** -0.5
    qb = q.reshape(b, nb, BLOCK, hkv, g, dh)
    pad = ((0, 0), (BLOCK, BLOCK), (0, 0), (0, 0))
    kp = jnp.pad(k, pad).reshape(b, nb + 2, BLOCK, hkv, dh)
    vp = jnp.pad(v, pad).reshape(b, nb + 2, BLOCK, hkv, dh)
    kb = jnp.concatenate([kp[:, :-2], kp[:, 1:-1], kp[:, 2:]], axis=2)
    vb = jnp.concatenate([vp[:, :-2], vp[:, 1:-1], vp[:, 2:]], axis=2)
    s_loc = jnp.einsum('bnqhgd,bnkhd->bnhgqk', qb, kb).astype(jnp.float32) * scale
    s_ctx = jnp.einsum('bnqhgd,bkhd->bnhgqk', qb, kc).astype(jnp.float32) * scale
    qpos = np.arange(nb)[:, None] * BLOCK + np.arange(BLOCK)[None, :]
    kpos = (np.arange(nb)[:, None] - 1) * BLOCK + np.arange(3 * BLOCK)[None, :]
    valid = ((np.abs(qpos[:, :, None] - kpos[:, None, :]) <= WINDOW)
             & (kpos[:, None, :] >= 0) & (kpos[:, None, :] < l))
    s_loc = jnp.where(valid[None, :, None, None], s_loc, NEG_INF)
    s_sink = jnp.broadcast_to(sink.astype(jnp.float32)[None, None, :, :, None, None],
                              s_loc.shape[:-1] + (1,))
    p = jax.nn.softmax(jnp.concatenate([s_loc, s_ctx, s_sink], axis=-1), axis=-1)
    n_loc = 3 * BLOCK
    n_ctx = kc.shape[1]
    p_loc = p[..., :n_loc].astype(v.dtype)
    p_ctx = p[..., n_loc:n_loc + n_ctx].astype(v.dtype)
    out = (jnp.einsum('bnhgqk,bnkhd->bnqhgd', p_loc, vb)
           + jnp.einsum('bnhgqk,bkhd->bnqhgd', p_ctx, vc))
    return out.reshape(b, l, hkv * g * dh)


def context_attention(qc, kc, vc, sink):
    b, lc, hkv, g, dh = qc.shape
    s = jnp.einsum('bqhgd,bkhd->bhgqk', qc, kc).astype(jnp.float32) * dh ** -0.5
    s_sink = jnp.broadcast_to(sink.astype(jnp.float32)[None, :, :, None, None], s.shape[:-1] + (1,))
    p = jax.nn.softmax(jnp.concatenate([s, s_sink], axis=-1), axis=-1)[..., :lc].astype(vc.dtype)
    return jnp.einsum('bhgqk,bkhd->bqhgd', p, vc).reshape(b, lc, hkv * g * dh)


def s5_discretize(a_re, a_im, log_dt, b_re, b_im):
    lam = lax.complex(a_re.astype(jnp.float32), a_im.astype(jnp.float32))
    dt = jnp.exp(log_dt.astype(jnp.float32))[:, None]
    a_bar = jnp.exp(lam * dt)
    bmat = lax.complex(b_re.astype(jnp.float32), b_im.astype(jnp.float32))
    b_bar = ((a_bar - 1.0) / lam)[..., None] * bmat
    return a_bar, b_bar


def _scan_combine(left, right):
    a_l, h_l = left
    a_r, h_r = right
    return a_l * a_r, a_r * h_l + h_r


def diag_scan(a_bar, bu, h0):
    if h0 is not None:
        bu = bu.at[:, 0].add(a_bar * h0)
    a = jnp.broadcast_to(a_bar, bu.shape)
    _, h = lax.associative_scan(_scan_combine, (a, bu), axis=1)
    return h


def s5_branch(u, uc, a_re, a_im, log_dt, b_re, b_im, c_re, c_im, d_skip, glu_w, glu_b, need_ctx):
    b, l, _ = u.shape
    lc = uc.shape[1]
    ul = u.astype(jnp.float32).reshape(b, l, SSM_GROUPS, SSM_GROUP).astype(jnp.complex64)
    ucg = uc.astype(jnp.float32).reshape(b, lc, SSM_GROUPS, SSM_GROUP).astype(jnp.complex64)
    d = d_skip.astype(jnp.float32)
    y = u.astype(jnp.float32) * d
    yc = uc.astype(jnp.float32) * d if need_ctx else None
    for direction in range(2):
        a_bar, b_bar = s5_discretize(a_re[direction], a_im[direction], log_dt[direction],
                                     b_re[direction], b_im[direction])
        cmat = lax.complex(c_re[direction].astype(jnp.float32), c_im[direction].astype(jnp.float32))
        bu = jnp.einsum('blgc,gpc->blgp', ul, b_bar)
        buc = jnp.einsum('blgc,gpc->blgp', ucg, b_bar)
        if direction == 1:
            bu, buc = bu[:, ::-1], buc[:, ::-1]
        hc = diag_scan(a_bar, buc, None)
        h = diag_scan(a_bar, bu, hc[:, -1])
        if direction == 1:
            h, hc = h[:, ::-1], hc[:, ::-1]
        y = y + jnp.real(jnp.einsum('blgp,gcp->blgc', h, cmat)).reshape(b, l, SSM_WIDTH)
        if need_ctx:
            yc = yc + jnp.real(jnp.einsum('blgp,gcp->blgc', hc, cmat)).reshape(b, lc, SSM_WIDTH)
    gw = glu_w.astype(jnp.float32)
    gb = glu_b.astype(jnp.float32)

    def glu(z):
        z = jax.nn.gelu(z)
        return z * jax.nn.sigmoid(z @ gw + gb)

    out = glu(y).astype(u.dtype)
    out_c = glu(yc).astype(u.dtype) if need_ctx else None
    return out, out_c


def attn_ssm_mixer(a, ac, cos, sin, w_in, w_out, sink, a_re, a_im, log_dt, b_re, b_im,
                   c_re, c_im, d_skip, glu_w, glu_b, need_ctx):
    b, l, _ = a.shape
    lc = ac.shape[1]
    cuts = [int(v) for v in np.cumsum([ATTN_WIDTH, KV_WIDTH, KV_WIDTH, ATTN_WIDTH, SSM_WIDTH])]
    q, k, v, g_attn, u, g_ssm = jnp.split(a @ w_in, cuts, axis=-1)
    qc, kc, vc, g_attn_c, uc, g_ssm_c = jnp.split(ac @ w_in, cuts, axis=-1)
    q = apply_axial_rope(q.reshape(b, l, ATTN_HEADS, HEAD_DIM), cos, sin)
    q = q.reshape(b, l, ATTN_KV_HEADS, ATTN_GROUP, HEAD_DIM)
    k = apply_axial_rope(k.reshape(b, l, ATTN_KV_HEADS, HEAD_DIM), cos, sin)
    v = v.reshape(b, l, ATTN_KV_HEADS, HEAD_DIM)
    kc = kc.reshape(b, lc, ATTN_KV_HEADS, HEAD_DIM)
    vc = vc.reshape(b, lc, ATTN_KV_HEADS, HEAD_DIM)
    sink = sink.reshape(ATTN_KV_HEADS, ATTN_GROUP)
    o_attn = window_attention(q, k, v, kc, vc, sink) * jax.nn.silu(g_attn)
    o_ssm, o_ssm_c = s5_branch(u, uc, a_re, a_im, log_dt, b_re, b_im, c_re, c_im,
                               d_skip, glu_w, glu_b, need_ctx)
    y = jnp.concatenate([o_attn, o_ssm * jax.nn.silu(g_ssm)], axis=-1) @ w_out
    yc = None
    if need_ctx:
        qc = qc.reshape(b, lc, ATTN_KV_HEADS, ATTN_GROUP, HEAD_DIM)
        o_attn_c = context_attention(qc, kc, vc, sink) * jax.nn.silu(g_attn_c)
        yc = jnp.concatenate([o_attn_c, o_ssm_c * jax.nn.silu(g_ssm_c)], axis=-1) @ w_out
    return y, yc


def multiscale_pool(u):
    t_len = u.shape[1]
    uf = u.astype(jnp.float32)
    cs = jnp.pad(jnp.cumsum(uf, axis=1), ((0, 0), (1, 0), (0, 0)))
    pos = np.arange(t_len)
    outs = []
    for gi, w in enumerate(POOL_WINDOWS):
        r = w // 2
        lo = np.clip(pos - r, 0, t_len)
        hi = np.clip(pos + r + 1, 0, t_len)
        inv_cnt = jnp.asarray(1.0 / (hi - lo), dtype=jnp.float32)[None, :, None]
        sl = slice(gi * POOL_GROUP, (gi + 1) * POOL_GROUP)
        csg = cs[..., sl]
        outs.append((csg[:, hi] - csg[:, lo]) * inv_cnt - uf[..., sl])
    return jnp.concatenate(outs, axis=-1)


def pool_mixer(a, w_in, w_out, pool_w, pool_scale):
    u, gate = jnp.split(a @ w_in, 2, axis=-1)
    b, t, _ = u.shape
    p = multiscale_pool(u).reshape(b, t, len(POOL_WINDOWS), POOL_GROUP)
    p = jnp.einsum('btgc,gcd->btgd', p, pool_w.astype(jnp.float32)).reshape(b, t, POOL_WIDTH)
    p = (p * pool_scale.astype(jnp.float32)).astype(a.dtype)
    return (p * jax.nn.silu(gate)) @ w_out


def setup_inputs(seed: int = 0) -> dict:
    key = jax.random.key(seed)
    ks = iter(jax.random.split(key, 32))
    n_even = (DEPTH + 1) // 2
    n_odd = DEPTH // 2

    def nrm(shape, scale):
        return jax.random.normal(next(ks), shape, jnp.float32) * scale

    n_idx = jnp.arange(SSM_STATE, dtype=jnp.float32)
    ssm_shape = (n_even, 2, SSM_GROUPS, SSM_STATE)
    return {
        'x': nrm((BATCH, SEQ, D_MODEL), 1.0),
        'c': nrm((BATCH, D_MODEL), 1.0),
        'ctx': nrm((BATCH, CTX_LEN, D_MODEL), 1.0),
        'c_ctx': nrm((D_MODEL,), 1.0),
        'ada_w': nrm((DEPTH, D_MODEL, 3 * D_MODEL), 0.5 * D_MODEL ** -0.5),
        'ada_b': nrm((DEPTH, 3 * D_MODEL), 0.01),
        'norm_g': 1.0 + nrm((DEPTH, D_MODEL), 0.02),
        'even_w_in': nrm((n_even, D_MODEL, EVEN_IN), D_MODEL ** -0.5),
        'even_w_out': nrm((n_even, EVEN_MIX, D_MODEL), EVEN_MIX ** -0.5),
        'attn_sink': nrm((n_even, ATTN_HEADS), 0.5),
        'ssm_a_re': -0.5 + nrm(ssm_shape, 0.01),
        'ssm_a_im': jnp.pi * n_idx + nrm(ssm_shape, 0.01),
        'ssm_log_dt': jax.random.uniform(next(ks), (n_even, 2, SSM_GROUPS), jnp.float32,
                                         np.log(DT_MIN), np.log(DT_MAX)),
        'ssm_b_re': nrm((n_even, 2, SSM_GROUPS, SSM_STATE, SSM_GROUP), (2 * SSM_GROUP) ** -0.5),
        'ssm_b_im': nrm((n_even, 2, SSM_GROUPS, SSM_STATE, SSM_GROUP), (2 * SSM_GROUP) ** -0.5),
        'ssm_c_re': nrm((n_even, 2, SSM_GROUPS, SSM_GROUP, SSM_STATE), (2 * SSM_STATE) ** -0.5),
        'ssm_c_im': nrm((n_even, 2, SSM_GROUPS, SSM_GROUP, SSM_STATE), (2 * SSM_STATE) ** -0.5),
        'ssm_d': nrm((n_even, SSM_WIDTH), 0.5),
        'glu_w': nrm((n_even, SSM_WIDTH, SSM_WIDTH), SSM_WIDTH ** -0.5),
        'glu_b': nrm((n_even, SSM_WIDTH), 0.01),
        'odd_w_in': nrm((n_odd, D_MODEL, 2 * POOL_WIDTH), D_MODEL ** -0.5),
        'odd_w_out': nrm((n_odd, POOL_WIDTH, D_MODEL), POOL_WIDTH ** -0.5),
        'pool_w': nrm((n_odd, len(POOL_WINDOWS), POOL_GROUP, POOL_GROUP), POOL_GROUP ** -0.5),
        'pool_scale': 1.0 + nrm((n_odd, POOL_WIDTH), 0.02),
        'final_g': 1.0 + nrm((D_MODEL,), 0.02),
    }


def reference(x, c, ctx, c_ctx, ada_w, ada_b, norm_g, even_w_in, even_w_out, attn_sink,
              ssm_a_re, ssm_a_im, ssm_log_dt, ssm_b_re, ssm_b_im, ssm_c_re, ssm_c_im, ssm_d,
              glu_w, glu_b, odd_w_in, odd_w_out, pool_w, pool_scale, final_g):
    cos, sin = axial_rope_tables(x.shape[1])
    h, hc = x, ctx
    s_lat = jax.nn.silu(c)
    s_ctx = jax.nn.silu(c_ctx)
    for i in range(DEPTH):
        need_ctx = i < DEPTH - 1
        shift, scale, gate = jnp.split((s_lat @ ada_w[i] + ada_b[i])[:, None, :], 3, axis=-1)
        shift_c, scale_c, gate_c = jnp.split(s_ctx @ ada_w[i] + ada_b[i], 3, axis=-1)
        a = rmsnorm(h, norm_g[i]) * (1.0 + scale) + shift
        ac = rmsnorm(hc, norm_g[i]) * (1.0 + scale_c) + shift_c
        j = i // 2
        if i % 2 == 0:
            y, yc = attn_ssm_mixer(a, ac, cos, sin, even_w_in[j], even_w_out[j], attn_sink[j],
                                   ssm_a_re[j], ssm_a_im[j], ssm_log_dt[j], ssm_b_re[j], ssm_b_im[j],
                                   ssm_c_re[j], ssm_c_im[j], ssm_d[j], glu_w[j], glu_b[j], need_ctx)
        else:
            y = pool_mixer(a, odd_w_in[j], odd_w_out[j], pool_w[j], pool_scale[j])
            yc = pool_mixer(ac, odd_w_in[j], odd_w_out[j], pool_w[j], pool_scale[j]) if need_ctx else None
        h = h + gate * y
        if need_ctx:
            hc = hc + gate_c * yc
    return rmsnorm(h, final_g)
```

```python
import numpy as np
from contextlib import ExitStack
import concourse.bass as bass
import concourse.mybir as mybir
from concourse.bass_utils import run_bass_kernel_spmd

F32 = mybir.dt.float32
BF16 = mybir.dt.bfloat16
AF = mybir.ActivationFunctionType
ALU = mybir.AluOpType

ENGS = ['pe', 'act', 'dve', 'pool', 'sp']
import os
SKIPK = int(os.environ.get('SKIPK', '0'))
PA = int(os.environ.get('PA', '0'))
PIPE = int(os.environ.get('PIPE', '1'))
STAGE = 0


class Tok:
    __slots__ = ('lw', 'rd')

    def __init__(self):
        self.lw = None
        self.rd = {}


class Prog:
    def __init__(self, nc, stack, ndq=12):
        self.nc = nc
        self.q = {e: [] for e in ENGS}
        self.cnt = {e: 0 for e in ['pe', 'act', 'dve', 'pool']}
        self.sems = {}
        self.esem = {}
        for e in ['pe', 'act', 'dve', 'pool']:
            self.esem[e] = 'c_' + e
            self.sems['c_' + e] = stack.enter_context(nc.semaphore('c_' + e))
        self.dq = {}
        for e in ['sp', 'pool', 'act']:
            names = []
            for i in range(ndq):
                n = 'd_%s%d' % (e, i)
                self.sems[n] = stack.enter_context(nc.semaphore(n))
                names.append(n)
            self.dq[e] = names
        self.dcnt = {'sp': 0, 'pool': 0, 'act': 0}
        self.waited = {e: {} for e in ENGS}
        self.pending = {e: {} for e in ENGS}

    def emit(self, eng, fn, rd=(), wr=(), dma=False):
        if dma:
            i = self.dcnt[eng]
            K = len(self.dq[eng])
            semname = self.dq[eng][i % K]
            val = 16 * (i // K + 1)
            self.dcnt[eng] += 1
            prev = (semname, val - 16) if val > 16 else None
            inc = 16
        else:
            self.cnt[eng] += 1
            semname = self.esem[eng]
            val = self.cnt[eng]
            prev = None
            inc = 1
        waits = {}

        def need(p):
            if p is None:
                return
            s, v = p
            if (not dma) and eng == 'pe' and s == 'c_pe':
                return
            if waits.get(s, 0) < v:
                waits[s] = v

        for s_, v_ in self.pending[eng].items():
            need((s_, v_))
        self.pending[eng] = {}
        need(prev)
        for t in rd:
            need(t.lw)
        for t in wr:
            need(t.lw)
            for s, v in t.rd.items():
                need((s, v))
        cache = self.waited[eng]
        w = []
        for s, v in waits.items():
            if cache.get(s, 0) < v:
                cache[s] = v
                w.append((s, v))
        self.q[eng].append((w, fn, semname, inc))
        for t in rd:
            if t.rd.get(semname, 0) < val:
                t.rd[semname] = val
        for t in wr:
            t.lw = (semname, val)
            t.rd = {}

    def barrier(self):
        cur = {}
        for e in ['pe', 'act', 'dve', 'pool']:
            if self.cnt[e] > 0:
                cur[self.esem[e]] = self.cnt[e]
        for e in ['sp', 'pool', 'act']:
            K = len(self.dq[e])
            for i in range(max(0, self.dcnt[e] - K), self.dcnt[e]):
                cur[self.dq[e][i % K]] = 16 * (i // K + 1)
        for e in ENGS:
            for s_, v_ in cur.items():
                if self.pending[e].get(s_, 0) < v_:
                    self.pending[e][s_] = v_

    def dma(self, out, in_, rd=(), wr=(), eng='sp', slow=False):
        if slow:
            self.emit(eng, lambda e: e.dma_start(out=out, in_=in_, allow_slow_non_contiguous=True), rd, wr, dma=True)
        else:
            self.emit(eng, lambda e: e.dma_start(out=out, in_=in_), rd, wr, dma=True)

    def mm(self, out, lhsT, rhs, start, stop, rd=(), wr=(), sgc=False):
        self.emit('pe', lambda e: e.matmul(out, lhsT, rhs, start=start, stop=stop, skip_group_check=sgc), rd, wr)

    def tr(self, out, in_, ident, rd=(), wr=()):
        self.emit('pe', lambda e: e.transpose(out, in_, ident), rd, wr)

    def act(self, out, in_, func, rd=(), wr=(), bias=None, scale=None, accum_out=None):
        kw = {}
        if bias is not None:
            kw['bias'] = bias
        if scale is not None:
            kw['scale'] = scale
        if accum_out is not None:
            kw['accum_out'] = accum_out
        self.emit('act', lambda e: e.activation(out, in_, func, **kw), rd, wr)

    def tt(self, out, in0, in1, op, rd=(), wr=(), eng='dve'):
        self.emit(eng, lambda e: e.tensor_tensor(out, in0, in1, op), rd, wr)

    def ts(self, out, in0, s1, s2, op0, op1=None, rd=(), wr=(), eng='dve'):
        if op1 is None:
            self.emit(eng, lambda e: e.tensor_scalar(out, in0, s1, None, op0), rd, wr)
        else:
            self.emit(eng, lambda e: e.tensor_scalar(out, in0, s1, s2, op0, op1), rd, wr)

    def stt(self, out, in0, scalar, in1, op0, op1, rd=(), wr=(), eng='dve'):
        self.emit(eng, lambda e: e.scalar_tensor_tensor(out, in0, scalar, in1, op0, op1), rd, wr)

    def copy(self, out, in_, rd=(), wr=(), eng='dve'):
        self.emit(eng, lambda e: e.tensor_copy(out, in_), rd, wr)

    def memset(self, ap, c, wr=(), eng='dve'):
        self.emit(eng, lambda e: e.memset(ap, c), (), wr)

    def recip(self, out, in_, rd=(), wr=()):
        self.emit('dve', lambda e: e.reciprocal(out, in_), rd, wr)

    def finish(self):
        nc = self.nc
        sems = self.sems
        with nc.Block() as b0:
            @b0.sync
            def _(e):
                for n, s in sems.items():
                    e.sem_clear(s)
        finals = {}
        for eng in ['sp', 'pool', 'act']:
            K = len(self.dq[eng])
            for i in range(self.dcnt[eng]):
                finals[self.dq[eng][i % K]] = 16 * (i // K + 1)
        with nc.Block() as block:
            def mk(engname):
                def body(e):
                    for (w, fn, semname, inc) in self.q[engname]:
                        for s, v in w:
                            e.wait_ge(sems[s], v)
                        ins = fn(e)
                        ins.then_inc(sems[semname], inc)
                    if engname == 'sp':
                        for s, v in finals.items():
                            e.wait_ge(sems[s], v)
                        for en in ['pe', 'act', 'dve', 'pool']:
                            if self.cnt[en] > 0:
                                e.wait_ge(sems[self.esem[en]], self.cnt[en])
                return body
            block.tensor(mk('pe'))
            block.scalar(mk('act'))
            block.vector(mk('dve'))
            block.gpsimd(mk('pool'))
            block.sync(mk('sp'))


POOL_R = [1, 2, 4, 8]


def _band_consts():
    B = np.zeros((4, 6, 128, 128), np.float32)
    tin = np.arange(128)[:, None]
    tout = np.arange(128)[None, :]
    for ri, r in enumerate(POOL_R):
        inband = (np.abs(tin - tout) <= r).astype(np.float32)
        full = 1.0 / (2 * r + 1)
        cnt_first = np.minimum(tout + r + 1, 2 * r + 1).astype(np.float32)
        cnt_last = np.minimum((127 - tout) + r + 1, 2 * r + 1).astype(np.float32)
        eye = np.eye(128, dtype=np.float32)
        B[ri, 0] = inband / cnt_first - eye
        B[ri, 1] = inband * full - eye
        B[ri, 2] = inband / cnt_last - eye
        B[ri, 3] = (np.abs(tin - 128 - tout) <= r).astype(np.float32) * full
        B[ri, 4] = (np.abs(tin + 128 - tout) <= r).astype(np.float32) * full
        cnt_both = (np.minimum(tout + r, r) + np.minimum(127 - tout, r) + 1).astype(np.float32)
        B[ri, 5] = inband / cnt_both - eye
    return B


class Ctx:
    pass


def build_program(NS, NT, layers, do_final, dbg=False):
    nc = bass.Bass("TRN2", target_bir_lowering=False)
    LAT = NT * 256
    TOK = 256 + LAT
    dt_in = lambda n, s: nc.dram_tensor(n, s, F32, kind="ExternalInput").ap()
    x_in = dt_in("x", [NS, LAT, 1024])
    ctx_in = dt_in("ctx", [NS, 256, 1024])
    c3_in = dt_in("c3", [3, 1024])
    ada_w = dt_in("ada_w", [4, 1024, 3072])
    ada_b = dt_in("ada_b", [4, 3072])
    norm_g = dt_in("norm_g", [4, 1024])
    odd_w_in = dt_in("odd_w_in", [2, 1024, 2048])
    odd_w_out = dt_in("odd_w_out", [2, 1024, 1024])
    pool_w = dt_in("pool_w", [2, 4, 256, 256])
    pool_scale = dt_in("pool_scale", [2, 1024])
    final_g = dt_in("final_g", [1024])
    IK = "ExternalOutput" if dbg else "Internal"
    EV = Ctx()
    EV.dbg = dbg
    EV.w_in = dt_in("even_w_in", [2, 1024, 2304])
    EV.w_out = dt_in("even_w_out", [2, 1024, 1024])
    EV.sink = dt_in("attn_sink", [2, 8])
    EV.a_re = dt_in("ssm_a_re", [2, 2, 32, 64])
    EV.a_im = dt_in("ssm_a_im", [2, 2, 32, 64])
    EV.log_dt = dt_in("ssm_log_dt", [2, 2, 32])
    EV.b_re = dt_in("ssm_b_re", [2, 2, 32, 64, 16])
    EV.b_im = dt_in("ssm_b_im", [2, 2, 32, 64, 16])
    EV.c_re = dt_in("ssm_c_re", [2, 2, 32, 16, 64])
    EV.c_im = dt_in("ssm_c_im", [2, 2, 32, 16, 64])
    EV.d = dt_in("ssm_d", [2, 512])
    EV.glu_w = dt_in("glu_w", [2, 512, 512])
    EV.glu_b = dt_in("glu_b", [2, 512])
    EV.rope_cos = dt_in("rope_cos", [128, NT * 256])
    EV.rope_sin = dt_in("rope_sin", [128, NT * 256])
    EV.perm = dt_in("perm", [128, 128])
    EV.masks = dt_in("masks", [2, 128, 128])
    EV.hmask = dt_in("hmask", [128, 3])
    NSUB_ = 2 * (NT + 1)
    EV.kT_d = nc.dram_tensor("kT_d", [NS, 128, 2, 256 + NT * 256], BF16, kind=IK).ap()
    EV.V_d = nc.dram_tensor("V_d", [NS, NSUB_, 128, 130], BF16, kind=IK).ap()
    EV.G_d = nc.dram_tensor("G_d", [NS, NT + 1, 128, 2048], F32, kind=IK).ap()
    EV.H_d = nc.dram_tensor("H_d", [NS, NT + 1, 128, 2048], BF16, kind=IK).ap()
    ident_in = dt_in("ident", [128, 128])
    bands_in = dt_in("bands", [4, 6, 128, 128])
    out = nc.dram_tensor("out", [NS, LAT, 1024], F32, kind="ExternalOutput").ap()
    hbuf = [nc.dram_tensor("hbuf%d" % i, [NS, TOK, 1024], F32, kind=("ExternalOutput" if dbg else "Internal")).ap()
            for i in range(2)]
    modrow = nc.dram_tensor("modrow", [4, 3, 3072], F32, kind="Internal").ap()

    with ExitStack() as st:
        P = Prog(nc, st)
        C = Ctx()
        C.nc, C.P, C.st = nc, P, st
        C.NS, C.NT, C.TOK, C.LAT = NS, NT, TOK, LAT
        sbc = [0]

        def sb(shape, dtype, name=None):
            sbc[0] += 1
            return st.enter_context(nc.sbuf_tensor(name or ("sb%d" % sbc[0]), shape, dtype))
        C.sb = sb
        C.ps = [st.enter_context(nc.psum_tensor("ps%d" % i, [128, 512], F32)) for i in range(7)]
        C.t_ps = [Tok() for _ in range(7)]
        t_modrow = Tok()
        t_h = [[[Tok() for _ in range(2 * (NT + 1))] for _ in range(NS)] for _ in range(2)]
        t_out = Tok()

        ident_f = sb([128, 128], F32)
        ident = sb([128, 128], BF16)
        t_ident = Tok()
        P.dma(ident_f[:], ident_in[:, :], wr=[t_ident])
        P.copy(ident[:], ident_f[:], rd=[t_ident], wr=[t_ident])
        C.ident, C.t_ident = ident, t_ident
        C.ident_f = ident_f
        fg_bc = sb([128, 1024], F32)
        t_fg = Tok()
        P.dma(fg_bc[:], final_g.partition_broadcast(128), wr=[t_fg])

        sT = sb([128, 8, 3], F32)
        t_sT = Tok()
        for r in range(3):
            P.dma(sT[:, :, r], c3_in[r, :].rearrange("(k p) -> p k", p=128), wr=[t_sT], slow=True)
        P.act(sT[:], sT[:], AF.Silu, rd=[t_sT], wr=[t_sT])
        ones1 = sb([1, 4], F32)
        t_ones1 = Tok()
        P.memset(ones1[:], 1.0, wr=[t_ones1])
        with ExitStack() as st2:
            wt = [st2.enter_context(nc.sbuf_tensor("adaw%d" % i, [128, 3072], F32)) for i in range(2)]
            t_wt = [Tok(), Tok()]
            brow = st2.enter_context(nc.sbuf_tensor("adab", [1, 3072], F32))
            t_brow = Tok()
            mrow = st2.enter_context(nc.sbuf_tensor("mrow", [3, 3072], F32))
            t_mrow = Tok()
            for L in layers:
                P.dma(brow[:], ada_b[L:L + 1, :], wr=[t_brow])
                for k in range(8):
                    P.dma(wt[k % 2][:], ada_w[L, k * 128:(k + 1) * 128, :], wr=[t_wt[k % 2]])
                    for n in range(6):
                        P.mm(C.ps[n][0:3, :], sT[:, k, :], wt[k % 2][:, n * 512:(n + 1) * 512], start=(k == 0),
                             stop=False, rd=[t_sT, t_wt[k % 2]], wr=[C.t_ps[n]])
                for n in range(6):
                    P.mm(C.ps[n][0:3, :], ones1[0:1, 0:3], brow[0:1, n * 512:(n + 1) * 512], start=False, stop=True,
                         rd=[t_ones1, t_brow], wr=[C.t_ps[n]])
                    P.copy(mrow[:, n * 512:(n + 1) * 512], C.ps[n][0:3, :], rd=[C.t_ps[n]], wr=[t_mrow])
                P.dma(modrow[L], mrow[:], rd=[t_mrow], wr=[t_modrow])

        gT = sb([128, 8], F32)
        scT = sb([128, 3, 8], F32)
        gsT = sb([128, 3, 8], F32)
        shT = sb([128, 3, 8], F32)
        gate_bc = sb([128, 3, 1024], F32)
        t_mod = Tok()
        C.gsT, C.shT, C.gate_bc, C.t_mod = gsT, shT, gate_bc, t_mod

        def load_mod(L):
            P.dma(gT[:], norm_g[L, :].rearrange("(k p) -> p k", p=128), wr=[t_mod], slow=True)
            for w in range(3):
                P.dma(shT[:, w, :], modrow[L, w, 0:1024].rearrange("(k p) -> p k", p=128), rd=[t_modrow],
                      wr=[t_mod], slow=True)
                P.dma(scT[:, w, :], modrow[L, w, 1024:2048].rearrange("(k p) -> p k", p=128), rd=[t_modrow],
                      wr=[t_mod], slow=True)
                P.dma(gate_bc[:, w, :], modrow[L, w, 2048:3072].partition_broadcast(128), rd=[t_modrow], wr=[t_mod])
                P.stt(gsT[:, w, :], scT[:, w, :], 1.0, gT[:], ALU.add, ALU.mult, rd=[t_mod], wr=[t_mod])

        def h_src(li, s, n):
            if li == 0:
                if n < 2:
                    return ctx_in[s, n * 128:(n + 1) * 128, :], None
                return x_in[s, (n - 2) * 128:(n - 1) * 128, :], None
            b = (li - 1) % 2
            return hbuf[b][s, n * 128:(n + 1) * 128, :], t_h[b][s][n]

        def h_dst(li, s, n):
            b = li % 2
            return hbuf[b][s, n * 128:(n + 1) * 128, :], t_h[b][s][n]
        C.h_src, C.h_dst = h_src, h_dst

        NH = 4
        C.hT = [sb([128, 1024], F32) for _ in range(NH)]
        C.t_hT = [Tok() for _ in range(NH)]
        C.stat = [sb([128, 4], F32) for _ in range(2)]
        C.t_stat = [Tok(), Tok()]
        C.xn = [sb([128, 1024], BF16) for _ in range(2)]
        C.t_xn = [Tok(), Tok()]
        C.pT = st.enter_context(nc.psum_tensor("psT", [128, 1024], BF16))
        C.t_pT = Tok()
        C.eps = sb([128, 1], F32)
        C.t_eps = Tok()
        P.memset(C.eps[:], 1e-6, wr=[C.t_eps])
        C.nrm_i = 0

        def load_h(li, s, n, slot):
            src, tk = h_src(li, s, n)
            P.dma(C.hT[slot][:], src, rd=([tk] if tk else []), wr=[C.t_hT[slot]])
        C.load_h = load_h

        def norm_T(slot, which, aT_ap_fn, t_aT):
            i = C.nrm_i % 2
            C.nrm_i += 1
            h = C.hT[slot]
            stt_ = C.stat[i]
            P.act(C.xn[i][:], h[:], AF.Square, rd=[C.t_hT[slot]], wr=[C.t_xn[i], C.t_stat[i]], accum_out=stt_[:, 0:1])
            P.act(stt_[:, 1:2], stt_[:, 0:1], AF.Sqrt, rd=[C.t_stat[i], C.t_eps], wr=[C.t_stat[i]],
                  bias=C.eps[:, 0:1], scale=1.0 / 1024)
            P.recip(stt_[:, 2:3], stt_[:, 1:2], rd=[C.t_stat[i]], wr=[C.t_stat[i]])
            P.ts(C.xn[i][:], h[:], stt_[:, 2:3], None, ALU.mult, rd=[C.t_hT[slot], C.t_stat[i]], wr=[C.t_xn[i]])
            for k in range(8):
                P.tr(C.pT[:, k * 128:(k + 1) * 128], C.xn[i][:, k * 128:(k + 1) * 128], C.ident[:],
                     rd=[C.t_xn[i], C.t_ident], wr=[C.t_pT])
            for k in range(8):
                P.act(aT_ap_fn(k), C.pT[:, k * 128:(k + 1) * 128], AF.Identity, rd=[C.t_pT, C.t_mod], wr=[t_aT],
                      bias=C.shT[:, which, k:k + 1], scale=C.gsT[:, which, k:k + 1])
        C.norm_T = norm_T

        def store_h(li, s, n, hn, t_hn, last):
            if last:
                if n < 2:
                    return
                i = C.nrm_i % 2
                C.nrm_i += 1
                stt_ = C.stat[i]
                P.act(C.xn[i][:], hn[:], AF.Square, rd=[t_hn], wr=[C.t_xn[i], C.t_stat[i]], accum_out=stt_[:, 0:1])
                P.act(stt_[:, 1:2], stt_[:, 0:1], AF.Sqrt, rd=[C.t_stat[i], C.t_eps], wr=[C.t_stat[i]],
                      bias=C.eps[:, 0:1], scale=1.0 / 1024)
                P.recip(stt_[:, 2:3], stt_[:, 1:2], rd=[C.t_stat[i]], wr=[C.t_stat[i]])
                P.stt(hn[:], hn[:], stt_[:, 2:3], fg_bc[:], ALU.mult, ALU.mult, rd=[t_hn, C.t_stat[i], t_fg],
                      wr=[t_hn])
                P.dma(out[s, (n - 2) * 128:(n - 1) * 128, :], hn[:], rd=[t_hn], wr=[t_out], eng='pool')
            else:
                dst, tk = h_dst(li, s, n)
                P.dma(dst, hn[:], rd=[t_hn], wr=[tk], eng='pool')
        C.store_h = store_h

        for idx, L in enumerate(layers):
            last = do_final and (idx == len(layers) - 1)
            need_ctx = not last
            load_mod(L)
            P.barrier()
            if L % 2 == 1:
                odd_layer(C, L, idx, odd_w_in[L // 2], odd_w_out[L // 2], pool_w[L // 2], pool_scale[L // 2],
                          bands_in, need_ctx, last)
            else:
                even_layer(C, L, idx, EV, need_ctx, last)
        P.finish()
    return nc


def load_weight_bf16(C, dst, src_ap, rows_k, cols, t_dst, stage, t_stage):
    P = C.P
    for k in range(rows_k):
        i = k % len(stage)
        P.dma(stage[i][:, 0:cols], src_ap[k * 128:(k + 1) * 128, :], wr=[t_stage[i]])
        P.copy(dst[:, k, :], stage[i][:, 0:cols], rd=[t_stage[i]], wr=[t_dst], eng='pool')


def odd_layer(C, L, idx, w_in_d, w_out_d, pool_w_d, pool_scale_d, bands_in, need_ctx, last):
    nc, P, NS, NT = C.nc, C.P, C.NS, C.NT
    with ExitStack() as st:
        sb = lambda shape, dtype, name: st.enter_context(nc.sbuf_tensor("o%d_%s" % (idx, name), shape, dtype))
        w_in = sb([128, 8, 2048], BF16, "w_in")
        w_out = sb([128, 8, 1024], BF16, "w_out")
        pw = sb([128, 8, 256], BF16, "pw")
        bands = sb([128, 4, 6, 128], BF16, "bands")
        psc = sb([128, 8], F32, "psc")
        stage = [sb([128, 2048], F32, "stage%d" % i) for i in range(2)]
        t_stage = [Tok(), Tok()]
        t_w = Tok()
        load_weight_bf16(C, w_in, w_in_d, 8, 2048, t_w, stage, t_stage)
        load_weight_bf16(C, w_out, w_out_d, 8, 1024, t_w, stage, t_stage)
        load_weight_bf16(C, pw, pool_w_d.rearrange("g c d -> (g c) d"), 8, 256, t_w, stage, t_stage)
        for ri in range(4):
            i = ri % 2
            P.dma(stage[i][:, 0:768].rearrange("p (v t) -> p v t", v=6), bands_in[ri].rearrange("v p t -> p v t"),
                  wr=[t_stage[i]])
            P.copy(bands[:, ri, :, :], stage[i][:, 0:768].rearrange("p (v t) -> p v t", v=6), rd=[t_stage[i]],
                   wr=[t_w], eng='pool')
        P.dma(psc[:], pool_scale_d.rearrange("(k p) -> p k", p=128), wr=[t_w], slow=True)

        NR = 4
        aT = [sb([128, 8, 128], BF16, "aT%d" % i) for i in range(2)]
        t_aT = [Tok(), Tok()]
        u_tm = [sb([128, 1024], BF16, "u%d" % i) for i in range(NR)]
        t_u = [Tok() for _ in range(NR)]
        sg = [sb([128, 8, 128], BF16, "sg%d" % i) for i in range(NR)]
        t_sg = [Tok() for _ in range(NR)]
        pTt = sb([128, 8, 128], BF16, "pTt")
        t_pTt = Tok()
        mT = sb([128, 8, 128], BF16, "mT")
        t_mT = Tok()
        tmp = sb([128, 1024], F32, "tmp")
        t_tmp = Tok()
        hn = [sb([128, 1024], F32, "hn%d" % i) for i in range(2)]
        t_hn = [Tok(), Tok()]
        ps, t_ps = C.ps, C.t_ps
        fin_i = [0]

        for s in range(NS):
            seqs = []
            if need_ctx:
                seqs.append((2, [0, 1]))
            seqs.append((s, list(range(2, 2 * (NT + 1)))))
            for which, subs in seqs:
                S = len(subs)

                BA, BC, BB, BD, BE = (5, 6), (1, 2), (0, 3), (4, 1), (2, 6)

                def f_norm(j):
                    n = subs[j]
                    slot = n % 4
                    C.load_h(idx, s, n, slot)
                    a = aT[j % 2]
                    C.norm_T(slot, which, lambda k: a[:, k, :], t_aT[j % 2])

                def f_u(j):
                    a = aT[j % 2]
                    r = j % NR
                    for nb in range(2):
                        bk = BA[nb]
                        for k in range(8):
                            P.mm(ps[bk][:, :], a[:, k, :], w_in[:, k, nb * 512:(nb + 1) * 512], start=(k == 0),
                                 stop=(k == 7), rd=[t_aT[j % 2], t_w], wr=[t_ps[bk]])
                        P.copy(u_tm[r][:, nb * 512:(nb + 1) * 512], ps[bk][:, :], rd=[t_ps[bk]], wr=[t_u[r]])

                def f_gate(j):
                    a = aT[j % 2]
                    r = j % NR
                    for m in range(8):
                        bank = BB[m // 4]
                        for k in range(8):
                            P.mm(ps[bank][:, (m % 4) * 128:(m % 4 + 1) * 128], w_in[:, k, 1024 + m * 128:1024 + (m + 1) * 128],
                                 a[:, k, :], start=(k == 0), stop=(k == 7), rd=[t_aT[j % 2], t_w], wr=[t_ps[bank]])
                    for b2 in range(2):
                        P.act(sg[r][:, b2 * 4:(b2 + 1) * 4, :], ps[BB[b2]][:, :].rearrange("p (m t) -> p m t", m=4),
                              AF.Silu, rd=[t_ps[BB[b2]]], wr=[t_sg[r]])

                def b_pool(j):
                    r = j % NR
                    if S == 1:
                        v = 5
                    elif j == 0:
                        v = 0
                    elif j == S - 1:
                        v = 2
                    else:
                        v = 1
                    for m in range(8):
                        bank = BC[m // 4]
                        ri = m // 2
                        dst = ps[bank][:, (m % 4) * 128:(m % 4 + 1) * 128]
                        parts = [(u_tm[r][:, m * 128:(m + 1) * 128], bands[:, ri, v, :], t_u[r])]
                        if j > 0:
                            rp = (j - 1) % NR
                            parts.append((u_tm[rp][64:128, m * 128:(m + 1) * 128], bands[64:128, ri, 3, :], t_u[rp]))
                        if j < S - 1:
                            rn = (j + 1) % NR
                            parts.append((u_tm[rn][0:32, m * 128:(m + 1) * 128], bands[0:32, ri, 4, :], t_u[rn]))
                        for pi, (l, rr, tk) in enumerate(parts):
                            P.mm(dst, l, rr, start=(pi == 0), stop=(pi == len(parts) - 1), rd=[tk, t_w], wr=[t_ps[bank]])
                    for b2 in range(2):
                        P.copy(pTt[:, b2 * 4:(b2 + 1) * 4, :], ps[BC[b2]][:, :].rearrange("p (m t) -> p m t", m=4),
                               rd=[t_ps[BC[b2]]], wr=[t_pTt])

                def b_pw(j):
                    r = j % NR
                    for g in range(4):
                        for dd in range(2):
                            m = g * 2 + dd
                            bank = BD[m // 4]
                            for cc in range(2):
                                P.mm(ps[bank][:, (m % 4) * 128:(m % 4 + 1) * 128], pw[:, g * 2 + cc, dd * 128:(dd + 1) * 128],
                                     pTt[:, g * 2 + cc, :], start=(cc == 0), stop=(cc == 1), rd=[t_pTt, t_w],
                                     wr=[t_ps[bank]])
                    for m in range(8):
                        bank = BD[m // 4]
                        P.stt(mT[:, m, :], ps[bank][:, (m % 4) * 128:(m % 4 + 1) * 128], psc[:, m:m + 1], sg[r][:, m, :],
                              ALU.mult, ALU.mult, rd=[t_ps[bank], t_sg[r], t_w], wr=[t_mT])

                def b_out(j):
                    n = subs[j]
                    slot = n % 4
                    hi = fin_i[0] % 2
                    fin_i[0] += 1
                    for nb in range(2):
                        bk = BE[nb]
                        for k in range(8):
                            P.mm(ps[bk][:, :], mT[:, k, :], w_out[:, k, nb * 512:(nb + 1) * 512], start=(k == 0),
                                 stop=(k == 7), rd=[t_mT, t_w], wr=[t_ps[bk]])
                        sl = slice(nb * 512, (nb + 1) * 512)
                        P.tt(tmp[:, sl], ps[bk][:, :], C.gate_bc[:, which, sl], ALU.mult,
                             rd=[t_ps[bk], C.t_mod], wr=[t_tmp])
                        P.tt(hn[hi][:, sl], tmp[:, sl], C.hT[slot][:, sl], ALU.add, rd=[t_tmp, C.t_hT[slot]],
                             wr=[t_hn[hi]], eng='pool')
                    C.store_h(idx, s, n, hn[hi], t_hn[hi], last)

                f_norm(0)
                for j in range(S + 1):
                    if j < S:
                        f_u(j)
                    if j >= 1:
                        b_pool(j - 1)
                    if j < S:
                        f_gate(j)
                    if j >= 1:
                        b_pw(j - 1)
                    if j + 1 < S:
                        f_norm(j + 1)
                    if j >= 1:
                        b_out(j - 1)


def make_in_maps(inputs, n_cores, NS, NT):
    f = lambda a: np.ascontiguousarray(np.asarray(a, dtype=np.float32))
    x = f(inputs['x'])
    ctx = f(inputs['ctx'])
    c = f(inputs['c'])
    shared = {k: f(inputs[k]) for k in ['ada_w', 'ada_b', 'norm_g', 'odd_w_in', 'odd_w_out', 'pool_w', 'pool_scale',
                                        'final_g']}
    for k in ['even_w_in', 'even_w_out', 'attn_sink', 'ssm_a_re', 'ssm_a_im', 'ssm_log_dt', 'ssm_b_re', 'ssm_b_im',
              'ssm_c_re', 'ssm_c_im', 'ssm_d', 'glu_w', 'glu_b']:
        shared[k] = f(inputs[k])
    t = np.arange(NT * 256)
    r = np.arange(128)
    inv_freq = (10000.0 ** (-np.arange(16, dtype=np.float32) / 16)).astype(np.float32)
    pos = np.where(((r >> 5) & 1)[:, None] == 0, (t // 64)[None, :], (t % 64)[None, :]).astype(np.float32)
    ang = pos * inv_freq[r & 15][:, None]
    shared['rope_cos'] = np.cos(ang).astype(np.float32)
    sgn = np.where(((r >> 4) & 1) == 0, -1.0, 1.0).astype(np.float32)[:, None]
    shared['rope_sin'] = (np.sin(ang) * sgn).astype(np.float32)
    pm = np.zeros((128, 128), np.float32)
    pm[r ^ 16, r] = 1.0
    shared['perm'] = pm
    kk = np.arange(128)[:, None]
    qq = np.arange(128)[None, :]
    shared['masks'] = np.stack([(kk >= qq), (kk <= qq)]).astype(np.float32)
    hm = np.zeros((128, 3), np.float32)
    hm[:64, 0] = 1.0
    hm[64:, 1] = 1.0
    hm[96:, 2] = 1.0
    shared['hmask'] = hm
    shared['ident'] = np.eye(128, dtype=np.float32)
    shared['bands'] = _band_consts()
    maps = []
    for i in range(n_cores):
        m = dict(shared)
        m['x'] = np.ascontiguousarray(x[i * NS:(i + 1) * NS, :NT * 256])
        m['ctx'] = np.ascontiguousarray(ctx[i * NS:(i + 1) * NS])
        c3 = np.zeros((3, 1024), np.float32)
        c3[0:NS] = c[i * NS:(i + 1) * NS]
        c3[2] = f(inputs['c_ctx'])
        m['c3'] = c3
        maps.append(m)
    return maps


def kernel(**inputs):
    NS, NT, n_cores = 2, 16, 8
    nc = build_program(NS, NT, [0, 1, 2, 3], True)
    maps = make_in_maps(inputs, n_cores, NS, NT)
    res = run_bass_kernel_spmd(nc, maps, core_ids=list(range(n_cores)))
    return np.concatenate([r['out'] for r in res.results], axis=0)


def even_layer(C, L, idx, EV, need_ctx, last):
    nc, P, NS, NT, TOK = C.nc, C.P, C.NS, C.NT, C.TOK
    j = L // 2
    NTL = NT + 1
    NSUB = 2 * NTL
    ps, t_ps = C.ps, C.t_ps
    QO, KO, VO, GAO, UO, GSO = 0, 512, 640, 768, 1280, 1792
    PI = float(np.pi)
    with ExitStack() as st:
        sbn = [0]

        def sb(shape, dtype, stack=st):
            sbn[0] += 1
            return stack.enter_context(nc.sbuf_tensor("e%d_%d" % (idx, sbn[0]), shape, dtype))
        w_in = sb([128, 8, 2304], BF16)
        t_w = Tok()
        dvec = sb([128, 4], F32)
        glub = sb([128, 4], F32)
        esink = sb([128, 8], F32)
        t_sm = Tok()
        hmask3 = sb([128, 3], F32)
        hmask = hmask3[:, 0:2]
        perm = sb([128, 128], BF16)
        masks = sb([128, 2, 128], BF16)
        s0 = ExitStack()
        stage0 = sb([128, 2304], F32, s0)
        stage = [stage0, stage0]
        t_st0 = Tok()
        t_stage = [t_st0, t_st0]
        load_weight_bf16(C, w_in, EV.w_in[j], 8, 2304, t_w, stage, t_stage)
        P.dma(dvec[:], EV.d[j].rearrange("(k p) -> p k", p=128), wr=[t_sm], slow=True)
        P.dma(glub[:], EV.glu_b[j].rearrange("(k p) -> p k", p=128), wr=[t_sm], slow=True)
        P.dma(esink[:], EV.sink[j].partition_broadcast(128), wr=[t_sm])
        P.act(esink[:], esink[:], AF.Exp, rd=[t_sm], wr=[t_sm])
        P.dma(hmask3[:], EV.hmask[:, :], wr=[t_sm])
        P.dma(stage0[:, 0:128], EV.perm[:, :], wr=[t_st0])
        P.copy(perm[:], stage0[:, 0:128], rd=[t_st0], wr=[t_sm])
        P.dma(stage0[:, 0:256].rearrange("p (a b) -> p a b", a=2), EV.masks.rearrange("a p b -> p a b"), wr=[t_st0])
        P.copy(masks[:], stage0[:, 0:256].rearrange("p (a b) -> p a b", a=2), rd=[t_st0], wr=[t_sm])
        P.barrier()
        s0.close()
        if STAGE == 0.1:
            return
        W2 = sb([128, 2, 2, 16], F32)
        W3 = sb([128, 2, 2, 16], F32)
        t_W = Tok()
        Psi = sb([128, 2, 2, 16, 8, 32], BF16)
        Kmat = sb([128, 4, 15, 128], BF16)
        t_Gam, t_Psi, t_K = Tok(), Tok(), Tok()
        P.memset(Kmat[:], 0.0, wr=[t_K], eng='pool')
        aT = sb([128, 8, 256], BF16)
        t_aT = Tok()
        uT = sb([128, 4, 256], BF16)
        t_uT = Tok()
        kraw = sb([128, 256], BF16)
        t_kraw = Tok()
        rcos = sb([128, 256], F32)
        rsin = sb([128, 256], F32)
        t_rope = Tok()
        r1 = sb([128, 256], F32)
        r2 = sb([128, 256], F32)
        t_r = Tok()
        sa = ExitStack()
        st.enter_context(sa)
        Gam = sb([128, 2, 4, 8, 2, 128], BF16, sa)
        with ExitStack() as sp:
            f = lambda shape: sb(shape, F32, sp)
            are, aim, ldt = f([128, 2, 16]), f([128, 2, 16]), f([128, 2, 16])
            Bre, Bim, Cre, Cim = (f([128, 2, 16, 16]) for _ in range(4))
            t_p = Tok()
            for h in range(2):
                sl = slice(64 * h, 64 * h + 64)
                P.dma(are[sl], EV.a_re[j, :, h::2, :].rearrange("d r p -> p d r"), wr=[t_p], slow=True)
                P.dma(aim[sl], EV.a_im[j, :, h::2, :].rearrange("d r p -> p d r"), wr=[t_p], slow=True)
                P.dma(ldt[sl], EV.log_dt[j, :, h::2].partition_broadcast(64), wr=[t_p], slow=True)
                for d in range(2):
                    P.dma(Bre[sl, d], EV.b_re[j, d, h::2, :, :].rearrange("r p c -> p r c"), wr=[t_p], slow=True)
                    P.dma(Bim[sl, d], EV.b_im[j, d, h::2, :, :].rearrange("r p c -> p r c"), wr=[t_p], slow=True)
            ccm = f([128, 128])
            t_ccm = Tok()
            for (src, dstC) in ((EV.c_re, Cre), (EV.c_im, Cim)):
                for d in range(2):
                    for blk in range(2):
                        for r in range(8):
                            pr = blk * 8 + r
                            P.dma(ccm[16 * r:16 * r + 16, :].rearrange("c (h p) -> c h p", h=2),
                                  src[j, d, 2 * pr:2 * pr + 2, :, :].rearrange("h c p -> c h p"), wr=[t_ccm])
                        P.tr(ps[0][:, 0:128], ccm[:], C.ident_f[:], rd=[t_ccm, C.t_ident], wr=[t_ps[0]])
                        P.copy(dstC[:, d, blk * 8:(blk + 1) * 8, :], ps[0][:, 0:128].rearrange("p (r c) -> p r c", r=8),
                               rd=[t_ps[0]], wr=[t_p])
            if STAGE == 0.2:
                return
            V3 = [128, 2, 16]
            dtv, x8, th8, mag, cs, sn = (f(V3) for _ in range(6))
            halfpi = f([128, 1])
            P.memset(halfpi[:], PI / 2, wr=[t_p])
            P.act(dtv[:], ldt[:], AF.Exp, rd=[t_p], wr=[t_p])
            P.stt(x8[:], are[:], 0.125, dtv[:], ALU.mult, ALU.mult, rd=[t_p], wr=[t_p])
            P.stt(th8[:], aim[:], 0.125, dtv[:], ALU.mult, ALU.mult, rd=[t_p], wr=[t_p])
            P.act(mag[:], x8[:], AF.Exp, rd=[t_p], wr=[t_p])
            P.act(cs[:], th8[:], AF.Sin, rd=[t_p], wr=[t_p], scale=-1.0, bias=halfpi[:, 0:1])
            P.act(sn[:], th8[:], AF.Sin, rd=[t_p], wr=[t_p])
            pwr = f([128, 9, 2, 16])
            pwi = f([128, 9, 2, 16])
            ta, tb = f(V3), f(V3)
            P.tt(ta[:], mag[:], cs[:], ALU.mult, rd=[t_p], wr=[t_p])
            P.tt(tb[:], mag[:], sn[:], ALU.mult, rd=[t_p], wr=[t_p])

            def cmul(or_, oi_, ar, ai, br, bi, t1, t2):
                P.tt(t1, ar, br, ALU.mult, rd=[t_p], wr=[t_p])
                P.tt(t2, ai, bi, ALU.mult, rd=[t_p], wr=[t_p])
                P.tt(or_, t1, t2, ALU.subtract, rd=[t_p], wr=[t_p])
                P.tt(t1, ar, bi, ALU.mult, rd=[t_p], wr=[t_p])
                P.tt(t2, ai, br, ALU.mult, rd=[t_p], wr=[t_p])
                P.tt(oi_, t1, t2, ALU.add, rd=[t_p], wr=[t_p])
            t1, t2, sqr, sqi = f(V3), f(V3), dtv, x8
            cur_r, cur_i = ta, tb
            for it in range(3):
                cmul(sqr[:], sqi[:], cur_r[:], cur_i[:], cur_r[:], cur_i[:], t1[:], t2[:])
                P.copy(cur_r[:], sqr[:], rd=[t_p], wr=[t_p])
                P.copy(cur_i[:], sqi[:], rd=[t_p], wr=[t_p])
            P.memset(pwr[:, 0], 1.0, wr=[t_p])
            P.memset(pwi[:, 0], 0.0, wr=[t_p])
            P.copy(pwr[:, 1], cur_r[:], rd=[t_p], wr=[t_p])
            P.copy(pwi[:, 1], cur_i[:], rd=[t_p], wr=[t_p])
            for k in range(2, 9):
                cmul(pwr[:, k], pwi[:, k], pwr[:, k - 1], pwi[:, k - 1], pwr[:, 1], pwi[:, 1], t1[:], t2[:])
            am1, nr, ni, inv, kr, ki = (f(V3) for _ in range(6))
            P.ts(am1[:], pwr[:, 1], -1.0, None, ALU.add, rd=[t_p], wr=[t_p])
            P.tt(t1[:], am1[:], are[:], ALU.mult, rd=[t_p], wr=[t_p])
            P.tt(t2[:], pwi[:, 1], aim[:], ALU.mult, rd=[t_p], wr=[t_p])
            P.tt(nr[:], t1[:], t2[:], ALU.add, rd=[t_p], wr=[t_p])
            P.tt(t1[:], pwi[:, 1], are[:], ALU.mult, rd=[t_p], wr=[t_p])
            P.tt(t2[:], am1[:], aim[:], ALU.mult, rd=[t_p], wr=[t_p])
            P.tt(ni[:], t1[:], t2[:], ALU.subtract, rd=[t_p], wr=[t_p])
            P.tt(t1[:], are[:], are[:], ALU.mult, rd=[t_p], wr=[t_p])
            P.tt(t2[:], aim[:], aim[:], ALU.mult, rd=[t_p], wr=[t_p])
            P.tt(inv[:], t1[:], t2[:], ALU.add, rd=[t_p], wr=[t_p])
            P.recip(inv[:], inv[:], rd=[t_p], wr=[t_p])
            P.tt(kr[:], nr[:], inv[:], ALU.mult, rd=[t_p], wr=[t_p])
            P.tt(ki[:], ni[:], inv[:], ALU.mult, rd=[t_p], wr=[t_p])
            V4 = [128, 2, 16, 16]
            bc4 = lambda a: a.unsqueeze(3).broadcast_to(V4)
            Bbr, Bbi, u1, u2 = f(V4), f(V4), f(V4), f(V4)
            cmul(Bbr[:], Bbi[:], bc4(kr[:]), bc4(ki[:]), Bre[:], Bim[:], u1[:], u2[:])
            for d in range(2):
                P.copy(W2[:, d, 0, :], pwr[:, 8, d, :], rd=[t_p], wr=[t_W])
                P.copy(W2[:, d, 1, :], pwr[:, 8, d, :], rd=[t_p], wr=[t_W])
                P.ts(W3[:, d, 0, :], pwi[:, 8, d, :], -1.0, None, ALU.mult, rd=[t_p], wr=[t_W])
                P.copy(W3[:, d, 1, :], pwi[:, 8, d, :], rd=[t_p], wr=[t_W])
            CP = sb([128, 2, 2, 16, 32], BF16, sp)
            hm4 = lambda sgn_ap: sgn_ap.unsqueeze(1).unsqueeze(3).broadcast_to([128, 16, 2, 16])
            nhmask = f([128, 2])
            P.ts(nhmask[:], hmask, -1.0, None, ALU.mult, rd=[t_sm], wr=[t_p])
            for d in range(2):
                P.tt(CP[:, 0, d].rearrange("p r (h c) -> p r h c", h=2), Cre[:, d].unsqueeze(2).broadcast_to([128, 16, 2, 16]),
                     hm4(hmask), ALU.mult, rd=[t_p, t_sm], wr=[t_p])
                P.tt(CP[:, 1, d].rearrange("p r (h c) -> p r h c", h=2), Cim[:, d].unsqueeze(2).broadcast_to([128, 16, 2, 16]),
                     hm4(nhmask[:]), ALU.mult, rd=[t_p, t_sm], wr=[t_p])
            Yr, Yi = f([128, 16, 16]), f([128, 16, 16])
            bc3 = lambda a: a.unsqueeze(2).broadcast_to([128, 16, 16])
            for d in range(2):
                for t in range(8):
                    k = t + 1 if d == 0 else 8 - t
                    cmul(Yr[:], Yi[:], Cre[:, d], Cim[:, d], bc3(pwr[:, k, d, :]), bc3(pwi[:, k, d, :]), u1[:, 0], u2[:, 0])
                    for ri_, (Y_, hm_) in enumerate(((Yr, hmask), (Yi, nhmask[:]))):
                        P.tt(Psi[:, d, ri_, :, t, :].rearrange("p r (h c) -> p r h c", h=2),
                             Y_[:].unsqueeze(2).broadcast_to([128, 16, 2, 16]), hm4(hm_), ALU.mult,
                             rd=[t_p, t_sm], wr=[t_Psi])
            if STAGE == 0.3:
                return
            Xr, Xi = Yr, Yi
            XP = [sb([128, 4, 2, 16], BF16, sp) for a_ in range(2)]
            Kst = sb([32, 4, 15, 32], BF16, sp)
            t_XP = [Tok(), Tok()]
            xi = 0
            for d in range(2):
                for k in range(8):
                    cmul(Xr[:], Xi[:], Bbr[:, d], Bbi[:, d], bc3(pwr[:, k, d, :]), bc3(pwi[:, k, d, :]), u1[:, 0], u2[:, 0])
                    s_idx = 7 - k if d == 0 else k
                    for i in range(4):
                        kb = 1 + (i % 2)
                        for ri, X in enumerate((Xr, Xi)):
                            xp = XP[xi % 2]
                            txp = t_XP[xi % 2]
                            xi += 1
                            P.tt(xp[:], X[:, 4 * i:4 * i + 4, :].unsqueeze(2).broadcast_to([128, 4, 2, 16]),
                                 hmask.unsqueeze(1).unsqueeze(3).broadcast_to([128, 4, 2, 16]), ALU.mult,
                                 rd=[t_p, t_sm], wr=[txp])
                            xpf = xp[:].rearrange("p q h c -> p (q h c)")
                            P.tr(C.pT[:, 0:128], xpf, C.ident[:], rd=[txp, C.t_ident], wr=[C.t_pT])
                            P.copy(Gam[:, d, i, s_idx, ri, :], C.pT[:, 0:128], rd=[C.t_pT], wr=[t_Gam])
                            for q in range(3):
                                P.mm(ps[kb][32 * q:32 * q + 32, 32 * q:32 * q + 32], xp[:, q].rearrange("p h c -> p (h c)"),
                                     CP[:, ri, d, 4 * i + q, :], start=(ri == 0), stop=(ri == 1), rd=[txp, t_p],
                                     wr=[t_ps[kb]])
                            P.mm(ps[kb + 2][0:32, 128:160], xp[:, 3].rearrange("p h c -> p (h c)"), CP[:, ri, d, 4 * i + 3, :],
                                 start=(ri == 0), stop=(ri == 1), rd=[txp, t_p], wr=[t_ps[kb + 2]])
                        li = 7 + k if d == 0 else 7 - k
                        for q in range(4):
                            blk = slice(32 * q, 32 * q + 32)
                            if q < 3:
                                dst_, src_, tkp = Kmat[blk, i, (7 if (d == 1 and k == 0) else li), blk], ps[kb][blk, blk], t_ps[kb]
                            else:
                                dst_, src_, tkp = Kst[:, i, (7 if (d == 1 and k == 0) else li), :], ps[kb + 2][0:32, 128:160], t_ps[kb + 2]
                            if d == 1 and k == 0:
                                P.tt(dst_, dst_, src_, ALU.add, rd=[tkp, t_K], wr=[t_K])
                            else:
                                P.copy(dst_, src_, rd=[tkp], wr=[t_K])
            for i in range(4):
                P.dma(Kmat[96:128, i, :, 96:128], Kst[:, i, :, :], rd=[t_K], wr=[t_K])
        if EV.dbg:
            for nm_, t_, tk_ in (("Gam", Gam, t_Gam), ("Psi", Psi, t_Psi), ("Kmat", Kmat, t_K)):
                n_el = 1
                for v_ in t_.shape[1:]:
                    n_el *= v_
                dd_ = nc.dram_tensor("dbg_%s_%d" % (nm_, idx), [128, n_el], BF16, kind="ExternalOutput").ap()
                names_ = "abcdefg"[:len(t_.shape) - 1]
                P.dma(dd_[:, :], t_[:].rearrange("p %s -> p (%s)" % (" ".join(names_), " ".join(names_))), rd=[tk_], wr=[Tok()])
            for nm_, t_ in (("W2", W2), ("W3", W3)):
                dd_ = nc.dram_tensor("dbg_%s_%d" % (nm_, idx), [128, 64], F32, kind="ExternalOutput").ap()
                P.dma(dd_[:, :], t_[:].rearrange("p a b c -> p (a b c)"), rd=[t_W], wr=[Tok()])
        P.barrier()
        if STAGE == 1:
            sa.close()
            return
        kdup = sb([128, 8, 2, 128], BF16, sa)
        uT3 = sb([128, 4, 256], BF16, sa)
        t_kdup = Tok()
        for hk in range(2):
            for cp in range(2):
                P.copy(kdup[:, :, hk, cp * 64:(cp + 1) * 64], w_in[:, :, KO + hk * 64:KO + (hk + 1) * 64], rd=[t_w],
                       wr=[t_kdup], eng='pool')
        kTt = sb([128, 2, 256], BF16, sa)
        t_kT = Tok()
        Vt = sb([128, 2, 65], BF16, sa)
        t_Vt = Tok()
        Gblk = sb([128, 64, 32], F32, sa)
        t_G = Tok()
        t_Gd = [[Tok() for _ in range(NTL)] for _ in range(NS)]
        t_Hd = [[Tok() for _ in range(NTL)] for _ in range(NS)]
        t_kd = [Tok() for _ in range(NS)]
        t_vd = [Tok() for _ in range(NS)]
        P.memset(Vt[:], 1.0, wr=[t_Vt], eng='pool')

        def gslot(s, tl, sub):
            return (2 * (s * NTL + tl) + sub) % 4

        def norm_tile(s, tl, which):
            for sub in range(2):
                n = 2 * tl + sub
                C.load_h(idx, s, n, gslot(s, tl, sub))
                C.norm_T(gslot(s, tl, sub), which, lambda k, sub=sub: aT[:, k, sub * 128:(sub + 1) * 128], t_aT)

        def rope(dst, raw, t_raw, tl, t_dst):
            P.mm(ps[6][:, 0:256], perm[:], raw, start=True, stop=True, rd=[t_raw, t_sm], wr=[t_ps[6]])
            P.tt(r1[:], raw, rcos[:], ALU.mult, rd=[t_raw, t_rope], wr=[t_r])
            P.tt(r2[:], ps[6][:, 0:256], rsin[:], ALU.mult, rd=[t_ps[6], t_rope], wr=[t_r])
            P.tt(dst, r1[:], r2[:], ALU.add, rd=[t_r], wr=[t_dst], eng='pool')

        def load_rope(tl):
            P.dma(rcos[:], EV.rope_cos[:, (tl - 1) * 256:tl * 256], wr=[t_rope])
            P.dma(rsin[:], EV.rope_sin[:, (tl - 1) * 256:tl * 256], wr=[t_rope])

        def proj_fm(m_cols, bank, ncols=256, extra=()):
            for k in range(8):
                P.mm(ps[bank][:, 0:256], m_cols(k), aT[:, k, :], start=(k == 0), stop=(k == 7),
                     rd=[t_aT, t_w] + list(extra), wr=[t_ps[bank]])

        tiles_all = [(s_, tl_) for s_ in range(NS) for tl_ in range(NTL)]

        def norm_next(s_, tl_, force=False):
            if (not PIPE) and not force:
                return
            ii = tiles_all.index((s_, tl_)) + 1
            if ii < len(tiles_all):
                s2, tl2 = tiles_all[ii]
                norm_tile(s2, tl2, 2 if tl2 == 0 else s2)
        norm_tile(0, 0, 2)
        for s in range(NS):
            for tl in range(NTL):
                which = 2 if tl == 0 else s
                if tl > 0:
                    load_rope(tl)
                for i in range(4):
                    b = i % 2
                    proj_fm(lambda k, i=i: w_in[:, k, UO + i * 128:UO + (i + 1) * 128], b)
                    P.copy(uT[:, i, :].rearrange("p (s j) -> p j s", s=8), ps[b][:, 0:256].rearrange("p (j s) -> p j s", s=8),
                           rd=[t_ps[b]], wr=[t_uT])
                if PA != 1:
                    P.ts(uT3[64:128], uT[64:128], hmask3[64:128, 2:3], None, ALU.mult, rd=[t_uT, t_sm], wr=[t_uT])
                for hk in range(2):
                    b = hk
                    proj_fm(lambda k, hk=hk: kdup[:, k, hk, :], b, extra=[t_kdup])
                    if tl == 0:
                        P.copy(kTt[:, hk, :], ps[b][:, 0:256], rd=[t_ps[b]], wr=[t_kT])
                    else:
                        P.copy(kraw[:], ps[b][:, 0:256], rd=[t_ps[b]], wr=[t_kraw])
                        rope(kTt[:, hk, :], kraw[:], t_kraw, tl, t_kT)
                P.dma(EV.kT_d[s, :, :, tl * 256:(tl + 1) * 256], kTt[:], rd=[t_kT], wr=[t_kd[s]], eng='pool')
                for sub in range(2):
                    for k in range(8):
                        P.mm(ps[2][:, 0:128], aT[:, k, sub * 128:(sub + 1) * 128], w_in[:, k, VO:VO + 128], start=(k == 0),
                             stop=(k == 7), rd=[t_aT, t_w], wr=[t_ps[2]])
                    P.copy(Vt[:, :, 0:64], ps[2][:, 0:128].rearrange("p (h e) -> p h e", h=2), rd=[t_ps[2]], wr=[t_Vt])
                    P.dma(EV.V_d[s, 2 * tl + sub], Vt[:].rearrange("p h e -> p (h e)"), rd=[t_Vt], wr=[t_vd[s]], eng='pool')
                norm_next(s, tl)
                for q in range(4):
                    rows = slice(32 * q, 32 * q + 32) if q < 3 else slice(64, 128)
                    usrc = uT if q < 3 else uT3
                    bank = 3 + q
                    for d in range({0: 2, 1: 0, 2: 0, 3: 1}[PA]):
                        for ri in range(2):
                            for i in range(4):
                                col = ((d * 2 + ri) * 4 + i) * 32
                                for s8 in range(8):
                                    if d == 0:
                                        rhs = usrc[rows, i, s8 * 32:(s8 + 1) * 32]
                                    else:
                                        rhs = usrc[rows, i, s8 * 32 + 31:(s8 * 32 - 1 if s8 > 0 else None):-1]
                                    P.mm(ps[bank][:, col:col + 32], Gam[rows, d, i, s8, ri, :], rhs, start=(s8 == 0),
                                         stop=(s8 == 7), rd=[t_Gam, t_uT], wr=[t_ps[bank]])
                    if PA == 0:
                        P.copy(Gblk[:, q::4, :], ps[bank][:, :].rearrange("p (r k) -> p r k", r=16),
                               rd=[t_ps[bank]], wr=[t_G])
                P.dma(EV.G_d[s, tl], Gblk[:].rearrange("p a k -> p (a k)"), rd=[t_G], wr=[t_Gd[s][tl]], eng='pool')
                if not PIPE:
                    norm_next(s, tl, force=True)
        P.barrier()
        sa.close()
        if STAGE == 2:
            return
        ss = ExitStack()
        _sb0 = sb
        sb = lambda shape, dtype: _sb0(shape, dtype, ss)
        NCH = NS * 2
        Gs = [sb([128, NCH, 2, 16, 32], F32) for _ in range(2)]
        t_Gs = [Tok(), Tok()]
        Hx = sb([128, 33, NCH, 2, 16], F32)
        t_Hx = Tok()
        Hb = sb([128, NCH, 2, 16, 32], BF16)
        t_Hb = Tok()
        W2c = sb([128, NCH, 2, 16], F32)
        W3c = sb([128, NCH, 2, 16], F32)
        sc1 = sb([128, NCH, 2, 16], F32)
        sc2 = sb([128, NCH, 2, 16], F32)
        sc3 = sb([128, NCH, 2, 16], F32)
        t_sa, t_sb, t_sc = Tok(), Tok(), Tok()
        for s in range(NS):
            for d in range(2):
                P.copy(W2c[:, s * 2 + d], W2[:, d], rd=[t_W], wr=[t_W], eng='pool')
                P.copy(W3c[:, s * 2 + d], W3[:, d], rd=[t_W], wr=[t_W], eng='pool')
        order = ([i for i in range(NTL)], [0] + list(range(NTL - 1, 0, -1)))
        P.memset(Hx[:, 0], 0.0, wr=[t_Hx])

        def load_g(step):
            g, tg = Gs[step % 2], t_Gs[step % 2]
            for s in range(NS):
                for d in range(2):
                    tl_ = order[d][step]
                    P.dma(g[:, s * 2 + d].rearrange("p a r k -> p (a r k)"), EV.G_d[s, tl_, :, d * 1024:(d + 1) * 1024],
                          rd=[t_Gd[s][tl_]], wr=[tg])
        load_g(0)
        for step in range(NTL):
            if step + 1 < NTL:
                load_g(step + 1)
            g, tg = Gs[step % 2], t_Gs[step % 2]
            for k in range(32):
                cur = Hx[:, k]
                P.tt(sc1[:], W2c[:], cur, ALU.mult, rd=[t_Hx, t_W], wr=[t_sa])
                P.tt(sc2[:], W3c[:], cur[:, :, ::-1, :], ALU.mult, rd=[t_Hx, t_W], wr=[t_sb])
                P.tt(sc3[:], sc1[:], g[:, :, :, :, k], ALU.add, rd=[t_sa, tg], wr=[t_sc])
                P.tt(Hx[:, k + 1], sc2[:], sc3[:], ALU.add, rd=[t_sb, t_sc], wr=[t_Hx])
            for s in range(NS):
                for d in range(2):
                    c = s * 2 + d
                    src = Hx[:, 0:32, c].rearrange("p k a r -> p a r k")
                    if d == 0:
                        P.copy(Hb[:, c], src, rd=[t_Hx], wr=[t_Hb], eng='pool')
                    else:
                        P.copy(Hb[:, c, :, :, ::-1], src, rd=[t_Hx], wr=[t_Hb], eng='pool')
                    tl_ = order[d][step]
                    P.dma(EV.H_d[s, tl_, :, d * 1024:(d + 1) * 1024], Hb[:, c].rearrange("p a r k -> p (a r k)"),
                          rd=[t_Hb], wr=[t_Hd[s][tl_]], eng='sp')
            P.copy(Hx[:, 0], Hx[:, 32], rd=[t_Hx], wr=[t_Hx])
        P.barrier()
        ss.close()
        sb = _sb0
        if STAGE == 3:
            return
        w_out = sb([128, 8, 1024], BF16)
        gluw = sb([128, 4, 512], BF16)
        qraw = sb([128, 256], BF16)
        t_qraw = Tok()
        qT = sb([128, 4, 256], BF16)
        t_qT = Tok()
        sgA = sb([128, 4, 256], BF16)
        sgS = sb([128, 4, 256], BF16)
        t_sg = Tok()
        Hin = sb([128, 2, 2, 16, 32], BF16)
        t_Hin = Tok()
        Zs = [sb([32, 8, 4, 32], BF16) for _ in range(2)]
        t_Zs = [Tok(), Tok()]
        yT = sb([128, 4, 256], F32)
        t_yT = Tok()
        zT = sb([128, 4, 256], BF16)
        t_zT = Tok()
        sig = sb([128, 256], BF16)
        t_sig = Tok()
        mixT = sb([128, 8, 256], BF16)
        t_mix = Tok()
        kctx = sb([128, 2, 256], BF16)
        kwin = sb([128, 2, 512], BF16)
        t_kw = Tok()
        vctx = sb([128, 2, 130], BF16)
        vwin = sb([128, 4, 130], BF16)
        t_vw = Tok()
        Pt = [sb([128, 5, 128], BF16) for _ in range(2)]
        t_Pt = [Tok(), Tok()]
        o_tm = sb([128, 512], BF16)
        t_otm = Tok()
        rden = sb([128, 8], F32)
        t_rden = Tok()
        tmp = sb([128, 1024], F32)
        t_tmp = Tok()
        g1 = tmp[:].rearrange("p (a b) -> p a b", a=4)
        t_g = t_tmp
        load_weight_bf16(C, w_out, EV.w_out[j], 8, 1024, t_w, [tmp], [t_tmp])
        load_weight_bf16(C, gluw, EV.glu_w[j], 4, 512, t_w, [tmp], [t_tmp])
        hn = [sb([128, 1024], F32) for _ in range(2)]
        t_hn = [Tok(), Tok()]
        hcnt = [0]
        for s in range(NS):
            P.dma(kctx[:], EV.kT_d[s, :, :, 0:256], rd=[t_kd[s]], wr=[t_kw])
            P.dma(vctx[:].rearrange("p a e -> p a e"), EV.V_d[s, 0:2].rearrange("a p e -> p a e"), rd=[t_vd[s]], wr=[t_vw])
            if s == 0:
                norm_tile(0, 0, 2)
            for tl in range(NTL):
                which = 2 if tl == 0 else s
                if tl > 0:
                    load_rope(tl)
                    lo = max(256, tl * 256 - 128)
                    hi = min(TOK, tl * 256 + 384)
                    off = lo - (tl * 256 - 128)
                    P.dma(kwin[:, :, off:off + (hi - lo)], EV.kT_d[s, :, :, lo:hi], rd=[t_kd[s]], wr=[t_kw])
                    n0 = 2 * tl - 1
                    for a in range(4):
                        n = n0 + a
                        if 2 <= n < NSUB:
                            P.dma(vwin[:, a, :], EV.V_d[s, n], rd=[t_vd[s]], wr=[t_vw])
                P.dma(Hin[:].rearrange("p d a r k -> p (d a r k)"), EV.H_d[s, tl], rd=[t_Hd[s][tl]], wr=[t_Hin])
                for m in range(4):
                    b = m % 2
                    proj_fm(lambda k, m=m: w_in[:, k, QO + m * 128:QO + (m + 1) * 128], b)
                    if tl == 0:
                        P.copy(qT[:, m, :], ps[b][:, 0:256], rd=[t_ps[b]], wr=[t_qT])
                    else:
                        P.copy(qraw[:], ps[b][:, 0:256], rd=[t_ps[b]], wr=[t_qraw])
                        rope(qT[:, m, :], qraw[:], t_qraw, tl, t_qT)
                for m in range(4):
                    b = m % 2
                    proj_fm(lambda k, m=m: w_in[:, k, GAO + m * 128:GAO + (m + 1) * 128], b)
                    P.act(sgA[:, m, :], ps[b][:, 0:256], AF.Silu, rd=[t_ps[b]], wr=[t_sg])
                for m in range(4):
                    b = m % 2
                    proj_fm(lambda k, m=m: w_in[:, k, GSO + m * 128:GSO + (m + 1) * 128], b)
                    P.act(sgS[:, m, :], ps[b][:, 0:256], AF.Silu, rd=[t_ps[b]], wr=[t_sg])
                for i in range(4):
                    b = i % 2
                    proj_fm(lambda k, i=i: w_in[:, k, UO + i * 128:UO + (i + 1) * 128], b)
                    P.copy(uT[:, i, :].rearrange("p (s j) -> p j s", s=8), ps[b][:, 0:256].rearrange("p (j s) -> p j s", s=8),
                           rd=[t_ps[b]], wr=[t_uT])
                norm_next(s, tl)
                def z_part(i):
                    zb = (0, 1) if i % 2 == 0 else (4, 5)
                    for q in range(4):
                        pr = 4 * i + q
                        cq = 32 * q if q < 3 else 128
                        bk = zb[q // 2]
                        cnt = 0
                        for d in range(2):
                            for ri in range(2):
                                cnt += 1
                                P.mm(ps[bk][0:32, (q % 2) * 256:(q % 2) * 256 + 256], Hin[:, d, ri, pr, :],
                                     Psi[:, d, ri, pr].rearrange("p t c -> p (t c)"), start=(cnt == 1), stop=(cnt == 4),
                                     rd=[t_Psi, t_Hin], wr=[t_ps[bk]])
                    for hb_ in range(2):
                        P.act(Zs[i % 2][:, :, 2 * hb_:2 * hb_ + 2, :],
                              ps[zb[hb_]][0:32, :].rearrange("p (q t c) -> p t q c", q=2, t=8), AF.Identity,
                              rd=[t_ps[zb[hb_]]], wr=[t_Zs[i % 2]])
                z_part(0)
                for i in range(4):
                    bank = 2 + i % 2
                    if i + 1 < 4:
                        z_part(i + 1)
                    for li_, tau in enumerate([0] + [v for v in range(-7, 8) if v != 0]):
                        t0, t1 = max(0, tau), min(7, 7 + tau)
                        P.mm(ps[bank][:, t0 * 32:(t1 + 1) * 32], Kmat[:, i, 7 + tau, :],
                             uT[:, i, (t0 - tau) * 32:(t1 - tau + 1) * 32], start=(li_ == 0), stop=False,
                             rd=[t_K, t_uT], wr=[t_ps[bank]], sgc=True)
                    for t in range(8):
                        P.mm(ps[bank][:, t * 32:(t + 1) * 32], Zs[i % 2][:, t].rearrange("p q c -> p (q c)"), C.ident[0:32, 0:32],
                             start=False, stop=(t == 7), rd=[t_Zs[i % 2], C.t_ident], wr=[t_ps[bank]], sgc=True)
                    P.stt(yT[:, i, :].rearrange("p (k t) -> p t k", t=8), uT[:, i, :].rearrange("p (t k) -> p t k", t=8),
                          dvec[:, i:i + 1], ps[bank][:, 0:256].rearrange("p (t k) -> p t k", t=8), ALU.mult, ALU.add,
                          rd=[t_uT, t_sm, t_ps[bank]], wr=[t_yT])
                P.tt(g1, yT[:], yT[:], ALU.mult, rd=[t_yT], wr=[t_g])
                P.ts(g1, g1, 0.044715, 1.0, ALU.mult, ALU.add, rd=[t_g], wr=[t_g])
                P.tt(g1, g1, yT[:], ALU.mult, rd=[t_g, t_yT], wr=[t_g])
                P.act(g1, g1, AF.Sigmoid, rd=[t_g], wr=[t_g], scale=1.5957691216057308)
                P.tt(zT[:], g1, yT[:], ALU.mult, rd=[t_g, t_yT], wr=[t_zT])
                def key_blocks(qb):
                    n = 2 * tl + qb
                    kbs = []
                    if tl > 0:
                        for a in range(3):
                            nn = n - 1 + a
                            if 2 <= nn < NSUB:
                                wa = qb + a
                                kbs.append(('w', wa, [0, None, 1][a]))
                    kbs.append(('c', 0, None))
                    kbs.append(('c', 1, None))
                    return kbs

                def att_scores(qb, h):
                    kbs = key_blocks(qb)
                    nb = len(kbs)
                    qs = slice(qb * 128, (qb + 1) * 128)
                    hk, m, half = h // 4, h // 2, h % 2
                    hr = slice(64 * half, 64 * half + 64)
                    pt, tpt = Pt[h % 2], t_Pt[h % 2]
                    for bi in range(nb):
                        kind, wa, mk = kbs[bi]
                        ksrc = kwin[hr, hk, wa * 128:(wa + 1) * 128] if kind == 'w' else kctx[hr, hk, wa * 128:(wa + 1) * 128]
                        bnk = (4 + half) if bi < 4 else (6, 0)[half]
                        col = (bi % 4) * 128
                        P.mm(ps[bnk][:, col:col + 128], ksrc, qT[hr, m, qs], start=True, stop=True, rd=[t_kw, t_qT],
                             wr=[t_ps[bnk]])
                    n4 = min(nb, 4)
                    P.act(pt[:, 0:n4, :], ps[4 + half][:, 0:n4 * 128].rearrange("p (b q) -> p b q", b=n4), AF.Exp,
                          rd=[t_ps[4 + half]], wr=[tpt], scale=0.125)
                    if nb > 4:
                        b5 = (6, 0)[half]
                        P.act(pt[:, 4, :], ps[b5][:, 0:128], AF.Exp, rd=[t_ps[b5]], wr=[tpt], scale=0.125)
                    for bi, (kind, wa, mk) in enumerate(kbs):
                        if mk is not None:
                            P.tt(pt[:, bi, :], pt[:, bi, :], masks[:, mk, :], ALU.mult, rd=[tpt, t_sm], wr=[tpt])

                def att_pv(qb, h):
                    kbs = key_blocks(qb)
                    nb = len(kbs)
                    hk = h // 4
                    pt, tpt = Pt[h % 2], t_Pt[h % 2]
                    ob = 2 + h // 4
                    oc = (h % 4) * 65
                    for bi, (kind, wa, mk) in enumerate(kbs):
                        vsrc = vwin[:, wa, hk * 65:(hk + 1) * 65] if kind == 'w' else vctx[:, wa, hk * 65:(hk + 1) * 65]
                        P.mm(ps[ob][:, oc:oc + 65], pt[:, bi, :], vsrc, start=(bi == 0), stop=(bi == nb - 1),
                             rd=[tpt, t_vw], wr=[t_ps[ob]])

                def att_norm(qb):
                    for hb in range(2):
                        ob = 2 + hb
                        o3 = ps[ob][:, 0:260].rearrange("p (h e) -> p h e", h=4)
                        P.tt(rden[:, hb * 4:(hb + 1) * 4], o3[:, :, 64], esink[:, hb * 4:(hb + 1) * 4], ALU.add,
                             rd=[t_ps[ob], t_sm], wr=[t_rden])
                        P.recip(rden[:, hb * 4:(hb + 1) * 4], rden[:, hb * 4:(hb + 1) * 4], rd=[t_rden], wr=[t_rden])
                        P.tt(o_tm[:, hb * 256:(hb + 1) * 256].rearrange("p (h e) -> p h e", h=4), o3[:, :, 0:64],
                             rden[:, hb * 4:(hb + 1) * 4].unsqueeze(2).broadcast_to([128, 4, 64]), ALU.mult,
                             rd=[t_ps[ob], t_rden], wr=[t_otm])

                def att_finish(qb):
                    qs = slice(qb * 128, (qb + 1) * 128)
                    for m in range(4):
                        P.tr(C.pT[:, m * 128:(m + 1) * 128], o_tm[:, m * 128:(m + 1) * 128], C.ident[:], rd=[t_otm, C.t_ident],
                             wr=[C.t_pT])
                    for m in range(4):
                        P.tt(mixT[:, m, qs], C.pT[:, m * 128:(m + 1) * 128], sgA[:, m, qs], ALU.mult, rd=[C.t_pT, t_sg],
                             wr=[t_mix])

                def glu_all():
                    for m in range(4):
                        b = m % 2
                        for k in range(4):
                            P.mm(ps[b][:, 0:256], gluw[:, k, m * 128:(m + 1) * 128], zT[:, k, :], start=(k == 0), stop=(k == 3),
                                 rd=[t_zT, t_w], wr=[t_ps[b]])
                        P.act(sig[:], ps[b][:, 0:256], AF.Sigmoid, rd=[t_ps[b], t_sm], wr=[t_sig], bias=glub[:, m:m + 1])
                        P.tt(sig[:], sig[:], zT[:, m, :], ALU.mult, rd=[t_sig, t_zT], wr=[t_sig])
                        P.tt(mixT[:, 4 + m, :], sig[:], sgS[:, m, :], ALU.mult, rd=[t_sig, t_sg], wr=[t_mix])

                att_scores(0, 0)
                for h in range(8):
                    if h + 1 < 8:
                        att_scores(0, h + 1)
                    att_pv(0, h)
                att_norm(0)
                att_scores(1, 0)
                att_finish(0)
                for h in range(8):
                    if h + 1 < 8:
                        att_scores(1, h + 1)
                    att_pv(1, h)
                att_norm(1)
                glu_all()
                att_finish(1)
                for sub in range(2):
                    n = 2 * tl + sub
                    slot = gslot(s, tl, sub)
                    hi_ = hcnt[0] % 2
                    hcnt[0] += 1
                    for nb2 in range(2):
                        for k in range(8):
                            P.mm(ps[nb2][:, :], mixT[:, k, sub * 128:(sub + 1) * 128], w_out[:, k, nb2 * 512:(nb2 + 1) * 512],
                                 start=(k == 0), stop=(k == 7), rd=[t_mix, t_w], wr=[t_ps[nb2]])
                        sl = slice(nb2 * 512, (nb2 + 1) * 512)
                        P.tt(tmp[:, sl], ps[nb2][:, :], C.gate_bc[:, which, sl], ALU.mult, rd=[t_ps[nb2], C.t_mod], wr=[t_tmp])
                        P.tt(hn[hi_][:, sl], tmp[:, sl], C.hT[slot][:, sl], ALU.add, rd=[t_tmp, C.t_hT[slot]],
                             wr=[t_hn[hi_]], eng='pool')
                    C.store_h(idx, s, n, hn[hi_], t_hn[hi_], last)
                if not PIPE:
                    norm_next(s, tl, force=True)
        P.barrier()
```

```python
import numpy as np
from contextlib import ExitStack
import concourse.bass as bass
import concourse.mybir as mybir
from concourse.bass_utils import run_bass_kernel_spmd

F32 = mybir.dt.float32
BF16 = mybir.dt.bfloat16
AF = mybir.ActivationFunctionType
ALU = mybir.AluOpType

ENGS = ['pe', 'act', 'dve', 'pool', 'sp']
import os
SKIPK = int(os.environ.get('SKIPK', '0'))
PA = int(os.environ.get('PA', '0'))
PIPE = int(os.environ.get('PIPE', '1'))
STAGE = 0


class Tok:
    __slots__ = ('lw', 'rd')

    def __init__(self):
        self.lw = None
        self.rd = {}


class Prog:
    def __init__(self, nc, stack, ndq=12):
        self.nc = nc
        self.q = {e: [] for e in ENGS}
        self.cnt = {e: 0 for e in ['pe', 'act', 'dve', 'pool']}
        self.sems = {}
        self.esem = {}
        for e in ['pe', 'act', 'dve', 'pool']:
            self.esem[e] = 'c_' + e
            self.sems['c_' + e] = stack.enter_context(nc.semaphore('c_' + e))
        self.dq = {}
        for e in ['sp', 'pool', 'act']:
            names = []
            for i in range(ndq):
                n = 'd_%s%d' % (e, i)
                self.sems[n] = stack.enter_context(nc.semaphore(n))
                names.append(n)
            self.dq[e] = names
        self.dcnt = {'sp': 0, 'pool': 0, 'act': 0}
        self.waited = {e: {} for e in ENGS}
        self.pending = {e: {} for e in ENGS}

    def emit(self, eng, fn, rd=(), wr=(), dma=False):
        if dma:
            i = self.dcnt[eng]
            K = len(self.dq[eng])
            semname = self.dq[eng][i % K]
            val = 16 * (i // K + 1)
            self.dcnt[eng] += 1
            prev = (semname, val - 16) if val > 16 else None
            inc = 16
        else:
            self.cnt[eng] += 1
            semname = self.esem[eng]
            val = self.cnt[eng]
            prev = None
            inc = 1
        waits = {}

        def need(p):
            if p is None:
                return
            s, v = p
            if (not dma) and eng == 'pe' and s == 'c_pe':
                return
            if waits.get(s, 0) < v:
                waits[s] = v

        for s_, v_ in self.pending[eng].items():
            need((s_, v_))
        self.pending[eng] = {}
        need(prev)
        for t in rd:
            need(t.lw)
        for t in wr:
            need(t.lw)
            for s, v in t.rd.items():
                need((s, v))
        cache = self.waited[eng]
        w = []
        for s, v in waits.items():
            if cache.get(s, 0) < v:
                cache[s] = v
                w.append((s, v))
        self.q[eng].append((w, fn, semname, inc))
        for t in rd:
            if t.rd.get(semname, 0) < val:
                t.rd[semname] = val
        for t in wr:
            t.lw = (semname, val)
            t.rd = {}

    def barrier(self):
        cur = {}
        for e in ['pe', 'act', 'dve', 'pool']:
            if self.cnt[e] > 0:
                cur[self.esem[e]] = self.cnt[e]
        for e in ['sp', 'pool', 'act']:
            K = len(self.dq[e])
            for i in range(max(0, self.dcnt[e] - K), self.dcnt[e]):
                cur[self.dq[e][i % K]] = 16 * (i // K + 1)
        for e in ENGS:
            for s_, v_ in cur.items():
                if self.pending[e].get(s_, 0) < v_:
                    self.pending[e][s_] = v_

    def dma(self, out, in_, rd=(), wr=(), eng='sp', slow=False):
        if slow:
            self.emit(eng, lambda e: e.dma_start(out=out, in_=in_, allow_slow_non_contiguous=True), rd, wr, dma=True)
        else:
            self.emit(eng, lambda e: e.dma_start(out=out, in_=in_), rd, wr, dma=True)

    def mm(self, out, lhsT, rhs, start, stop, rd=(), wr=(), sgc=False):
        self.emit('pe', lambda e: e.matmul(out, lhsT, rhs, start=start, stop=stop, skip_group_check=sgc), rd, wr)

    def tr(self, out, in_, ident, rd=(), wr=()):
        self.emit('pe', lambda e: e.transpose(out, in_, ident), rd, wr)

    def act(self, out, in_, func, rd=(), wr=(), bias=None, scale=None, accum_out=None):
        kw = {}
        if bias is not None:
            kw['bias'] = bias
        if scale is not None:
            kw['scale'] = scale
        if accum_out is not None:
            kw['accum_out'] = accum_out
        self.emit('act', lambda e: e.activation(out, in_, func, **kw), rd, wr)

    def tt(self, out, in0, in1, op, rd=(), wr=(), eng='dve'):
        self.emit(eng, lambda e: e.tensor_tensor(out, in0, in1, op), rd, wr)

    def ts(self, out, in0, s1, s2, op0, op1=None, rd=(), wr=(), eng='dve'):
        if op1 is None:
            self.emit(eng, lambda e: e.tensor_scalar(out, in0, s1, None, op0), rd, wr)
        else:
            self.emit(eng, lambda e: e.tensor_scalar(out, in0, s1, s2, op0, op1), rd, wr)

    def stt(self, out, in0, scalar, in1, op0, op1, rd=(), wr=(), eng='dve'):
        self.emit(eng, lambda e: e.scalar_tensor_tensor(out, in0, scalar, in1, op0, op1), rd, wr)

    def copy(self, out, in_, rd=(), wr=(), eng='dve'):
        self.emit(eng, lambda e: e.tensor_copy(out, in_), rd, wr)

    def memset(self, ap, c, wr=(), eng='dve'):
        self.emit(eng, lambda e: e.memset(ap, c), (), wr)

    def recip(self, out, in_, rd=(), wr=()):
        self.emit('dve', lambda e: e.reciprocal(out, in_), rd, wr)

    def finish(self):
        nc = self.nc
        sems = self.sems
        with nc.Block() as b0:
            @b0.sync
            def _(e):
                for n, s in sems.items():
                    e.sem_clear(s)
        finals = {}
        for eng in ['sp', 'pool', 'act']:
            K = len(self.dq[eng])
            for i in range(self.dcnt[eng]):
                finals[self.dq[eng][i % K]] = 16 * (i // K + 1)
        with nc.Block() as block:
            def mk(engname):
                def body(e):
                    for (w, fn, semname, inc) in self.q[engname]:
                        for s, v in w:
                            e.wait_ge(sems[s], v)
                        ins = fn(e)
                        ins.then_inc(sems[semname], inc)
                    if engname == 'sp':
                        for s, v in finals.items():
                            e.wait_ge(sems[s], v)
                        for en in ['pe', 'act', 'dve', 'pool']:
                            if self.cnt[en] > 0:
                                e.wait_ge(sems[self.esem[en]], self.cnt[en])
                return body
            block.tensor(mk('pe'))
            block.scalar(mk('act'))
            block.vector(mk('dve'))
            block.gpsimd(mk('pool'))
            block.sync(mk('sp'))


POOL_R = [1, 2, 4, 8]


def _band_consts():
    B = np.zeros((4, 6, 128, 128), np.float32)
    tin = np.arange(128)[:, None]
    tout = np.arange(128)[None, :]
    for ri, r in enumerate(POOL_R):
        inband = (np.abs(tin - tout) <= r).astype(np.float32)
        full = 1.0 / (2 * r + 1)
        cnt_first = np.minimum(tout + r + 1, 2 * r + 1).astype(np.float32)
        cnt_last = np.minimum((127 - tout) + r + 1, 2 * r + 1).astype(np.float32)
        eye = np.eye(128, dtype=np.float32)
        B[ri, 0] = inband / cnt_first - eye
        B[ri, 1] = inband * full - eye
        B[ri, 2] = inband / cnt_last - eye
        B[ri, 3] = (np.abs(tin - 128 - tout) <= r).astype(np.float32) * full
        B[ri, 4] = (np.abs(tin + 128 - tout) <= r).astype(np.float32) * full
        cnt_both = (np.minimum(tout + r, r) + np.minimum(127 - tout, r) + 1).astype(np.float32)
        B[ri, 5] = inband / cnt_both - eye
    return B


class Ctx:
    pass


def build_program(NS, NT, layers, do_final, dbg=False):
    nc = bass.Bass("TRN2", target_bir_lowering=False)
    LAT = NT * 256
    TOK = 256 + LAT
    dt_in = lambda n, s: nc.dram_tensor(n, s, F32, kind="ExternalInput").ap()
    x_in = dt_in("x", [NS, LAT, 1024])
    ctx_in = dt_in("ctx", [NS, 256, 1024])
    c3_in = dt_in("c3", [3, 1024])
    ada_w = dt_in("ada_w", [4, 1024, 3072])
    ada_b = dt_in("ada_b", [4, 3072])
    norm_g = dt_in("norm_g", [4, 1024])
    odd_w_in = dt_in("odd_w_in", [2, 1024, 2048])
    odd_w_out = dt_in("odd_w_out", [2, 1024, 1024])
    pool_w = dt_in("pool_w", [2, 4, 256, 256])
    pool_scale = dt_in("pool_scale", [2, 1024])
    final_g = dt_in("final_g", [1024])
    IK = "ExternalOutput" if dbg else "Internal"
    EV = Ctx()
    EV.dbg = dbg
    EV.w_in = dt_in("even_w_in", [2, 1024, 2304])
    EV.w_out = dt_in("even_w_out", [2, 1024, 1024])
    EV.sink = dt_in("attn_sink", [2, 8])
    EV.a_re = dt_in("ssm_a_re", [2, 2, 32, 64])
    EV.a_im = dt_in("ssm_a_im", [2, 2, 32, 64])
    EV.log_dt = dt_in("ssm_log_dt", [2, 2, 32])
    EV.b_re = dt_in("ssm_b_re", [2, 2, 32, 64, 16])
    EV.b_im = dt_in("ssm_b_im", [2, 2, 32, 64, 16])
    EV.c_re = dt_in("ssm_c_re", [2, 2, 32, 16, 64])
    EV.c_im = dt_in("ssm_c_im", [2, 2, 32, 16, 64])
    EV.d = dt_in("ssm_d", [2, 512])
    EV.glu_w = dt_in("glu_w", [2, 512, 512])
    EV.glu_b = dt_in("glu_b", [2, 512])
    EV.rope_cos = dt_in("rope_cos", [128, NT * 256])
    EV.rope_sin = dt_in("rope_sin", [128, NT * 256])
    EV.perm = dt_in("perm", [128, 128])
    EV.masks = dt_in("masks", [2, 128, 128])
    EV.hmask = dt_in("hmask", [128, 3])
    NSUB_ = 2 * (NT + 1)
    EV.kT_d = nc.dram_tensor("kT_d", [NS, 128, 2, 256 + NT * 256], BF16, kind=IK).ap()
    EV.V_d = nc.dram_tensor("V_d", [NS, NSUB_, 128, 130], BF16, kind=IK).ap()
    EV.G_d = nc.dram_tensor("G_d", [NS, NT + 1, 128, 2048], F32, kind=IK).ap()
    EV.H_d = nc.dram_tensor("H_d", [NS, NT + 1, 128, 2048], BF16, kind=IK).ap()
    ident_in = dt_in("ident", [128, 128])
    bands_in = dt_in("bands", [4, 6, 128, 128])
    out = nc.dram_tensor("out", [NS, LAT, 1024], F32, kind="ExternalOutput").ap()
    hbuf = [nc.dram_tensor("hbuf%d" % i, [NS, TOK, 1024], F32, kind=("ExternalOutput" if dbg else "Internal")).ap()
            for i in range(2)]
    modrow = nc.dram_tensor("modrow", [4, 3, 3072], F32, kind="Internal").ap()

    with ExitStack() as st:
        P = Prog(nc, st)
        C = Ctx()
        C.nc, C.P, C.st = nc, P, st
        C.NS, C.NT, C.TOK, C.LAT = NS, NT, TOK, LAT
        sbc = [0]

        def sb(shape, dtype, name=None):
            sbc[0] += 1
            return st.enter_context(nc.sbuf_tensor(name or ("sb%d" % sbc[0]), shape, dtype))
        C.sb = sb
        C.ps = [st.enter_context(nc.psum_tensor("ps%d" % i, [128, 512], F32)) for i in range(7)]
        C.t_ps = [Tok() for _ in range(7)]
        t_modrow = Tok()
        t_h = [[[Tok() for _ in range(2 * (NT + 1))] for _ in range(NS)] for _ in range(2)]
        t_out = Tok()

        ident_f = sb([128, 128], F32)
        ident = sb([128, 128], BF16)
        t_ident = Tok()
        P.dma(ident_f[:], ident_in[:, :], wr=[t_ident])
        P.copy(ident[:], ident_f[:], rd=[t_ident], wr=[t_ident])
        C.ident, C.t_ident = ident, t_ident
        C.ident_f = ident_f
        fg_bc = sb([128, 1024], F32)
        t_fg = Tok()
        P.dma(fg_bc[:], final_g.partition_broadcast(128), wr=[t_fg])

        sT = sb([128, 8, 3], F32)
        t_sT = Tok()
        for r in range(3):
            P.dma(sT[:, :, r], c3_in[r, :].rearrange("(k p) -> p k", p=128), wr=[t_sT], slow=True)
        P.act(sT[:], sT[:], AF.Silu, rd=[t_sT], wr=[t_sT])
        ones1 = sb([1, 4], F32)
        t_ones1 = Tok()
        P.memset(ones1[:], 1.0, wr=[t_ones1])
        with ExitStack() as st2:
            wt = [st2.enter_context(nc.sbuf_tensor("adaw%d" % i, [128, 3072], F32)) for i in range(2)]
            t_wt = [Tok(), Tok()]
            brow = st2.enter_context(nc.sbuf_tensor("adab", [1, 3072], F32))
            t_brow = Tok()
            mrow = st2.enter_context(nc.sbuf_tensor("mrow", [3, 3072], F32))
            t_mrow = Tok()
            for L in layers:
                P.dma(brow[:], ada_b[L:L + 1, :], wr=[t_brow])
                for k in range(8):
                    P.dma(wt[k % 2][:], ada_w[L, k * 128:(k + 1) * 128, :], wr=[t_wt[k % 2]])
                    for n in range(6):
                        P.mm(C.ps[n][0:3, :], sT[:, k, :], wt[k % 2][:, n * 512:(n + 1) * 512], start=(k == 0),
                             stop=False, rd=[t_sT, t_wt[k % 2]], wr=[C.t_ps[n]])
                for n in range(6):
                    P.mm(C.ps[n][0:3, :], ones1[0:1, 0:3], brow[0:1, n * 512:(n + 1) * 512], start=False, stop=True,
                         rd=[t_ones1, t_brow], wr=[C.t_ps[n]])
                    P.copy(mrow[:, n * 512:(n + 1) * 512], C.ps[n][0:3, :], rd=[C.t_ps[n]], wr=[t_mrow])
                P.dma(modrow[L], mrow[:], rd=[t_mrow], wr=[t_modrow])

        gT = sb([128, 8], F32)
        scT = sb([128, 3, 8], F32)
        gsT = sb([128, 3, 8], F32)
        shT = sb([128, 3, 8], F32)
        gate_bc = sb([128, 3, 1024], F32)
        t_mod = Tok()
        C.gsT, C.shT, C.gate_bc, C.t_mod = gsT, shT, gate_bc, t_mod

        def load_mod(L):
            P.dma(gT[:], norm_g[L, :].rearrange("(k p) -> p k", p=128), wr=[t_mod], slow=True)
            for w in range(3):
                P.dma(shT[:, w, :], modrow[L, w, 0:1024].rearrange("(k p) -> p k", p=128), rd=[t_modrow],
                      wr=[t_mod], slow=True)
                P.dma(scT[:, w, :], modrow[L, w, 1024:2048].rearrange("(k p) -> p k", p=128), rd=[t_modrow],
                      wr=[t_mod], slow=True)
                P.dma(gate_bc[:, w, :], modrow[L, w, 2048:3072].partition_broadcast(128), rd=[t_modrow], wr=[t_mod])
                P.stt(gsT[:, w, :], scT[:, w, :], 1.0, gT[:], ALU.add, ALU.mult, rd=[t_mod], wr=[t_mod])

        def h_src(li, s, n):
            if li == 0:
                if n < 2:
                    return ctx_in[s, n * 128:(n + 1) * 128, :], None
                return x_in[s, (n - 2) * 128:(n - 1) * 128, :], None
            b = (li - 1) % 2
            return hbuf[b][s, n * 128:(n + 1) * 128, :], t_h[b][s][n]

        def h_dst(li, s, n):
            b = li % 2
            return hbuf[b][s, n * 128:(n + 1) * 128, :], t_h[b][s][n]
        C.h_src, C.h_dst = h_src, h_dst

        NH = 4
        C.hT = [sb([128, 1024], F32) for _ in range(NH)]
        C.t_hT = [Tok() for _ in range(NH)]
        C.stat = [sb([128, 4], F32) for _ in range(2)]
        C.t_stat = [Tok(), Tok()]
        C.xn = [sb([128, 1024], BF16) for _ in range(2)]
        C.t_xn = [Tok(), Tok()]
        C.pT = st.enter_context(nc.psum_tensor("psT", [128, 1024], BF16))
        C.t_pT = Tok()
        C.eps = sb([128, 1], F32)
        C.t_eps = Tok()
        P.memset(C.eps[:], 1e-6, wr=[C.t_eps])
        C.nrm_i = 0

        def load_h(li, s, n, slot):
            src, tk = h_src(li, s, n)
            P.dma(C.hT[slot][:], src, rd=([tk] if tk else []), wr=[C.t_hT[slot]])
        C.load_h = load_h

        def norm_T(slot, which, aT_ap_fn, t_aT):
            i = C.nrm_i % 2
            C.nrm_i += 1
            h = C.hT[slot]
            stt_ = C.stat[i]
            P.act(C.xn[i][:], h[:], AF.Square, rd=[C.t_hT[slot]], wr=[C.t_xn[i], C.t_stat[i]], accum_out=stt_[:, 0:1])
            P.act(stt_[:, 1:2], stt_[:, 0:1], AF.Sqrt, rd=[C.t_stat[i], C.t_eps], wr=[C.t_stat[i]],
                  bias=C.eps[:, 0:1], scale=1.0 / 1024)
            P.recip(stt_[:, 2:3], stt_[:, 1:2], rd=[C.t_stat[i]], wr=[C.t_stat[i]])
            P.ts(C.xn[i][:], h[:], stt_[:, 2:3], None, ALU.mult, rd=[C.t_hT[slot], C.t_stat[i]], wr=[C.t_xn[i]])
            for k in range(8):
                P.tr(C.pT[:, k * 128:(k + 1) * 128], C.xn[i][:, k * 128:(k + 1) * 128], C.ident[:],
                     rd=[C.t_xn[i], C.t_ident], wr=[C.t_pT])
            for k in range(8):
                P.act(aT_ap_fn(k), C.pT[:, k * 128:(k + 1) * 128], AF.Identity, rd=[C.t_pT, C.t_mod], wr=[t_aT],
                      bias=C.shT[:, which, k:k + 1], scale=C.gsT[:, which, k:k + 1])
        C.norm_T = norm_T

        def store_h(li, s, n, hn, t_hn, last):
            if last:
                if n < 2:
                    return
                i = C.nrm_i % 2
                C.nrm_i += 1
                stt_ = C.stat[i]
                P.act(C.xn[i][:], hn[:], AF.Square, rd=[t_hn], wr=[C.t_xn[i], C.t_stat[i]], accum_out=stt_[:, 0:1])
                P.act(stt_[:, 1:2], stt_[:, 0:1], AF.Sqrt, rd=[C.t_stat[i], C.t_eps], wr=[C.t_stat[i]],
                      bias=C.eps[:, 0:1], scale=1.0 / 1024)
                P.recip(stt_[:, 2:3], stt_[:, 1:2], rd=[C.t_stat[i]], wr=[C.t_stat[i]])
                P.stt(hn[:], hn[:], stt_[:, 2:3], fg_bc[:], ALU.mult, ALU.mult, rd=[t_hn, C.t_stat[i], t_fg],
                      wr=[t_hn])
                P.dma(out[s, (n - 2) * 128:(n - 1) * 128, :], hn[:], rd=[t_hn], wr=[t_out], eng='pool')
            else:
                dst, tk = h_dst(li, s, n)
                P.dma(dst, hn[:], rd=[t_hn], wr=[tk], eng='pool')
        C.store_h = store_h

        for idx, L in enumerate(layers):
            last = do_final and (idx == len(layers) - 1)
            need_ctx = not last
            load_mod(L)
            P.barrier()
            if L % 2 == 1:
                odd_layer(C, L, idx, odd_w_in[L // 2], odd_w_out[L // 2], pool_w[L // 2], pool_scale[L // 2],
                          bands_in, need_ctx, last)
            else:
                even_layer(C, L, idx, EV, need_ctx, last)
        P.finish()
    return nc


def load_weight_bf16(C, dst, src_ap, rows_k, cols, t_dst, stage, t_stage):
    P = C.P
    for k in range(rows_k):
        i = k % len(stage)
        P.dma(stage[i][:, 0:cols], src_ap[k * 128:(k + 1) * 128, :], wr=[t_stage[i]])
        P.copy(dst[:, k, :], stage[i][:, 0:cols], rd=[t_stage[i]], wr=[t_dst], eng='pool')


def odd_layer(C, L, idx, w_in_d, w_out_d, pool_w_d, pool_scale_d, bands_in, need_ctx, last):
    nc, P, NS, NT = C.nc, C.P, C.NS, C.NT
    with ExitStack() as st:
        sb = lambda shape, dtype, name: st.enter_context(nc.sbuf_tensor("o%d_%s" % (idx, name), shape, dtype))
        w_in = sb([128, 8, 2048], BF16, "w_in")
        w_out = sb([128, 8, 1024], BF16, "w_out")
        pw = sb([128, 8, 256], BF16, "pw")
        bands = sb([128, 4, 6, 128], BF16, "bands")
        psc = sb([128, 8], F32, "psc")
        stage = [sb([128, 2048], F32, "stage%d" % i) for i in range(2)]
        t_stage = [Tok(), Tok()]
        t_w = Tok()
        load_weight_bf16(C, w_in, w_in_d, 8, 2048, t_w, stage, t_stage)
        load_weight_bf16(C, w_out, w_out_d, 8, 1024, t_w, stage, t_stage)
        load_weight_bf16(C, pw, pool_w_d.rearrange("g c d -> (g c) d"), 8, 256, t_w, stage, t_stage)
        for ri in range(4):
            i = ri % 2
            P.dma(stage[i][:, 0:768].rearrange("p (v t) -> p v t", v=6), bands_in[ri].rearrange("v p t -> p v t"),
                  wr=[t_stage[i]])
            P.copy(bands[:, ri, :, :], stage[i][:, 0:768].rearrange("p (v t) -> p v t", v=6), rd=[t_stage[i]],
                   wr=[t_w], eng='pool')
        P.dma(psc[:], pool_scale_d.rearrange("(k p) -> p k", p=128), wr=[t_w], slow=True)

        NR = 4
        aT = [sb([128, 8, 128], BF16, "aT%d" % i) for i in range(2)]
        t_aT = [Tok(), Tok()]
        u_tm = [sb([128, 1024], BF16, "u%d" % i) for i in range(NR)]
        t_u = [Tok() for _ in range(NR)]
        sg = [sb([128, 8, 128], BF16, "sg%d" % i) for i in range(NR)]
        t_sg = [Tok() for _ in range(NR)]
        pTt = sb([128, 8, 128], BF16, "pTt")
        t_pTt = Tok()
        mT = sb([128, 8, 128], BF16, "mT")
        t_mT = Tok()
        tmp = sb([128, 1024], F32, "tmp")
        t_tmp = Tok()
        hn = [sb([128, 1024], F32, "hn%d" % i) for i in range(2)]
        t_hn = [Tok(), Tok()]
        ps, t_ps = C.ps, C.t_ps
        fin_i = [0]

        for s in range(NS):
            seqs = []
            if need_ctx:
                seqs.append((2, [0, 1]))
            seqs.append((s, list(range(2, 2 * (NT + 1)))))
            for which, subs in seqs:
                S = len(subs)

                BA, BC, BB, BD, BE = (5, 6), (1, 2), (0, 3), (4, 1), (2, 6)

                def f_norm(j):
                    n = subs[j]
                    slot = n % 4
                    C.load_h(idx, s, n, slot)
                    a = aT[j % 2]
                    C.norm_T(slot, which, lambda k: a[:, k, :], t_aT[j % 2])

                def f_u(j):
                    a = aT[j % 2]
                    r = j % NR
                    for nb in range(2):
                        bk = BA[nb]
                        for k in range(8):
                            P.mm(ps[bk][:, :], a[:, k, :], w_in[:, k, nb * 512:(nb + 1) * 512], start=(k == 0),
                                 stop=(k == 7), rd=[t_aT[j % 2], t_w], wr=[t_ps[bk]])
                        P.copy(u_tm[r][:, nb * 512:(nb + 1) * 512], ps[bk][:, :], rd=[t_ps[bk]], wr=[t_u[r]])

                def f_gate(j):
                    a = aT[j % 2]
                    r = j % NR
                    for m in range(8):
                        bank = BB[m // 4]
                        for k in range(8):
                            P.mm(ps[bank][:, (m % 4) * 128:(m % 4 + 1) * 128], w_in[:, k, 1024 + m * 128:1024 + (m + 1) * 128],
                                 a[:, k, :], start=(k == 0), stop=(k == 7), rd=[t_aT[j % 2], t_w], wr=[t_ps[bank]])
                    for b2 in range(2):
                        P.act(sg[r][:, b2 * 4:(b2 + 1) * 4, :], ps[BB[b2]][:, :].rearrange("p (m t) -> p m t", m=4),
                              AF.Silu, rd=[t_ps[BB[b2]]], wr=[t_sg[r]])

                def b_pool(j):
                    r = j % NR
                    if S == 1:
                        v = 5
                    elif j == 0:
                        v = 0
                    elif j == S - 1:
                        v = 2
                    else:
                        v = 1
                    for m in range(8):
                        bank = BC[m // 4]
                        ri = m // 2
                        dst = ps[bank][:, (m % 4) * 128:(m % 4 + 1) * 128]
                        parts = [(u_tm[r][:, m * 128:(m + 1) * 128], bands[:, ri, v, :], t_u[r])]
                        if j > 0:
                            rp = (j - 1) % NR
                            parts.append((u_tm[rp][64:128, m * 128:(m + 1) * 128], bands[64:128, ri, 3, :], t_u[rp]))
                        if j < S - 1:
                            rn = (j + 1) % NR
                            parts.append((u_tm[rn][0:32, m * 128:(m + 1) * 128], bands[0:32, ri, 4, :], t_u[rn]))
                        for pi, (l, rr, tk) in enumerate(parts):
                            P.mm(dst, l, rr, start=(pi == 0), stop=(pi == len(parts) - 1), rd=[tk, t_w], wr=[t_ps[bank]])
                    for b2 in range(2):
                        P.copy(pTt[:, b2 * 4:(b2 + 1) * 4, :], ps[BC[b2]][:, :].rearrange("p (m t) -> p m t", m=4),
                               rd=[t_ps[BC[b2]]], wr=[t_pTt])

                def b_pw(j):
                    r = j % NR
                    for g in range(4):
                        for dd in range(2):
                            m = g * 2 + dd
                            bank = BD[m // 4]
                            for cc in range(2):
                                P.mm(ps[bank][:, (m % 4) * 128:(m % 4 + 1) * 128], pw[:, g * 2 + cc, dd * 128:(dd + 1) * 128],
                                     pTt[:, g * 2 + cc, :], start=(cc == 0), stop=(cc == 1), rd=[t_pTt, t_w],
                                     wr=[t_ps[bank]])
                    for m in range(8):
                        bank = BD[m // 4]
                        P.stt(mT[:, m, :], ps[bank][:, (m % 4) * 128:(m % 4 + 1) * 128], psc[:, m:m + 1], sg[r][:, m, :],
                              ALU.mult, ALU.mult, rd=[t_ps[bank], t_sg[r], t_w], wr=[t_mT])

                def b_out(j):
                    n = subs[j]
                    slot = n % 4
                    hi = fin_i[0] % 2
                    fin_i[0] += 1
                    for nb in range(2):
                        bk = BE[nb]
                        for k in range(8):
                            P.mm(ps[bk][:, :], mT[:, k, :], w_out[:, k, nb * 512:(nb + 1) * 512], start=(k == 0),
                                 stop=(k == 7), rd=[t_mT, t_w], wr=[t_ps[bk]])
                        sl = slice(nb * 512, (nb + 1) * 512)
                        P.tt(tmp[:, sl], ps[bk][:, :], C.gate_bc[:, which, sl], ALU.mult,
                             rd=[t_ps[bk], C.t_mod], wr=[t_tmp])
                        P.tt(hn[hi][:, sl], tmp[:, sl], C.hT[slot][:, sl], ALU.add, rd=[t_tmp, C.t_hT[slot]],
                             wr=[t_hn[hi]], eng='pool')
                    C.store_h(idx, s, n, hn[hi], t_hn[hi], last)

                f_norm(0)
                for j in range(S + 1):
                    if j < S:
                        f_u(j)
                    if j >= 1:
                        b_pool(j - 1)
                    if j < S:
                        f_gate(j)
                    if j >= 1:
                        b_pw(j - 1)
                    if j + 1 < S:
                        f_norm(j + 1)
                    if j >= 1:
                        b_out(j - 1)


def make_in_maps(inputs, n_cores, NS, NT):
    f = lambda a: np.ascontiguousarray(np.asarray(a, dtype=np.float32))
    x = f(inputs['x'])
    ctx = f(inputs['ctx'])
    c = f(inputs['c'])
    shared = {k: f(inputs[k]) for k in ['ada_w', 'ada_b', 'norm_g', 'odd_w_in', 'odd_w_out', 'pool_w', 'pool_scale',
                                        'final_g']}
    for k in ['even_w_in', 'even_w_out', 'attn_sink', 'ssm_a_re', 'ssm_a_im', 'ssm_log_dt', 'ssm_b_re', 'ssm_b_im',
              'ssm_c_re', 'ssm_c_im', 'ssm_d', 'glu_w', 'glu_b']:
        shared[k] = f(inputs[k])
    t = np.arange(NT * 256)
    r = np.arange(128)
    inv_freq = (10000.0 ** (-np.arange(16, dtype=np.float32) / 16)).astype(np.float32)
    pos = np.where(((r >> 5) & 1)[:, None] == 0, (t // 64)[None, :], (t % 64)[None, :]).astype(np.float32)
    ang = pos * inv_freq[r & 15][:, None]
    shared['rope_cos'] = np.cos(ang).astype(np.float32)
    sgn = np.where(((r >> 4) & 1) == 0, -1.0, 1.0).astype(np.float32)[:, None]
    shared['rope_sin'] = (np.sin(ang) * sgn).astype(np.float32)
    pm = np.zeros((128, 128), np.float32)
    pm[r ^ 16, r] = 1.0
    shared['perm'] = pm
    kk = np.arange(128)[:, None]
    qq = np.arange(128)[None, :]
    shared['masks'] = np.stack([(kk >= qq), (kk <= qq)]).astype(np.float32)
    hm = np.zeros((128, 3), np.float32)
    hm[:64, 0] = 1.0
    hm[64:, 1] = 1.0
    hm[96:, 2] = 1.0
    shared['hmask'] = hm
    shared['ident'] = np.eye(128, dtype=np.float32)
    shared['bands'] = _band_consts()
    maps = []
    for i in range(n_cores):
        m = dict(shared)
        m['x'] = np.ascontiguousarray(x[i * NS:(i + 1) * NS, :NT * 256])
        m['ctx'] = np.ascontiguousarray(ctx[i * NS:(i + 1) * NS])
        c3 = np.zeros((3, 1024), np.float32)
        c3[0:NS] = c[i * NS:(i + 1) * NS]
        c3[2] = f(inputs['c_ctx'])
        m['c3'] = c3
        maps.append(m)
    return maps


def kernel(**inputs):
    NS, NT, n_cores = 2, 16, 8
    nc = build_program(NS, NT, [0, 1, 2, 3], True)
    maps = make_in_maps(inputs, n_cores, NS, NT)
    res = run_bass_kernel_spmd(nc, maps, core_ids=list(range(n_cores)))
    return np.concatenate([r['out'] for r in res.results], axis=0)


def even_layer(C, L, idx, EV, need_ctx, last):
    nc, P, NS, NT, TOK = C.nc, C.P, C.NS, C.NT, C.TOK
    j = L // 2
    NTL = NT + 1
    NSUB = 2 * NTL
    ps, t_ps = C.ps, C.t_ps
    QO, KO, VO, GAO, UO, GSO = 0, 512, 640, 768, 1280, 1792
    PI = float(np.pi)
    with ExitStack() as st:
        sbn = [0]

        def sb(shape, dtype, stack=st):
            sbn[0] += 1
            return stack.enter_context(nc.sbuf_tensor("e%d_%d" % (idx, sbn[0]), shape, dtype))
        w_in = sb([128, 8, 2304], BF16)
        t_w = Tok()
        dvec = sb([128, 4], F32)
        glub = sb([128, 4], F32)
        esink = sb([128, 8], F32)
        t_sm = Tok()
        hmask3 = sb([128, 3], F32)
        hmask = hmask3[:, 0:2]
        perm = sb([128, 128], BF16)
        masks = sb([128, 2, 128], BF16)
        s0 = ExitStack()
        stage0 = sb([128, 2304], F32, s0)
        stage = [stage0, stage0]
        t_st0 = Tok()
        t_stage = [t_st0, t_st0]
        load_weight_bf16(C, w_in, EV.w_in[j], 8, 2304, t_w, stage, t_stage)
        P.dma(dvec[:], EV.d[j].rearrange("(k p) -> p k", p=128), wr=[t_sm], slow=True)
        P.dma(glub[:], EV.glu_b[j].rearrange("(k p) -> p k", p=128), wr=[t_sm], slow=True)
        P.dma(esink[:], EV.sink[j].partition_broadcast(128), wr=[t_sm])
        P.act(esink[:], esink[:], AF.Exp, rd=[t_sm], wr=[t_sm])
        P.dma(hmask3[:], EV.hmask[:, :], wr=[t_sm])
        P.dma(stage0[:, 0:128], EV.perm[:, :], wr=[t_st0])
        P.copy(perm[:], stage0[:, 0:128], rd=[t_st0], wr=[t_sm])
        P.dma(stage0[:, 0:256].rearrange("p (a b) -> p a b", a=2), EV.masks.rearrange("a p b -> p a b"), wr=[t_st0])
        P.copy(masks[:], stage0[:, 0:256].rearrange("p (a b) -> p a b", a=2), rd=[t_st0], wr=[t_sm])
        P.barrier()
        s0.close()
        if STAGE == 0.1:
            return
        W2 = sb([128, 2, 2, 16], F32)
        W3 = sb([128, 2, 2, 16], F32)
        t_W = Tok()
        Psi = sb([128, 2, 2, 16, 8, 32], BF16)
        Kmat = sb([128, 4, 15, 128], BF16)
        t_Gam, t_Psi, t_K = Tok(), Tok(), Tok()
        P.memset(Kmat[:], 0.0, wr=[t_K], eng='pool')
        aT = sb([128, 8, 256], BF16)
        t_aT = Tok()
        uT = sb([128, 4, 256], BF16)
        t_uT = Tok()
        kraw = sb([128, 256], BF16)
        t_kraw = Tok()
        rcos = sb([128, 256], F32)
        rsin = sb([128, 256], F32)
        t_rope = Tok()
        r1 = sb([128, 256], F32)
        r2 = sb([128, 256], F32)
        t_r = Tok()
        sa = ExitStack()
        st.enter_context(sa)
        Gam = sb([128, 2, 4, 8, 2, 128], BF16, sa)
        with ExitStack() as sp:
            f = lambda shape: sb(shape, F32, sp)
            are, aim, ldt = f([128, 2, 16]), f([128, 2, 16]), f([128, 2, 16])
            Bre, Bim, Cre, Cim = (f([128, 2, 16, 16]) for _ in range(4))
            t_p = Tok()
            for h in range(2):
                sl = slice(64 * h, 64 * h + 64)
                P.dma(are[sl], EV.a_re[j, :, h::2, :].rearrange("d r p -> p d r"), wr=[t_p], slow=True)
                P.dma(aim[sl], EV.a_im[j, :, h::2, :].rearrange("d r p -> p d r"), wr=[t_p], slow=True)
                P.dma(ldt[sl], EV.log_dt[j, :, h::2].partition_broadcast(64), wr=[t_p], slow=True)
                for d in range(2):
                    P.dma(Bre[sl, d], EV.b_re[j, d, h::2, :, :].rearrange("r p c -> p r c"), wr=[t_p], slow=True)
                    P.dma(Bim[sl, d], EV.b_im[j, d, h::2, :, :].rearrange("r p c -> p r c"), wr=[t_p], slow=True)
            ccm = f([128, 128])
            t_ccm = Tok()
            for (src, dstC) in ((EV.c_re, Cre), (EV.c_im, Cim)):
                for d in range(2):
                    for blk in range(2):
                        for r in range(8):
                            pr = blk * 8 + r
                            P.dma(ccm[16 * r:16 * r + 16, :].rearrange("c (h p) -> c h p", h=2),
                                  src[j, d, 2 * pr:2 * pr + 2, :, :].rearrange("h c p -> c h p"), wr=[t_ccm])
                        P.tr(ps[0][:, 0:128], ccm[:], C.ident_f[:], rd=[t_ccm, C.t_ident], wr=[t_ps[0]])
                        P.copy(dstC[:, d, blk * 8:(blk + 1) * 8, :], ps[0][:, 0:128].rearrange("p (r c) -> p r c", r=8),
                               rd=[t_ps[0]], wr=[t_p])
            if STAGE == 0.2:
                return
            V3 = [128, 2, 16]
            dtv, x8, th8, mag, cs, sn = (f(V3) for _ in range(6))
            halfpi = f([128, 1])
            P.memset(halfpi[:], PI / 2, wr=[t_p])
            P.act(dtv[:], ldt[:], AF.Exp, rd=[t_p], wr=[t_p])
            P.stt(x8[:], are[:], 0.125, dtv[:], ALU.mult, ALU.mult, rd=[t_p], wr=[t_p])
            P.stt(th8[:], aim[:], 0.125, dtv[:], ALU.mult, ALU.mult, rd=[t_p], wr=[t_p])
            P.act(mag[:], x8[:], AF.Exp, rd=[t_p], wr=[t_p])
            P.act(cs[:], th8[:], AF.Sin, rd=[t_p], wr=[t_p], scale=-1.0, bias=halfpi[:, 0:1])
            P.act(sn[:], th8[:], AF.Sin, rd=[t_p], wr=[t_p])
            pwr = f([128, 9, 2, 16])
            pwi = f([128, 9, 2, 16])
            ta, tb = f(V3), f(V3)
            P.tt(ta[:], mag[:], cs[:], ALU.mult, rd=[t_p], wr=[t_p])
            P.tt(tb[:], mag[:], sn[:], ALU.mult, rd=[t_p], wr=[t_p])

            def cmul(or_, oi_, ar, ai, br, bi, t1, t2):
                P.tt(t1, ar, br, ALU.mult, rd=[t_p], wr=[t_p])
                P.tt(t2, ai, bi, ALU.mult, rd=[t_p], wr=[t_p])
                P.tt(or_, t1, t2, ALU.subtract, rd=[t_p], wr=[t_p])
                P.tt(t1, ar, bi, ALU.mult, rd=[t_p], wr=[t_p])
                P.tt(t2, ai, br, ALU.mult, rd=[t_p], wr=[t_p])
                P.tt(oi_, t1, t2, ALU.add, rd=[t_p], wr=[t_p])
            t1, t2, sqr, sqi = f(V3), f(V3), dtv, x8
            cur_r, cur_i = ta, tb
            for it in range(3):
                cmul(sqr[:], sqi[:], cur_r[:], cur_i[:], cur_r[:], cur_i[:], t1[:], t2[:])
                P.copy(cur_r[:], sqr[:], rd=[t_p], wr=[t_p])
                P.copy(cur_i[:], sqi[:], rd=[t_p], wr=[t_p])
            P.memset(pwr[:, 0], 1.0, wr=[t_p])
            P.memset(pwi[:, 0], 0.0, wr=[t_p])
            P.copy(pwr[:, 1], cur_r[:], rd=[t_p], wr=[t_p])
            P.copy(pwi[:, 1], cur_i[:], rd=[t_p], wr=[t_p])
            for k in range(2, 9):
                cmul(pwr[:, k], pwi[:, k], pwr[:, k - 1], pwi[:, k - 1], pwr[:, 1], pwi[:, 1], t1[:], t2[:])
            am1, nr, ni, inv, kr, ki = (f(V3) for _ in range(6))
            P.ts(am1[:], pwr[:, 1], -1.0, None, ALU.add, rd=[t_p], wr=[t_p])
            P.tt(t1[:], am1[:], are[:], ALU.mult, rd=[t_p], wr=[t_p])
            P.tt(t2[:], pwi[:, 1], aim[:], ALU.mult, rd=[t_p], wr=[t_p])
            P.tt(nr[:], t1[:], t2[:], ALU.add, rd=[t_p], wr=[t_p])
            P.tt(t1[:], pwi[:, 1], are[:], ALU.mult, rd=[t_p], wr=[t_p])
            P.tt(t2[:], am1[:], aim[:], ALU.mult, rd=[t_p], wr=[t_p])
            P.tt(ni[:], t1[:], t2[:], ALU.subtract, rd=[t_p], wr=[t_p])
            P.tt(t1[:], are[:], are[:], ALU.mult, rd=[t_p], wr=[t_p])
            P.tt(t2[:], aim[:], aim[:], ALU.mult, rd=[t_p], wr=[t_p])
            P.tt(inv[:], t1[:], t2[:], ALU.add, rd=[t_p], wr=[t_p])
            P.recip(inv[:], inv[:], rd=[t_p], wr=[t_p])
            P.tt(kr[:], nr[:], inv[:], ALU.mult, rd=[t_p], wr=[t_p])
            P.tt(ki[:], ni[:], inv[:], ALU.mult, rd=[t_p], wr=[t_p])
            V4 = [128, 2, 16, 16]
            bc4 = lambda a: a.unsqueeze(3).broadcast_to(V4)
            Bbr, Bbi, u1, u2 = f(V4), f(V4), f(V4), f(V4)
            cmul(Bbr[:], Bbi[:], bc4(kr[:]), bc4(ki[:]), Bre[:], Bim[:], u1[:], u2[:])
            for d in range(2):
                P.copy(W2[:, d, 0, :], pwr[:, 8, d, :], rd=[t_p], wr=[t_W])
                P.copy(W2[:, d, 1, :], pwr[:, 8, d, :], rd=[t_p], wr=[t_W])
                P.ts(W3[:, d, 0, :], pwi[:, 8, d, :], -1.0, None, ALU.mult, rd=[t_p], wr=[t_W])
                P.copy(W3[:, d, 1, :], pwi[:, 8, d, :], rd=[t_p], wr=[t_W])
            CP = sb([128, 2, 2, 16, 32], BF16, sp)
            hm4 = lambda sgn_ap: sgn_ap.unsqueeze(1).unsqueeze(3).broadcast_to([128, 16, 2, 16])
            nhmask = f([128, 2])
            P.ts(nhmask[:], hmask, -1.0, None, ALU.mult, rd=[t_sm], wr=[t_p])
            for d in range(2):
                P.tt(CP[:, 0, d].rearrange("p r (h c) -> p r h c", h=2), Cre[:, d].unsqueeze(2).broadcast_to([128, 16, 2, 16]),
                     hm4(hmask), ALU.mult, rd=[t_p, t_sm], wr=[t_p])
                P.tt(CP[:, 1, d].rearrange("p r (h c) -> p r h c", h=2), Cim[:, d].unsqueeze(2).broadcast_to([128, 16, 2, 16]),
                     hm4(nhmask[:]), ALU.mult, rd=[t_p, t_sm], wr=[t_p])
            Yr, Yi = f([128, 16, 16]), f([128, 16, 16])
            bc3 = lambda a: a.unsqueeze(2).broadcast_to([128, 16, 16])
            for d in range(2):
                for t in range(8):
                    k = t + 1 if d == 0 else 8 - t
                    cmul(Yr[:], Yi[:], Cre[:, d], Cim[:, d], bc3(pwr[:, k, d, :]), bc3(pwi[:, k, d, :]), u1[:, 0], u2[:, 0])
                    for ri_, (Y_, hm_) in enumerate(((Yr, hmask), (Yi, nhmask[:]))):
                        P.tt(Psi[:, d, ri_, :, t, :].rearrange("p r (h c) -> p r h c", h=2),
                             Y_[:].unsqueeze(2).broadcast_to([128, 16, 2, 16]), hm4(hm_), ALU.mult,
                             rd=[t_p, t_sm], wr=[t_Psi])
            if STAGE == 0.3:
                return
            Xr, Xi = Yr, Yi
            XP = [sb([128, 4, 2, 16], BF16, sp) for a_ in range(2)]
            Kst = sb([32, 4, 15, 32], BF16, sp)
            t_XP = [Tok(), Tok()]
            xi = 0
            for d in range(2):
                for k in range(8):
                    cmul(Xr[:], Xi[:], Bbr[:, d], Bbi[:, d], bc3(pwr[:, k, d, :]), bc3(pwi[:, k, d, :]), u1[:, 0], u2[:, 0])
                    s_idx = 7 - k if d == 0 else k
                    for i in range(4):
                        kb = 1 + (i % 2)
                        for ri, X in enumerate((Xr, Xi)):
                            xp = XP[xi % 2]
                            txp = t_XP[xi % 2]
                            xi += 1
                            P.tt(xp[:], X[:, 4 * i:4 * i + 4, :].unsqueeze(2).broadcast_to([128, 4, 2, 16]),
                                 hmask.unsqueeze(1).unsqueeze(3).broadcast_to([128, 4, 2, 16]), ALU.mult,
                                 rd=[t_p, t_sm], wr=[txp])
                            xpf = xp[:].rearrange("p q h c -> p (q h c)")
                            P.tr(C.pT[:, 0:128], xpf, C.ident[:], rd=[txp, C.t_ident], wr=[C.t_pT])
                            P.copy(Gam[:, d, i, s_idx, ri, :], C.pT[:, 0:128], rd=[C.t_pT], wr=[t_Gam])
                            for q in range(3):
                                P.mm(ps[kb][32 * q:32 * q + 32, 32 * q:32 * q + 32], xp[:, q].rearrange("p h c -> p (h c)"),
                                     CP[:, ri, d, 4 * i + q, :], start=(ri == 0), stop=(ri == 1), rd=[txp, t_p],
                                     wr=[t_ps[kb]])
                            P.mm(ps[kb + 2][0:32, 128:160], xp[:, 3].rearrange("p h c -> p (h c)"), CP[:, ri, d, 4 * i + 3, :],
                                 start=(ri == 0), stop=(ri == 1), rd=[txp, t_p], wr=[t_ps[kb + 2]])
                        li = 7 + k if d == 0 else 7 - k
                        for q in range(4):
                            blk = slice(32 * q, 32 * q + 32)
                            if q < 3:
                                dst_, src_, tkp = Kmat[blk, i, (7 if (d == 1 and k == 0) else li), blk], ps[kb][blk, blk], t_ps[kb]
                            else:
                                dst_, src_, tkp = Kst[:, i, (7 if (d == 1 and k == 0) else li), :], ps[kb + 2][0:32, 128:160], t_ps[kb + 2]
                            if d == 1 and k == 0:
                                P.tt(dst_, dst_, src_, ALU.add, rd=[tkp, t_K], wr=[t_K])
                            else:
                                P.copy(dst_, src_, rd=[tkp], wr=[t_K])
            for i in range(4):
                P.dma(Kmat[96:128, i, :, 96:128], Kst[:, i, :, :], rd=[t_K], wr=[t_K])
        if EV.dbg:
            for nm_, t_, tk_ in (("Gam", Gam, t_Gam), ("Psi", Psi, t_Psi), ("Kmat", Kmat, t_K)):
                n_el = 1
                for v_ in t_.shape[1:]:
                    n_el *= v_
                dd_ = nc.dram_tensor("dbg_%s_%d" % (nm_, idx), [128, n_el], BF16, kind="ExternalOutput").ap()
                names_ = "abcdefg"[:len(t_.shape) - 1]
                P.dma(dd_[:, :], t_[:].rearrange("p %s -> p (%s)" % (" ".join(names_), " ".join(names_))), rd=[tk_], wr=[Tok()])
            for nm_, t_ in (("W2", W2), ("W3", W3)):
                dd_ = nc.dram_tensor("dbg_%s_%d" % (nm_, idx), [128, 64], F32, kind="ExternalOutput").ap()
                P.dma(dd_[:, :], t_[:].rearrange("p a b c -> p (a b c)"), rd=[t_W], wr=[Tok()])
        P.barrier()
        if STAGE == 1:
            sa.close()
            return
        kdup = sb([128, 8, 2, 128], BF16, sa)
        uT3 = sb([128, 4, 256], BF16, sa)
        t_kdup = Tok()
        for hk in range(2):
            for cp in range(2):
                P.copy(kdup[:, :, hk, cp * 64:(cp + 1) * 64], w_in[:, :, KO + hk * 64:KO + (hk + 1) * 64], rd=[t_w],
                       wr=[t_kdup], eng='pool')
        kTt = sb([128, 2, 256], BF16, sa)
        t_kT = Tok()
        Vt = sb([128, 2, 65], BF16, sa)
        t_Vt = Tok()
        Gblk = sb([128, 64, 32], F32, sa)
        t_G = Tok()
        t_Gd = [[Tok() for _ in range(NTL)] for _ in range(NS)]
        t_Hd = [[Tok() for _ in range(NTL)] for _ in range(NS)]
        t_kd = [Tok() for _ in range(NS)]
        t_vd = [Tok() for _ in range(NS)]
        P.memset(Vt[:], 1.0, wr=[t_Vt], eng='pool')

        def gslot(s, tl, sub):
            return (2 * (s * NTL + tl) + sub) % 4

        def norm_tile(s, tl, which):
            for sub in range(2):
                n = 2 * tl + sub
                C.load_h(idx, s, n, gslot(s, tl, sub))
                C.norm_T(gslot(s, tl, sub), which, lambda k, sub=sub: aT[:, k, sub * 128:(sub + 1) * 128], t_aT)

        def rope(dst, raw, t_raw, tl, t_dst):
            P.mm(ps[6][:, 0:256], perm[:], raw, start=True, stop=True, rd=[t_raw, t_sm], wr=[t_ps[6]])
            P.tt(r1[:], raw, rcos[:], ALU.mult, rd=[t_raw, t_rope], wr=[t_r])
            P.tt(r2[:], ps[6][:, 0:256], rsin[:], ALU.mult, rd=[t_ps[6], t_rope], wr=[t_r])
            P.tt(dst, r1[:], r2[:], ALU.add, rd=[t_r], wr=[t_dst], eng='pool')

        def load_rope(tl):
            P.dma(rcos[:], EV.rope_cos[:, (tl - 1) * 256:tl * 256], wr=[t_rope])
            P.dma(rsin[:], EV.rope_sin[:, (tl - 1) * 256:tl * 256], wr=[t_rope])

        def proj_fm(m_cols, bank, ncols=256, extra=()):
            for k in range(8):
                P.mm(ps[bank][:, 0:256], m_cols(k), aT[:, k, :], start=(k == 0), stop=(k == 7),
                     rd=[t_aT, t_w] + list(extra), wr=[t_ps[bank]])

        tiles_all = [(s_, tl_) for s_ in range(NS) for tl_ in range(NTL)]

        def norm_next(s_, tl_, force=False):
            if (not PIPE) and not force:
                return
            ii = tiles_all.index((s_, tl_)) + 1
            if ii < len(tiles_all):
                s2, tl2 = tiles_all[ii]
                norm_tile(s2, tl2, 2 if tl2 == 0 else s2)
        norm_tile(0, 0, 2)
        for s in range(NS):
            for tl in range(NTL):
                which = 2 if tl == 0 else s
                if tl > 0:
                    load_rope(tl)
                for i in range(4):
                    b = i % 2
                    proj_fm(lambda k, i=i: w_in[:, k, UO + i * 128:UO + (i + 1) * 128], b)
                    P.copy(uT[:, i, :].rearrange("p (s j) -> p j s", s=8), ps[b][:, 0:256].rearrange("p (j s) -> p j s", s=8),
                           rd=[t_ps[b]], wr=[t_uT])
                if PA != 1:
                    P.ts(uT3[64:128], uT[64:128], hmask3[64:128, 2:3], None, ALU.mult, rd=[t_uT, t_sm], wr=[t_uT])
                for hk in range(2):
                    b = hk
                    proj_fm(lambda k, hk=hk: kdup[:, k, hk, :], b, extra=[t_kdup])
                    if tl == 0:
                        P.copy(kTt[:, hk, :], ps[b][:, 0:256], rd=[t_ps[b]], wr=[t_kT])
                    else:
                        P.copy(kraw[:], ps[b][:, 0:256], rd=[t_ps[b]], wr=[t_kraw])
                        rope(kTt[:, hk, :], kraw[:], t_kraw, tl, t_kT)
                P.dma(EV.kT_d[s, :, :, tl * 256:(tl + 1) * 256], kTt[:], rd=[t_kT], wr=[t_kd[s]], eng='pool')
                for sub in range(2):
                    for k in range(8):
                        P.mm(ps[2][:, 0:128], aT[:, k, sub * 128:(sub + 1) * 128], w_in[:, k, VO:VO + 128], start=(k == 0),
                             stop=(k == 7), rd=[t_aT, t_w], wr=[t_ps[2]])
                    P.copy(Vt[:, :, 0:64], ps[2][:, 0:128].rearrange("p (h e) -> p h e", h=2), rd=[t_ps[2]], wr=[t_Vt])
                    P.dma(EV.V_d[s, 2 * tl + sub], Vt[:].rearrange("p h e -> p (h e)"), rd=[t_Vt], wr=[t_vd[s]], eng='pool')
                norm_next(s, tl)
                def g_mm(q, d, ri, i, s8):
                    rows = slice(32 * q, 32 * q + 32) if q < 3 else slice(64, 128)
                    usrc = uT if q < 3 else uT3
                    col = ((d * 2 + ri) * 4 + i) * 32
                    if d == 0:
                        rhs = usrc[rows, i, s8 * 32:(s8 + 1) * 32]
                    else:
                        rhs = usrc[rows, i, s8 * 32 + 31:(s8 * 32 - 1 if s8 > 0 else None):-1]
                    P.mm(ps[3 + q][:, col:col + 32], Gam[rows, d, i, s8, ri, :], rhs, start=(s8 == 0),
                         stop=(s8 == 7), rd=[t_Gam, t_uT], wr=[t_ps[3 + q]])
                for qs_ in ((0, 1, 2), (3,)):
                    for d in range(2):
                        for ri in range(2):
                            for i in range(4):
                                for s8 in range(8):
                                    for q in qs_:
                                        g_mm(q, d, ri, i, s8)
                    for q in qs_:
                        P.copy(Gblk[:, q::4, :], ps[3 + q][:, :].rearrange("p (r k) -> p r k", r=16),
                               rd=[t_ps[3 + q]], wr=[t_G])
                P.dma(EV.G_d[s, tl], Gblk[:].rearrange("p a k -> p (a k)"), rd=[t_G], wr=[t_Gd[s][tl]], eng='pool')
                if not PIPE:
                    norm_next(s, tl, force=True)
        P.barrier()
        sa.close()
        if STAGE == 2:
            return
        ss = ExitStack()
        _sb0 = sb
        sb = lambda shape, dtype: _sb0(shape, dtype, ss)
        NCH = NS * 2
        Gs = [sb([128, NCH, 2, 16, 32], F32) for _ in range(2)]
        t_Gs = [Tok(), Tok()]
        Hx = sb([128, 33, NCH, 2, 16], F32)
        t_Hx = Tok()
        Hb = sb([128, NCH, 2, 16, 32], BF16)
        t_Hb = Tok()
        W2c = sb([128, NCH, 2, 16], F32)
        W3c = sb([128, NCH, 2, 16], F32)
        sc1 = sb([128, NCH, 2, 16], F32)
        sc2 = sb([128, NCH, 2, 16], F32)
        sc3 = sb([128, NCH, 2, 16], F32)
        t_sa, t_sb, t_sc = Tok(), Tok(), Tok()
        for s in range(NS):
            for d in range(2):
                P.copy(W2c[:, s * 2 + d], W2[:, d], rd=[t_W], wr=[t_W], eng='pool')
                P.copy(W3c[:, s * 2 + d], W3[:, d], rd=[t_W], wr=[t_W], eng='pool')
        order = ([i for i in range(NTL)], [0] + list(range(NTL - 1, 0, -1)))
        P.memset(Hx[:, 0], 0.0, wr=[t_Hx])

        def load_g(step):
            g, tg = Gs[step % 2], t_Gs[step % 2]
            for s in range(NS):
                for d in range(2):
                    tl_ = order[d][step]
                    P.dma(g[:, s * 2 + d].rearrange("p a r k -> p (a r k)"), EV.G_d[s, tl_, :, d * 1024:(d + 1) * 1024],
                          rd=[t_Gd[s][tl_]], wr=[tg])
        load_g(0)
        for step in range(NTL):
            if step + 1 < NTL:
                load_g(step + 1)
            g, tg = Gs[step % 2], t_Gs[step % 2]
            for k in range(32):
                cur = Hx[:, k]
                P.tt(sc1[:], W2c[:], cur, ALU.mult, rd=[t_Hx, t_W], wr=[t_sa])
                P.tt(sc2[:], W3c[:], cur[:, :, ::-1, :], ALU.mult, rd=[t_Hx, t_W], wr=[t_sb])
                P.tt(sc3[:], sc1[:], g[:, :, :, :, k], ALU.add, rd=[t_sa, tg], wr=[t_sc])
                P.tt(Hx[:, k + 1], sc2[:], sc3[:], ALU.add, rd=[t_sb, t_sc], wr=[t_Hx])
            for s in range(NS):
                for d in range(2):
                    c = s * 2 + d
                    src = Hx[:, 0:32, c].rearrange("p k a r -> p a r k")
                    if d == 0:
                        P.copy(Hb[:, c], src, rd=[t_Hx], wr=[t_Hb], eng='pool')
                    else:
                        P.copy(Hb[:, c, :, :, ::-1], src, rd=[t_Hx], wr=[t_Hb], eng='pool')
                    tl_ = order[d][step]
                    P.dma(EV.H_d[s, tl_, :, d * 1024:(d + 1) * 1024], Hb[:, c].rearrange("p a r k -> p (a r k)"),
                          rd=[t_Hb], wr=[t_Hd[s][tl_]], eng='sp')
            P.copy(Hx[:, 0], Hx[:, 32], rd=[t_Hx], wr=[t_Hx])
        P.barrier()
        ss.close()
        sb = _sb0
        if STAGE == 3:
            return
        w_out = sb([128, 8, 1024], BF16)
        gluw = sb([128, 4, 512], BF16)
        qraw = sb([128, 256], BF16)
        t_qraw = Tok()
        qT = sb([128, 4, 256], BF16)
        t_qT = Tok()
        sgA = sb([128, 4, 256], BF16)
        sgS = sb([128, 4, 256], BF16)
        t_sg = Tok()
        Hin = sb([128, 2, 2, 16, 32], BF16)
        t_Hin = Tok()
        Zs = [sb([32, 8, 4, 32], BF16) for _ in range(2)]
        t_Zs = [Tok(), Tok()]
        yT = sb([128, 4, 256], F32)
        t_yT = Tok()
        zT = sb([128, 4, 256], BF16)
        t_zT = Tok()
        sig = sb([128, 256], BF16)
        t_sig = Tok()
        mixT = sb([128, 8, 256], BF16)
        t_mix = Tok()
        kctx = sb([128, 2, 256], BF16)
        kwin = sb([128, 2, 512], BF16)
        t_kw = Tok()
        vctx = sb([128, 2, 130], BF16)
        vwin = sb([128, 4, 130], BF16)
        t_vw = Tok()
        Pt = [sb([128, 5, 128], BF16) for _ in range(2)]
        t_Pt = [Tok(), Tok()]
        o_tm = sb([128, 512], BF16)
        t_otm = Tok()
        rden = sb([128, 8], F32)
        t_rden = Tok()
        tmp = sb([128, 1024], F32)
        t_tmp = Tok()
        g1 = tmp[:].rearrange("p (a b) -> p a b", a=4)
        t_g = t_tmp
        load_weight_bf16(C, w_out, EV.w_out[j], 8, 1024, t_w, [tmp], [t_tmp])
        load_weight_bf16(C, gluw, EV.glu_w[j], 4, 512, t_w, [tmp], [t_tmp])
        hn = [sb([128, 1024], F32) for _ in range(2)]
        t_hn = [Tok(), Tok()]
        hcnt = [0]
        for s in range(NS):
            P.dma(kctx[:], EV.kT_d[s, :, :, 0:256], rd=[t_kd[s]], wr=[t_kw])
            P.dma(vctx[:].rearrange("p a e -> p a e"), EV.V_d[s, 0:2].rearrange("a p e -> p a e"), rd=[t_vd[s]], wr=[t_vw])
            if s == 0:
                norm_tile(0, 0, 2)
            for tl in range(NTL):
                which = 2 if tl == 0 else s
                if tl > 0:
                    load_rope(tl)
                    lo = max(256, tl * 256 - 128)
                    hi = min(TOK, tl * 256 + 384)
                    off = lo - (tl * 256 - 128)
                    P.dma(kwin[:, :, off:off + (hi - lo)], EV.kT_d[s, :, :, lo:hi], rd=[t_kd[s]], wr=[t_kw])
                    n0 = 2 * tl - 1
                    for a in range(4):
                        n = n0 + a
                        if 2 <= n < NSUB:
                            P.dma(vwin[:, a, :], EV.V_d[s, n], rd=[t_vd[s]], wr=[t_vw])
                P.dma(Hin[:].rearrange("p d a r k -> p (d a r k)"), EV.H_d[s, tl], rd=[t_Hd[s][tl]], wr=[t_Hin])
                for m in range(4):
                    b = m % 2
                    proj_fm(lambda k, m=m: w_in[:, k, QO + m * 128:QO + (m + 1) * 128], b)
                    if tl == 0:
                        P.copy(qT[:, m, :], ps[b][:, 0:256], rd=[t_ps[b]], wr=[t_qT])
                    else:
                        P.copy(qraw[:], ps[b][:, 0:256], rd=[t_ps[b]], wr=[t_qraw])
                        rope(qT[:, m, :], qraw[:], t_qraw, tl, t_qT)
                for m in range(4):
                    b = m % 2
                    proj_fm(lambda k, m=m: w_in[:, k, GAO + m * 128:GAO + (m + 1) * 128], b)
                    P.act(sgA[:, m, :], ps[b][:, 0:256], AF.Silu, rd=[t_ps[b]], wr=[t_sg])
                for m in range(4):
                    b = m % 2
                    proj_fm(lambda k, m=m: w_in[:, k, GSO + m * 128:GSO + (m + 1) * 128], b)
                    P.act(sgS[:, m, :], ps[b][:, 0:256], AF.Silu, rd=[t_ps[b]], wr=[t_sg])
                for i in range(4):
                    b = i % 2
                    proj_fm(lambda k, i=i: w_in[:, k, UO + i * 128:UO + (i + 1) * 128], b)
                    P.copy(uT[:, i, :].rearrange("p (s j) -> p j s", s=8), ps[b][:, 0:256].rearrange("p (j s) -> p j s", s=8),
                           rd=[t_ps[b]], wr=[t_uT])
                norm_next(s, tl)
                def z_part(i):
                    zb = (0, 1) if i % 2 == 0 else (4, 5)
                    for q in range(4):
                        pr = 4 * i + q
                        cq = 32 * q if q < 3 else 128
                        bk = zb[q // 2]
                        cnt = 0
                        for d in range(2):
                            for ri in range(2):
                                cnt += 1
                                P.mm(ps[bk][0:32, (q % 2) * 256:(q % 2) * 256 + 256], Hin[:, d, ri, pr, :],
                                     Psi[:, d, ri, pr].rearrange("p t c -> p (t c)"), start=(cnt == 1), stop=(cnt == 4),
                                     rd=[t_Psi, t_Hin], wr=[t_ps[bk]])
                    for hb_ in range(2):
                        P.act(Zs[i % 2][:, :, 2 * hb_:2 * hb_ + 2, :],
                              ps[zb[hb_]][0:32, :].rearrange("p (q t c) -> p t q c", q=2, t=8), AF.Identity,
                              rd=[t_ps[zb[hb_]]], wr=[t_Zs[i % 2]])
                z_part(0)
                for i in range(4):
                    bank = 2 + i % 2
                    if i + 1 < 4:
                        z_part(i + 1)
                    for li_, tau in enumerate([0] + [v for v in range(-7, 8) if v != 0]):
                        t0, t1 = max(0, tau), min(7, 7 + tau)
                        P.mm(ps[bank][:, t0 * 32:(t1 + 1) * 32], Kmat[:, i, 7 + tau, :],
                             uT[:, i, (t0 - tau) * 32:(t1 - tau + 1) * 32], start=(li_ == 0), stop=False,
                             rd=[t_K, t_uT], wr=[t_ps[bank]], sgc=True)
                    for t in range(8):
                        P.mm(ps[bank][:, t * 32:(t + 1) * 32], Zs[i % 2][:, t].rearrange("p q c -> p (q c)"), C.ident[0:32, 0:32],
                             start=False, stop=(t == 7), rd=[t_Zs[i % 2], C.t_ident], wr=[t_ps[bank]], sgc=True)
                    P.stt(yT[:, i, :].rearrange("p (k t) -> p t k", t=8), uT[:, i, :].rearrange("p (t k) -> p t k", t=8),
                          dvec[:, i:i + 1], ps[bank][:, 0:256].rearrange("p (t k) -> p t k", t=8), ALU.mult, ALU.add,
                          rd=[t_uT, t_sm, t_ps[bank]], wr=[t_yT])
                P.tt(g1, yT[:], yT[:], ALU.mult, rd=[t_yT], wr=[t_g])
                P.ts(g1, g1, 0.044715, 1.0, ALU.mult, ALU.add, rd=[t_g], wr=[t_g])
                P.tt(g1, g1, yT[:], ALU.mult, rd=[t_g, t_yT], wr=[t_g])
                P.act(g1, g1, AF.Sigmoid, rd=[t_g], wr=[t_g], scale=1.5957691216057308)
                P.tt(zT[:], g1, yT[:], ALU.mult, rd=[t_g, t_yT], wr=[t_zT])
                def key_blocks(qb):
                    n = 2 * tl + qb
                    kbs = []
                    if tl > 0:
                        for a in range(3):
                            nn = n - 1 + a
                            if 2 <= nn < NSUB:
                                wa = qb + a
                                kbs.append(('w', wa, [0, None, 1][a]))
                    kbs.append(('c', 0, None))
                    kbs.append(('c', 1, None))
                    return kbs

                def att_scores(qb, h):
                    kbs = key_blocks(qb)
                    nb = len(kbs)
                    qs = slice(qb * 128, (qb + 1) * 128)
                    hk, m, half = h // 4, h // 2, h % 2
                    hr = slice(64 * half, 64 * half + 64)
                    pt, tpt = Pt[h % 2], t_Pt[h % 2]
                    for bi in range(nb):
                        kind, wa, mk = kbs[bi]
                        ksrc = kwin[hr, hk, wa * 128:(wa + 1) * 128] if kind == 'w' else kctx[hr, hk, wa * 128:(wa + 1) * 128]
                        bnk = (4 + half) if bi < 4 else (6, 0)[half]
                        col = (bi % 4) * 128
                        P.mm(ps[bnk][:, col:col + 128], ksrc, qT[hr, m, qs], start=True, stop=True, rd=[t_kw, t_qT],
                             wr=[t_ps[bnk]])
                    n4 = min(nb, 4)
                    P.act(pt[:, 0:n4, :], ps[4 + half][:, 0:n4 * 128].rearrange("p (b q) -> p b q", b=n4), AF.Exp,
                          rd=[t_ps[4 + half]], wr=[tpt], scale=0.125)
                    if nb > 4:
                        b5 = (6, 0)[half]
                        P.act(pt[:, 4, :], ps[b5][:, 0:128], AF.Exp, rd=[t_ps[b5]], wr=[tpt], scale=0.125)
                    for bi, (kind, wa, mk) in enumerate(kbs):
                        if mk is not None:
                            P.tt(pt[:, bi, :], pt[:, bi, :], masks[:, mk, :], ALU.mult, rd=[tpt, t_sm], wr=[tpt])

                def att_pv(qb, h):
                    kbs = key_blocks(qb)
                    nb = len(kbs)
                    hk = h // 4
                    pt, tpt = Pt[h % 2], t_Pt[h % 2]
                    ob = 2 + h // 4
                    oc = (h % 4) * 65
                    for bi, (kind, wa, mk) in enumerate(kbs):
                        vsrc = vwin[:, wa, hk * 65:(hk + 1) * 65] if kind == 'w' else vctx[:, wa, hk * 65:(hk + 1) * 65]
                        P.mm(ps[ob][:, oc:oc + 65], pt[:, bi, :], vsrc, start=(bi == 0), stop=(bi == nb - 1),
                             rd=[tpt, t_vw], wr=[t_ps[ob]])

                def att_norm(qb):
                    for hb in range(2):
                        ob = 2 + hb
                        o3 = ps[ob][:, 0:260].rearrange("p (h e) -> p h e", h=4)
                        P.tt(rden[:, hb * 4:(hb + 1) * 4], o3[:, :, 64], esink[:, hb * 4:(hb + 1) * 4], ALU.add,
                             rd=[t_ps[ob], t_sm], wr=[t_rden])
                        P.recip(rden[:, hb * 4:(hb + 1) * 4], rden[:, hb * 4:(hb + 1) * 4], rd=[t_rden], wr=[t_rden])
                        P.tt(o_tm[:, hb * 256:(hb + 1) * 256].rearrange("p (h e) -> p h e", h=4), o3[:, :, 0:64],
                             rden[:, hb * 4:(hb + 1) * 4].unsqueeze(2).broadcast_to([128, 4, 64]), ALU.mult,
                             rd=[t_ps[ob], t_rden], wr=[t_otm])

                def att_finish(qb):
                    qs = slice(qb * 128, (qb + 1) * 128)
                    for m in range(4):
                        P.tr(C.pT[:, m * 128:(m + 1) * 128], o_tm[:, m * 128:(m + 1) * 128], C.ident[:], rd=[t_otm, C.t_ident],
                             wr=[C.t_pT])
                    for m in range(4):
                        P.tt(mixT[:, m, qs], C.pT[:, m * 128:(m + 1) * 128], sgA[:, m, qs], ALU.mult, rd=[C.t_pT, t_sg],
                             wr=[t_mix])

                def glu_all():
                    for m in range(4):
                        b = m % 2
                        for k in range(4):
                            P.mm(ps[b][:, 0:256], gluw[:, k, m * 128:(m + 1) * 128], zT[:, k, :], start=(k == 0), stop=(k == 3),
                                 rd=[t_zT, t_w], wr=[t_ps[b]])
                        P.act(sig[:], ps[b][:, 0:256], AF.Sigmoid, rd=[t_ps[b], t_sm], wr=[t_sig], bias=glub[:, m:m + 1])
                        P.tt(sig[:], sig[:], zT[:, m, :], ALU.mult, rd=[t_sig, t_zT], wr=[t_sig])
                        P.tt(mixT[:, 4 + m, :], sig[:], sgS[:, m, :], ALU.mult, rd=[t_sig, t_sg], wr=[t_mix])

                att_scores(0, 0)
                for h in range(8):
                    if h + 1 < 8:
                        att_scores(0, h + 1)
                    att_pv(0, h)
                att_norm(0)
                att_scores(1, 0)
                att_finish(0)
                for h in range(8):
                    if h + 1 < 8:
                        att_scores(1, h + 1)
                    att_pv(1, h)
                att_norm(1)
                glu_all()
                att_finish(1)
                for sub in range(2):
                    n = 2 * tl + sub
                    slot = gslot(s, tl, sub)
                    hi_ = hcnt[0] % 2
                    hcnt[0] += 1
                    for nb2 in range(2):
                        for k in range(8):
                            P.mm(ps[nb2][:, :], mixT[:, k, sub * 128:(sub + 1) * 128], w_out[:, k, nb2 * 512:(nb2 + 1) * 512],
                                 start=(k == 0), stop=(k == 7), rd=[t_mix, t_w], wr=[t_ps[nb2]])
                        sl = slice(nb2 * 512, (nb2 + 1) * 512)
                        P.tt(tmp[:, sl], ps[nb2][:, :], C.gate_bc[:, which, sl], ALU.mult, rd=[t_ps[nb2], C.t_mod], wr=[t_tmp])
                        P.tt(hn[hi_][:, sl], tmp[:, sl], C.hT[slot][:, sl], ALU.add, rd=[t_tmp, C.t_hT[slot]],
                             wr=[t_hn[hi_]], eng='pool')
                    C.store_h(idx, s, n, hn[hi_], t_hn[hi_], last)
                if not PIPE:
                    norm_next(s, tl, force=True)
        P.barrier()
```

```python
import numpy as np
from contextlib import ExitStack
import concourse.bass as bass
import concourse.mybir as mybir
from concourse.bass_utils import run_bass_kernel_spmd

F32 = mybir.dt.float32
BF16 = mybir.dt.bfloat16
AF = mybir.ActivationFunctionType
ALU = mybir.AluOpType

ENGS = ['pe', 'act', 'dve', 'pool', 'sp']
import os
SKIPK = int(os.environ.get('SKIPK', '0'))
PA = int(os.environ.get('PA', '0'))
PIPE = int(os.environ.get('PIPE', '1'))
STAGE = 0


class Tok:
    __slots__ = ('lw', 'rd')

    def __init__(self):
        self.lw = None
        self.rd = {}


class Prog:
    def __init__(self, nc, stack, ndq=12):
        self.nc = nc
        self.q = {e: [] for e in ENGS}
        self.cnt = {e: 0 for e in ['pe', 'act', 'dve', 'pool']}
        self.sems = {}
        self.esem = {}
        for e in ['pe', 'act', 'dve', 'pool']:
            self.esem[e] = 'c_' + e
            self.sems['c_' + e] = stack.enter_context(nc.semaphore('c_' + e))
        self.dq = {}
        for e in ['sp', 'pool', 'act']:
            names = []
            for i in range(ndq):
                n = 'd_%s%d' % (e, i)
                self.sems[n] = stack.enter_context(nc.semaphore(n))
                names.append(n)
            self.dq[e] = names
        self.dcnt = {'sp': 0, 'pool': 0, 'act': 0}
        self.waited = {e: {} for e in ENGS}
        self.pending = {e: {} for e in ENGS}

    def emit(self, eng, fn, rd=(), wr=(), dma=False):
        if dma:
            i = self.dcnt[eng]
            K = len(self.dq[eng])
            semname = self.dq[eng][i % K]
            val = 16 * (i // K + 1)
            self.dcnt[eng] += 1
            prev = (semname, val - 16) if val > 16 else None
            inc = 16
        else:
            self.cnt[eng] += 1
            semname = self.esem[eng]
            val = self.cnt[eng]
            prev = None
            inc = 1
        waits = {}

        def need(p):
            if p is None:
                return
            s, v = p
            if (not dma) and eng == 'pe' and s == 'c_pe':
                return
            if waits.get(s, 0) < v:
                waits[s] = v

        for s_, v_ in self.pending[eng].items():
            need((s_, v_))
        self.pending[eng] = {}
        need(prev)
        for t in rd:
            need(t.lw)
        for t in wr:
            need(t.lw)
            for s, v in t.rd.items():
                need((s, v))
        cache = self.waited[eng]
        w = []
        for s, v in waits.items():
            if cache.get(s, 0) < v:
                cache[s] = v
                w.append((s, v))
        self.q[eng].append((w, fn, semname, inc))
        for t in rd:
            if t.rd.get(semname, 0) < val:
                t.rd[semname] = val
        for t in wr:
            t.lw = (semname, val)
            t.rd = {}

    def barrier(self):
        cur = {}
        for e in ['pe', 'act', 'dve', 'pool']:
            if self.cnt[e] > 0:
                cur[self.esem[e]] = self.cnt[e]
        for e in ['sp', 'pool', 'act']:
            K = len(self.dq[e])
            for i in range(max(0, self.dcnt[e] - K), self.dcnt[e]):
                cur[self.dq[e][i % K]] = 16 * (i // K + 1)
        for e in ENGS:
            for s_, v_ in cur.items():
                if self.pending[e].get(s_, 0) < v_:
                    self.pending[e][s_] = v_

    def dma(self, out, in_, rd=(), wr=(), eng='sp', slow=False):
        if slow:
            self.emit(eng, lambda e: e.dma_start(out=out, in_=in_, allow_slow_non_contiguous=True), rd, wr, dma=True)
        else:
            self.emit(eng, lambda e: e.dma_start(out=out, in_=in_), rd, wr, dma=True)

    def mm(self, out, lhsT, rhs, start, stop, rd=(), wr=(), sgc=False):
        self.emit('pe', lambda e: e.matmul(out, lhsT, rhs, start=start, stop=stop, skip_group_check=sgc), rd, wr)

    def tr(self, out, in_, ident, rd=(), wr=()):
        self.emit('pe', lambda e: e.transpose(out, in_, ident), rd, wr)

    def act(self, out, in_, func, rd=(), wr=(), bias=None, scale=None, accum_out=None):
        kw = {}
        if bias is not None:
            kw['bias'] = bias
        if scale is not None:
            kw['scale'] = scale
        if accum_out is not None:
            kw['accum_out'] = accum_out
        self.emit('act', lambda e: e.activation(out, in_, func, **kw), rd, wr)

    def tt(self, out, in0, in1, op, rd=(), wr=(), eng='dve'):
        self.emit(eng, lambda e: e.tensor_tensor(out, in0, in1, op), rd, wr)

    def ts(self, out, in0, s1, s2, op0, op1=None, rd=(), wr=(), eng='dve'):
        if op1 is None:
            self.emit(eng, lambda e: e.tensor_scalar(out, in0, s1, None, op0), rd, wr)
        else:
            self.emit(eng, lambda e: e.tensor_scalar(out, in0, s1, s2, op0, op1), rd, wr)

    def stt(self, out, in0, scalar, in1, op0, op1, rd=(), wr=(), eng='dve'):
        self.emit(eng, lambda e: e.scalar_tensor_tensor(out, in0, scalar, in1, op0, op1), rd, wr)

    def copy(self, out, in_, rd=(), wr=(), eng='dve'):
        self.emit(eng, lambda e: e.tensor_copy(out, in_), rd, wr)

    def memset(self, ap, c, wr=(), eng='dve'):
        self.emit(eng, lambda e: e.memset(ap, c), (), wr)

    def recip(self, out, in_, rd=(), wr=()):
        self.emit('dve', lambda e: e.reciprocal(out, in_), rd, wr)

    def finish(self):
        nc = self.nc
        sems = self.sems
        with nc.Block() as b0:
            @b0.sync
            def _(e):
                for n, s in sems.items():
                    e.sem_clear(s)
        finals = {}
        for eng in ['sp', 'pool', 'act']:
            K = len(self.dq[eng])
            for i in range(self.dcnt[eng]):
                finals[self.dq[eng][i % K]] = 16 * (i // K + 1)
        with nc.Block() as block:
            def mk(engname):
                def body(e):
                    for (w, fn, semname, inc) in self.q[engname]:
                        for s, v in w:
                            e.wait_ge(sems[s], v)
                        ins = fn(e)
                        ins.then_inc(sems[semname], inc)
                    if engname == 'sp':
                        for s, v in finals.items():
                            e.wait_ge(sems[s], v)
                        for en in ['pe', 'act', 'dve', 'pool']:
                            if self.cnt[en] > 0:
                                e.wait_ge(sems[self.esem[en]], self.cnt[en])
                return body
            block.tensor(mk('pe'))
            block.scalar(mk('act'))
            block.vector(mk('dve'))
            block.gpsimd(mk('pool'))
            block.sync(mk('sp'))


POOL_R = [1, 2, 4, 8]


def _band_consts():
    B = np.zeros((4, 6, 128, 128), np.float32)
    tin = np.arange(128)[:, None]
    tout = np.arange(128)[None, :]
    for ri, r in enumerate(POOL_R):
        inband = (np.abs(tin - tout) <= r).astype(np.float32)
        full = 1.0 / (2 * r + 1)
        cnt_first = np.minimum(tout + r + 1, 2 * r + 1).astype(np.float32)
        cnt_last = np.minimum((127 - tout) + r + 1, 2 * r + 1).astype(np.float32)
        eye = np.eye(128, dtype=np.float32)
        B[ri, 0] = inband / cnt_first - eye
        B[ri, 1] = inband * full - eye
        B[ri, 2] = inband / cnt_last - eye
        B[ri, 3] = (np.abs(tin - 128 - tout) <= r).astype(np.float32) * full
        B[ri, 4] = (np.abs(tin + 128 - tout) <= r).astype(np.float32) * full
        cnt_both = (np.minimum(tout + r, r) + np.minimum(127 - tout, r) + 1).astype(np.float32)
        B[ri, 5] = inband / cnt_both - eye
    return B


class Ctx:
    pass


def build_program(NS, NT, layers, do_final, dbg=False):
    nc = bass.Bass("TRN2", target_bir_lowering=False)
    LAT = NT * 256
    TOK = 256 + LAT
    dt_in = lambda n, s: nc.dram_tensor(n, s, F32, kind="ExternalInput").ap()
    x_in = dt_in("x", [NS, LAT, 1024])
    ctx_in = dt_in("ctx", [NS, 256, 1024])
    c3_in = dt_in("c3", [3, 1024])
    ada_w = dt_in("ada_w", [4, 1024, 3072])
    ada_b = dt_in("ada_b", [4, 3072])
    norm_g = dt_in("norm_g", [4, 1024])
    odd_w_in = dt_in("odd_w_in", [2, 1024, 2048])
    odd_w_out = dt_in("odd_w_out", [2, 1024, 1024])
    pool_w = dt_in("pool_w", [2, 4, 256, 256])
    pool_scale = dt_in("pool_scale", [2, 1024])
    final_g = dt_in("final_g", [1024])
    IK = "ExternalOutput" if dbg else "Internal"
    EV = Ctx()
    EV.dbg = dbg
    EV.w_in = dt_in("even_w_in", [2, 1024, 2304])
    EV.w_out = dt_in("even_w_out", [2, 1024, 1024])
    EV.sink = dt_in("attn_sink", [2, 8])
    EV.a_re = dt_in("ssm_a_re", [2, 2, 32, 64])
    EV.a_im = dt_in("ssm_a_im", [2, 2, 32, 64])
    EV.log_dt = dt_in("ssm_log_dt", [2, 2, 32])
    EV.b_re = dt_in("ssm_b_re", [2, 2, 32, 64, 16])
    EV.b_im = dt_in("ssm_b_im", [2, 2, 32, 64, 16])
    EV.c_re = dt_in("ssm_c_re", [2, 2, 32, 16, 64])
    EV.c_im = dt_in("ssm_c_im", [2, 2, 32, 16, 64])
    EV.d = dt_in("ssm_d", [2, 512])
    EV.glu_w = dt_in("glu_w", [2, 512, 512])
    EV.glu_b = dt_in("glu_b", [2, 512])
    EV.rope_cos = dt_in("rope_cos", [128, NT * 256])
    EV.rope_sin = dt_in("rope_sin", [128, NT * 256])
    EV.perm = dt_in("perm", [128, 128])
    EV.masks = dt_in("masks", [2, 128, 128])
    EV.hmask = dt_in("hmask", [128, 3])
    NSUB_ = 2 * (NT + 1)
    EV.kT_d = nc.dram_tensor("kT_d", [NS, 128, 2, 256 + NT * 256], BF16, kind=IK).ap()
    EV.V_d = nc.dram_tensor("V_d", [NS, NSUB_, 128, 130], BF16, kind=IK).ap()
    EV.G_d = nc.dram_tensor("G_d", [NS, NT + 1, 128, 2048], F32, kind=IK).ap()
    EV.H_d = nc.dram_tensor("H_d", [NS, NT + 1, 128, 2048], BF16, kind=IK).ap()
    ident_in = dt_in("ident", [128, 128])
    bands_in = dt_in("bands", [4, 6, 128, 128])
    out = nc.dram_tensor("out", [NS, LAT, 1024], F32, kind="ExternalOutput").ap()
    hbuf = [nc.dram_tensor("hbuf%d" % i, [NS, TOK, 1024], F32, kind=("ExternalOutput" if dbg else "Internal")).ap()
            for i in range(2)]
    modrow = nc.dram_tensor("modrow", [4, 3, 3072], F32, kind="Internal").ap()

    with ExitStack() as st:
        P = Prog(nc, st)
        C = Ctx()
        C.nc, C.P, C.st = nc, P, st
        C.NS, C.NT, C.TOK, C.LAT = NS, NT, TOK, LAT
        sbc = [0]

        def sb(shape, dtype, name=None):
            sbc[0] += 1
            return st.enter_context(nc.sbuf_tensor(name or ("sb%d" % sbc[0]), shape, dtype))
        C.sb = sb
        C.ps = [st.enter_context(nc.psum_tensor("ps%d" % i, [128, 512], F32)) for i in range(7)]
        C.t_ps = [Tok() for _ in range(7)]
        t_modrow = Tok()
        t_h = [[[Tok() for _ in range(2 * (NT + 1))] for _ in range(NS)] for _ in range(2)]
        t_out = Tok()

        ident_f = sb([128, 128], F32)
        ident = sb([128, 128], BF16)
        t_ident = Tok()
        P.dma(ident_f[:], ident_in[:, :], wr=[t_ident])
        P.copy(ident[:], ident_f[:], rd=[t_ident], wr=[t_ident])
        C.ident, C.t_ident = ident, t_ident
        C.ident_f = ident_f
        fg_bc = sb([128, 1024], F32)
        t_fg = Tok()
        P.dma(fg_bc[:], final_g.partition_broadcast(128), wr=[t_fg])

        sT = sb([128, 8, 3], F32)
        t_sT = Tok()
        for r in range(3):
            P.dma(sT[:, :, r], c3_in[r, :].rearrange("(k p) -> p k", p=128), wr=[t_sT], slow=True)
        P.act(sT[:], sT[:], AF.Silu, rd=[t_sT], wr=[t_sT])
        ones1 = sb([1, 4], F32)
        t_ones1 = Tok()
        P.memset(ones1[:], 1.0, wr=[t_ones1])
        with ExitStack() as st2:
            wt = [st2.enter_context(nc.sbuf_tensor("adaw%d" % i, [128, 3072], F32)) for i in range(2)]
            t_wt = [Tok(), Tok()]
            brow = st2.enter_context(nc.sbuf_tensor("adab", [1, 3072], F32))
            t_brow = Tok()
            mrow = st2.enter_context(nc.sbuf_tensor("mrow", [3, 3072], F32))
            t_mrow = Tok()
            for L in layers:
                P.dma(brow[:], ada_b[L:L + 1, :], wr=[t_brow])
                for k in range(8):
                    P.dma(wt[k % 2][:], ada_w[L, k * 128:(k + 1) * 128, :], wr=[t_wt[k % 2]])
                    for n in range(6):
                        P.mm(C.ps[n][0:3, :], sT[:, k, :], wt[k % 2][:, n * 512:(n + 1) * 512], start=(k == 0),
                             stop=False, rd=[t_sT, t_wt[k % 2]], wr=[C.t_ps[n]])
                for n in range(6):
                    P.mm(C.ps[n][0:3, :], ones1[0:1, 0:3], brow[0:1, n * 512:(n + 1) * 512], start=False, stop=True,
                         rd=[t_ones1, t_brow], wr=[C.t_ps[n]])
                    P.copy(mrow[:, n * 512:(n + 1) * 512], C.ps[n][0:3, :], rd=[C.t_ps[n]], wr=[t_mrow])
                P.dma(modrow[L], mrow[:], rd=[t_mrow], wr=[t_modrow])

        gT = sb([128, 8], F32)
        scT = sb([128, 3, 8], F32)
        gsT = sb([128, 3, 8], F32)
        shT = sb([128, 3, 8], F32)
        gate_bc = sb([128, 3, 1024], F32)
        t_mod = Tok()
        C.gsT, C.shT, C.gate_bc, C.t_mod = gsT, shT, gate_bc, t_mod

        def load_mod(L):
            P.dma(gT[:], norm_g[L, :].rearrange("(k p) -> p k", p=128), wr=[t_mod], slow=True)
            for w in range(3):
                P.dma(shT[:, w, :], modrow[L, w, 0:1024].rearrange("(k p) -> p k", p=128), rd=[t_modrow],
                      wr=[t_mod], slow=True)
                P.dma(scT[:, w, :], modrow[L, w, 1024:2048].rearrange("(k p) -> p k", p=128), rd=[t_modrow],
                      wr=[t_mod], slow=True)
                P.dma(gate_bc[:, w, :], modrow[L, w, 2048:3072].partition_broadcast(128), rd=[t_modrow], wr=[t_mod])
                P.stt(gsT[:, w, :], scT[:, w, :], 1.0, gT[:], ALU.add, ALU.mult, rd=[t_mod], wr=[t_mod])

        def h_src(li, s, n):
            if li == 0:
                if n < 2:
                    return ctx_in[s, n * 128:(n + 1) * 128, :], None
                return x_in[s, (n - 2) * 128:(n - 1) * 128, :], None
            b = (li - 1) % 2
            return hbuf[b][s, n * 128:(n + 1) * 128, :], t_h[b][s][n]

        def h_dst(li, s, n):
            b = li % 2
            return hbuf[b][s, n * 128:(n + 1) * 128, :], t_h[b][s][n]
        C.h_src, C.h_dst = h_src, h_dst

        NH = 4
        C.hT = [sb([128, 1024], F32) for _ in range(NH)]
        C.t_hT = [Tok() for _ in range(NH)]
        C.stat = [sb([128, 4], F32) for _ in range(2)]
        C.t_stat = [Tok(), Tok()]
        C.xn = [sb([128, 1024], BF16) for _ in range(2)]
        C.t_xn = [Tok(), Tok()]
        C.pT = st.enter_context(nc.psum_tensor("psT", [128, 1024], BF16))
        C.t_pT = Tok()
        C.eps = sb([128, 1], F32)
        C.t_eps = Tok()
        P.memset(C.eps[:], 1e-6, wr=[C.t_eps])
        C.nrm_i = 0

        def load_h(li, s, n, slot):
            src, tk = h_src(li, s, n)
            P.dma(C.hT[slot][:], src, rd=([tk] if tk else []), wr=[C.t_hT[slot]])
        C.load_h = load_h

        def norm_T(slot, which, aT_ap_fn, t_aT):
            i = C.nrm_i % 2
            C.nrm_i += 1
            h = C.hT[slot]
            stt_ = C.stat[i]
            P.act(C.xn[i][:], h[:], AF.Square, rd=[C.t_hT[slot]], wr=[C.t_xn[i], C.t_stat[i]], accum_out=stt_[:, 0:1])
            P.act(stt_[:, 1:2], stt_[:, 0:1], AF.Sqrt, rd=[C.t_stat[i], C.t_eps], wr=[C.t_stat[i]],
                  bias=C.eps[:, 0:1], scale=1.0 / 1024)
            P.recip(stt_[:, 2:3], stt_[:, 1:2], rd=[C.t_stat[i]], wr=[C.t_stat[i]])
            P.ts(C.xn[i][:], h[:], stt_[:, 2:3], None, ALU.mult, rd=[C.t_hT[slot], C.t_stat[i]], wr=[C.t_xn[i]])
            for k in range(8):
                P.tr(C.pT[:, k * 128:(k + 1) * 128], C.xn[i][:, k * 128:(k + 1) * 128], C.ident[:],
                     rd=[C.t_xn[i], C.t_ident], wr=[C.t_pT])
            for k in range(8):
                P.act(aT_ap_fn(k), C.pT[:, k * 128:(k + 1) * 128], AF.Identity, rd=[C.t_pT, C.t_mod], wr=[t_aT],
                      bias=C.shT[:, which, k:k + 1], scale=C.gsT[:, which, k:k + 1])
        C.norm_T = norm_T

        def store_h(li, s, n, hn, t_hn, last):
            if last:
                if n < 2:
                    return
                i = C.nrm_i % 2
                C.nrm_i += 1
                stt_ = C.stat[i]
                P.act(C.xn[i][:], hn[:], AF.Square, rd=[t_hn], wr=[C.t_xn[i], C.t_stat[i]], accum_out=stt_[:, 0:1])
                P.act(stt_[:, 1:2], stt_[:, 0:1], AF.Sqrt, rd=[C.t_stat[i], C.t_eps], wr=[C.t_stat[i]],
                      bias=C.eps[:, 0:1], scale=1.0 / 1024)
                P.recip(stt_[:, 2:3], stt_[:, 1:2], rd=[C.t_stat[i]], wr=[C.t_stat[i]])
                P.stt(hn[:], hn[:], stt_[:, 2:3], fg_bc[:], ALU.mult, ALU.mult, rd=[t_hn, C.t_stat[i], t_fg],
                      wr=[t_hn])
                P.dma(out[s, (n - 2) * 128:(n - 1) * 128, :], hn[:], rd=[t_hn], wr=[t_out], eng='pool')
            else:
                dst, tk = h_dst(li, s, n)
                P.dma(dst, hn[:], rd=[t_hn], wr=[tk], eng='pool')
        C.store_h = store_h

        for idx, L in enumerate(layers):
            last = do_final and (idx == len(layers) - 1)
            need_ctx = not last
            load_mod(L)
            P.barrier()
            if L % 2 == 1:
                odd_layer(C, L, idx, odd_w_in[L // 2], odd_w_out[L // 2], pool_w[L // 2], pool_scale[L // 2],
                          bands_in, need_ctx, last)
            else:
                even_layer(C, L, idx, EV, need_ctx, last)
        P.finish()
    return nc


def load_weight_bf16(C, dst, src_ap, rows_k, cols, t_dst, stage, t_stage):
    P = C.P
    for k in range(rows_k):
        i = k % len(stage)
        P.dma(stage[i][:, 0:cols], src_ap[k * 128:(k + 1) * 128, :], wr=[t_stage[i]])
        P.copy(dst[:, k, :], stage[i][:, 0:cols], rd=[t_stage[i]], wr=[t_dst], eng=('pool' if k % 2 == 0 else 'dve'))


def odd_layer(C, L, idx, w_in_d, w_out_d, pool_w_d, pool_scale_d, bands_in, need_ctx, last):
    nc, P, NS, NT = C.nc, C.P, C.NS, C.NT
    with ExitStack() as st:
        sb = lambda shape, dtype, name: st.enter_context(nc.sbuf_tensor("o%d_%s" % (idx, name), shape, dtype))
        w_in = sb([128, 8, 2048], BF16, "w_in")
        w_out = sb([128, 8, 1024], BF16, "w_out")
        pw = sb([128, 8, 256], BF16, "pw")
        bands = sb([128, 4, 6, 128], BF16, "bands")
        psc = sb([128, 8], F32, "psc")
        stage = [sb([128, 2048], F32, "stage%d" % i) for i in range(2)]
        t_stage = [Tok(), Tok()]
        t_w = Tok()
        load_weight_bf16(C, w_in, w_in_d, 8, 2048, t_w, stage, t_stage)
        load_weight_bf16(C, w_out, w_out_d, 8, 1024, t_w, stage, t_stage)
        load_weight_bf16(C, pw, pool_w_d.rearrange("g c d -> (g c) d"), 8, 256, t_w, stage, t_stage)
        for ri in range(4):
            i = ri % 2
            P.dma(stage[i][:, 0:768].rearrange("p (v t) -> p v t", v=6), bands_in[ri].rearrange("v p t -> p v t"),
                  wr=[t_stage[i]])
            P.copy(bands[:, ri, :, :], stage[i][:, 0:768].rearrange("p (v t) -> p v t", v=6), rd=[t_stage[i]],
                   wr=[t_w], eng='pool')
        P.dma(psc[:], pool_scale_d.rearrange("(k p) -> p k", p=128), wr=[t_w], slow=True)

        NR = 4
        aT = [sb([128, 8, 128], BF16, "aT%d" % i) for i in range(2)]
        t_aT = [Tok(), Tok()]
        u_tm = [sb([128, 1024], BF16, "u%d" % i) for i in range(NR)]
        t_u = [Tok() for _ in range(NR)]
        sg = [sb([128, 8, 128], BF16, "sg%d" % i) for i in range(NR)]
        t_sg = [Tok() for _ in range(NR)]
        pTt = sb([128, 8, 128], BF16, "pTt")
        t_pTt = Tok()
        mT = sb([128, 8, 128], BF16, "mT")
        t_mT = Tok()
        tmp = sb([128, 1024], F32, "tmp")
        t_tmp = Tok()
        hn = [sb([128, 1024], F32, "hn%d" % i) for i in range(2)]
        t_hn = [Tok(), Tok()]
        ps, t_ps = C.ps, C.t_ps
        fin_i = [0]

        for s in range(NS):
            seqs = []
            if need_ctx:
                seqs.append((2, [0, 1]))
            seqs.append((s, list(range(2, 2 * (NT + 1)))))
            for which, subs in seqs:
                S = len(subs)

                BA, BC, BB, BD, BE = (5, 6), (1, 2), (0, 3), (4, 1), (2, 6)

                def f_norm(j):
                    n = subs[j]
                    slot = n % 4
                    C.load_h(idx, s, n, slot)
                    a = aT[j % 2]
                    C.norm_T(slot, which, lambda k: a[:, k, :], t_aT[j % 2])

                def f_u(j):
                    a = aT[j % 2]
                    r = j % NR
                    for nb in range(2):
                        bk = BA[nb]
                        for k in range(8):
                            P.mm(ps[bk][:, :], a[:, k, :], w_in[:, k, nb * 512:(nb + 1) * 512], start=(k == 0),
                                 stop=(k == 7), rd=[t_aT[j % 2], t_w], wr=[t_ps[bk]])
                        P.copy(u_tm[r][:, nb * 512:(nb + 1) * 512], ps[bk][:, :], rd=[t_ps[bk]], wr=[t_u[r]])

                def f_gate(j):
                    a = aT[j % 2]
                    r = j % NR
                    for m in range(8):
                        bank = BB[m // 4]
                        for k in range(8):
                            P.mm(ps[bank][:, (m % 4) * 128:(m % 4 + 1) * 128], w_in[:, k, 1024 + m * 128:1024 + (m + 1) * 128],
                                 a[:, k, :], start=(k == 0), stop=(k == 7), rd=[t_aT[j % 2], t_w], wr=[t_ps[bank]])
                    for b2 in range(2):
                        P.act(sg[r][:, b2 * 4:(b2 + 1) * 4, :], ps[BB[b2]][:, :].rearrange("p (m t) -> p m t", m=4),
                              AF.Silu, rd=[t_ps[BB[b2]]], wr=[t_sg[r]])

                def b_pool(j):
                    r = j % NR
                    if S == 1:
                        v = 5
                    elif j == 0:
                        v = 0
                    elif j == S - 1:
                        v = 2
                    else:
                        v = 1
                    for m in range(8):
                        bank = BC[m // 4]
                        ri = m // 2
                        dst = ps[bank][:, (m % 4) * 128:(m % 4 + 1) * 128]
                        parts = [(u_tm[r][:, m * 128:(m + 1) * 128], bands[:, ri, v, :], t_u[r])]
                        if j > 0:
                            rp = (j - 1) % NR
                            parts.append((u_tm[rp][64:128, m * 128:(m + 1) * 128], bands[64:128, ri, 3, :], t_u[rp]))
                        if j < S - 1:
                            rn = (j + 1) % NR
                            parts.append((u_tm[rn][0:32, m * 128:(m + 1) * 128], bands[0:32, ri, 4, :], t_u[rn]))
                        for pi, (l, rr, tk) in enumerate(parts):
                            P.mm(dst, l, rr, start=(pi == 0), stop=(pi == len(parts) - 1), rd=[tk, t_w], wr=[t_ps[bank]])
                    for b2 in range(2):
                        P.copy(pTt[:, b2 * 4:(b2 + 1) * 4, :], ps[BC[b2]][:, :].rearrange("p (m t) -> p m t", m=4),
                               rd=[t_ps[BC[b2]]], wr=[t_pTt])

                def b_pw(j):
                    r = j % NR
                    for g in range(4):
                        for dd in range(2):
                            m = g * 2 + dd
                            bank = BD[m // 4]
                            for cc in range(2):
                                P.mm(ps[bank][:, (m % 4) * 128:(m % 4 + 1) * 128], pw[:, g * 2 + cc, dd * 128:(dd + 1) * 128],
                                     pTt[:, g * 2 + cc, :], start=(cc == 0), stop=(cc == 1), rd=[t_pTt, t_w],
                                     wr=[t_ps[bank]])
                    for m in range(8):
                        bank = BD[m // 4]
                        P.stt(mT[:, m, :], ps[bank][:, (m % 4) * 128:(m % 4 + 1) * 128], psc[:, m:m + 1], sg[r][:, m, :],
                              ALU.mult, ALU.mult, rd=[t_ps[bank], t_sg[r], t_w], wr=[t_mT])

                def b_out(j):
                    n = subs[j]
                    slot = n % 4
                    hi = fin_i[0] % 2
                    fin_i[0] += 1
                    for nb in range(2):
                        bk = BE[nb]
                        for k in range(8):
                            P.mm(ps[bk][:, :], mT[:, k, :], w_out[:, k, nb * 512:(nb + 1) * 512], start=(k == 0),
                                 stop=(k == 7), rd=[t_mT, t_w], wr=[t_ps[bk]])
                        sl = slice(nb * 512, (nb + 1) * 512)
                        P.tt(tmp[:, sl], ps[bk][:, :], C.gate_bc[:, which, sl], ALU.mult,
                             rd=[t_ps[bk], C.t_mod], wr=[t_tmp])
                        P.tt(hn[hi][:, sl], tmp[:, sl], C.hT[slot][:, sl], ALU.add, rd=[t_tmp, C.t_hT[slot]],
                             wr=[t_hn[hi]], eng='pool')
                    C.store_h(idx, s, n, hn[hi], t_hn[hi], last)

                f_norm(0)
                for j in range(S + 1):
                    if j < S:
                        f_u(j)
                    if j >= 1:
                        b_pool(j - 1)
                    if j < S:
                        f_gate(j)
                    if j >= 1:
                        b_pw(j - 1)
                    if j + 1 < S:
                        f_norm(j + 1)
                    if j >= 1:
                        b_out(j - 1)


def make_in_maps(inputs, n_cores, NS, NT):
    f = lambda a: np.ascontiguousarray(np.asarray(a, dtype=np.float32))
    x = f(inputs['x'])
    ctx = f(inputs['ctx'])
    c = f(inputs['c'])
    shared = {k: f(inputs[k]) for k in ['ada_w', 'ada_b', 'norm_g', 'odd_w_in', 'odd_w_out', 'pool_w', 'pool_scale',
                                        'final_g']}
    for k in ['even_w_in', 'even_w_out', 'attn_sink', 'ssm_a_re', 'ssm_a_im', 'ssm_log_dt', 'ssm_b_re', 'ssm_b_im',
              'ssm_c_re', 'ssm_c_im', 'ssm_d', 'glu_w', 'glu_b']:
        shared[k] = f(inputs[k])
    t = np.arange(NT * 256)
    r = np.arange(128)
    inv_freq = (10000.0 ** (-np.arange(16, dtype=np.float32) / 16)).astype(np.float32)
    pos = np.where(((r >> 5) & 1)[:, None] == 0, (t // 64)[None, :], (t % 64)[None, :]).astype(np.float32)
    ang = pos * inv_freq[r & 15][:, None]
    shared['rope_cos'] = np.cos(ang).astype(np.float32)
    sgn = np.where(((r >> 4) & 1) == 0, -1.0, 1.0).astype(np.float32)[:, None]
    shared['rope_sin'] = (np.sin(ang) * sgn).astype(np.float32)
    pm = np.zeros((128, 128), np.float32)
    pm[r ^ 16, r] = 1.0
    shared['perm'] = pm
    kk = np.arange(128)[:, None]
    qq = np.arange(128)[None, :]
    shared['masks'] = np.stack([(kk >= qq), (kk <= qq)]).astype(np.float32)
    hm = np.zeros((128, 3), np.float32)
    hm[:64, 0] = 1.0
    hm[64:, 1] = 1.0
    hm[96:, 2] = 1.0
    shared['hmask'] = hm
    shared['ident'] = np.eye(128, dtype=np.float32)
    shared['bands'] = _band_consts()
    maps = []
    for i in range(n_cores):
        m = dict(shared)
        m['x'] = np.ascontiguousarray(x[i * NS:(i + 1) * NS, :NT * 256])
        m['ctx'] = np.ascontiguousarray(ctx[i * NS:(i + 1) * NS])
        c3 = np.zeros((3, 1024), np.float32)
        c3[0:NS] = c[i * NS:(i + 1) * NS]
        c3[2] = f(inputs['c_ctx'])
        m['c3'] = c3
        maps.append(m)
    return maps


def kernel(**inputs):
    NS, NT, n_cores = 2, 16, 8
    nc = build_program(NS, NT, [0, 1, 2, 3], True)
    maps = make_in_maps(inputs, n_cores, NS, NT)
    res = run_bass_kernel_spmd(nc, maps, core_ids=list(range(n_cores)))
    return np.concatenate([r['out'] for r in res.results], axis=0)


def even_layer(C, L, idx, EV, need_ctx, last):
    nc, P, NS, NT, TOK = C.nc, C.P, C.NS, C.NT, C.TOK
    j = L // 2
    NTL = NT + 1
    NSUB = 2 * NTL
    ps, t_ps = C.ps, C.t_ps
    QO, KO, VO, GAO, UO, GSO = 0, 512, 640, 768, 1280, 1792
    PI = float(np.pi)
    with ExitStack() as st:
        sbn = [0]

        def sb(shape, dtype, stack=st):
            sbn[0] += 1
            return stack.enter_context(nc.sbuf_tensor("e%d_%d" % (idx, sbn[0]), shape, dtype))
        w_in = sb([128, 8, 2304], BF16)
        t_w = Tok()
        dvec = sb([128, 4], F32)
        glub = sb([128, 4], F32)
        esink = sb([128, 8], F32)
        t_sm = Tok()
        hmask3 = sb([128, 3], F32)
        hmask = hmask3[:, 0:2]
        perm = sb([128, 128], BF16)
        masks = sb([128, 2, 128], BF16)
        s0 = ExitStack()
        stage0 = sb([128, 2304], F32, s0)
        stage1 = sb([128, 2304], F32, s0)
        stage = [stage0, stage1]
        t_st0 = Tok()
        t_stage = [t_st0, Tok()]
        load_weight_bf16(C, w_in, EV.w_in[j], 8, 2304, t_w, stage, t_stage)
        P.dma(dvec[:], EV.d[j].rearrange("(k p) -> p k", p=128), wr=[t_sm], slow=True)
        P.dma(glub[:], EV.glu_b[j].rearrange("(k p) -> p k", p=128), wr=[t_sm], slow=True)
        P.dma(esink[:], EV.sink[j].partition_broadcast(128), wr=[t_sm])
        P.act(esink[:], esink[:], AF.Exp, rd=[t_sm], wr=[t_sm])
        P.dma(hmask3[:], EV.hmask[:, :], wr=[t_sm])
        P.dma(stage0[:, 0:128], EV.perm[:, :], wr=[t_st0])
        P.copy(perm[:], stage0[:, 0:128], rd=[t_st0], wr=[t_sm])
        P.dma(stage0[:, 0:256].rearrange("p (a b) -> p a b", a=2), EV.masks.rearrange("a p b -> p a b"), wr=[t_st0])
        P.copy(masks[:], stage0[:, 0:256].rearrange("p (a b) -> p a b", a=2), rd=[t_st0], wr=[t_sm])
        P.barrier()
        s0.close()
        if STAGE == 0.1:
            return
        W2 = sb([128, 2, 2, 16], F32)
        W3 = sb([128, 2, 2, 16], F32)
        t_W = Tok()
        Psi = sb([128, 2, 2, 16, 8, 32], BF16)
        Kmat = sb([128, 4, 15, 128], BF16)
        t_Gam, t_Psi, t_K = Tok(), Tok(), Tok()
        P.memset(Kmat[:], 0.0, wr=[t_K], eng='pool')
        aT = sb([128, 8, 256], BF16)
        t_aT = Tok()
        uT = sb([128, 4, 256], BF16)
        t_uT = Tok()
        kraw = sb([128, 256], BF16)
        t_kraw = Tok()
        rcos = sb([128, 256], F32)
        rsin = sb([128, 256], F32)
        t_rope = Tok()
        r1 = sb([128, 256], F32)
        r2 = sb([128, 256], F32)
        t_r = Tok()
        sa = ExitStack()
        st.enter_context(sa)
        Gam = sb([128, 2, 4, 8, 2, 128], BF16, sa)
        with ExitStack() as sp:
            f = lambda shape: sb(shape, F32, sp)
            are, aim, ldt = f([128, 2, 16]), f([128, 2, 16]), f([128, 2, 16])
            Bre, Bim, Cre, Cim = (f([128, 2, 16, 16]) for _ in range(4))
            t_p = Tok()
            for h in range(2):
                sl = slice(64 * h, 64 * h + 64)
                P.dma(are[sl], EV.a_re[j, :, h::2, :].rearrange("d r p -> p d r"), wr=[t_p], slow=True)
                P.dma(aim[sl], EV.a_im[j, :, h::2, :].rearrange("d r p -> p d r"), wr=[t_p], slow=True)
                P.dma(ldt[sl], EV.log_dt[j, :, h::2].partition_broadcast(64), wr=[t_p], slow=True)
                for d in range(2):
                    P.dma(Bre[sl, d], EV.b_re[j, d, h::2, :, :].rearrange("r p c -> p r c"), wr=[t_p], slow=True)
                    P.dma(Bim[sl, d], EV.b_im[j, d, h::2, :, :].rearrange("r p c -> p r c"), wr=[t_p], slow=True)
            ccm = f([128, 128])
            t_ccm = Tok()
            for (src, dstC) in ((EV.c_re, Cre), (EV.c_im, Cim)):
                for d in range(2):
                    for blk in range(2):
                        for r in range(8):
                            pr = blk * 8 + r
                            P.dma(ccm[16 * r:16 * r + 16, :].rearrange("c (h p) -> c h p", h=2),
                                  src[j, d, 2 * pr:2 * pr + 2, :, :].rearrange("h c p -> c h p"), wr=[t_ccm])
                        P.tr(ps[0][:, 0:128], ccm[:], C.ident_f[:], rd=[t_ccm, C.t_ident], wr=[t_ps[0]])
                        P.copy(dstC[:, d, blk * 8:(blk + 1) * 8, :], ps[0][:, 0:128].rearrange("p (r c) -> p r c", r=8),
                               rd=[t_ps[0]], wr=[t_p])
            if STAGE == 0.2:
                return
            V3 = [128, 2, 16]
            dtv, x8, th8, mag, cs, sn = (f(V3) for _ in range(6))
            halfpi = f([128, 1])
            P.memset(halfpi[:], PI / 2, wr=[t_p])
            P.act(dtv[:], ldt[:], AF.Exp, rd=[t_p], wr=[t_p])
            P.stt(x8[:], are[:], 0.125, dtv[:], ALU.mult, ALU.mult, rd=[t_p], wr=[t_p])
            P.stt(th8[:], aim[:], 0.125, dtv[:], ALU.mult, ALU.mult, rd=[t_p], wr=[t_p])
            P.act(mag[:], x8[:], AF.Exp, rd=[t_p], wr=[t_p])
            P.act(cs[:], th8[:], AF.Sin, rd=[t_p], wr=[t_p], scale=-1.0, bias=halfpi[:, 0:1])
            P.act(sn[:], th8[:], AF.Sin, rd=[t_p], wr=[t_p])
            pwr = f([128, 9, 2, 16])
            pwi = f([128, 9, 2, 16])
            ta, tb = f(V3), f(V3)
            P.tt(ta[:], mag[:], cs[:], ALU.mult, rd=[t_p], wr=[t_p])
            P.tt(tb[:], mag[:], sn[:], ALU.mult, rd=[t_p], wr=[t_p])

            def cmul(or_, oi_, ar, ai, br, bi, t1, t2):
                P.tt(t1, ar, br, ALU.mult, rd=[t_p], wr=[t_p])
                P.tt(t2, ai, bi, ALU.mult, rd=[t_p], wr=[t_p])
                P.tt(or_, t1, t2, ALU.subtract, rd=[t_p], wr=[t_p])
                P.tt(t1, ar, bi, ALU.mult, rd=[t_p], wr=[t_p])
                P.tt(t2, ai, br, ALU.mult, rd=[t_p], wr=[t_p])
                P.tt(oi_, t1, t2, ALU.add, rd=[t_p], wr=[t_p])
            t1, t2, sqr, sqi = f(V3), f(V3), dtv, x8
            cur_r, cur_i = ta, tb
            for it in range(3):
                cmul(sqr[:], sqi[:], cur_r[:], cur_i[:], cur_r[:], cur_i[:], t1[:], t2[:])
                P.copy(cur_r[:], sqr[:], rd=[t_p], wr=[t_p])
                P.copy(cur_i[:], sqi[:], rd=[t_p], wr=[t_p])
            P.memset(pwr[:, 0], 1.0, wr=[t_p])
            P.memset(pwi[:, 0], 0.0, wr=[t_p])
            P.copy(pwr[:, 1], cur_r[:], rd=[t_p], wr=[t_p])
            P.copy(pwi[:, 1], cur_i[:], rd=[t_p], wr=[t_p])
            for k in range(2, 9):
                cmul(pwr[:, k], pwi[:, k], pwr[:, k - 1], pwi[:, k - 1], pwr[:, 1], pwi[:, 1], t1[:], t2[:])
            am1, nr, ni, inv, kr, ki = (f(V3) for _ in range(6))
            P.ts(am1[:], pwr[:, 1], -1.0, None, ALU.add, rd=[t_p], wr=[t_p])
            P.tt(t1[:], am1[:], are[:], ALU.mult, rd=[t_p], wr=[t_p])
            P.tt(t2[:], pwi[:, 1], aim[:], ALU.mult, rd=[t_p], wr=[t_p])
            P.tt(nr[:], t1[:], t2[:], ALU.add, rd=[t_p], wr=[t_p])
            P.tt(t1[:], pwi[:, 1], are[:], ALU.mult, rd=[t_p], wr=[t_p])
            P.tt(t2[:], am1[:], aim[:], ALU.mult, rd=[t_p], wr=[t_p])
            P.tt(ni[:], t1[:], t2[:], ALU.subtract, rd=[t_p], wr=[t_p])
            P.tt(t1[:], are[:], are[:], ALU.mult, rd=[t_p], wr=[t_p])
            P.tt(t2[:], aim[:], aim[:], ALU.mult, rd=[t_p], wr=[t_p])
            P.tt(inv[:], t1[:], t2[:], ALU.add, rd=[t_p], wr=[t_p])
            P.recip(inv[:], inv[:], rd=[t_p], wr=[t_p])
            P.tt(kr[:], nr[:], inv[:], ALU.mult, rd=[t_p], wr=[t_p])
            P.tt(ki[:], ni[:], inv[:], ALU.mult, rd=[t_p], wr=[t_p])
            V4 = [128, 2, 16, 16]
            bc4 = lambda a: a.unsqueeze(3).broadcast_to(V4)
            Bbr, Bbi, u1, u2 = f(V4), f(V4), f(V4), f(V4)
            cmul(Bbr[:], Bbi[:], bc4(kr[:]), bc4(ki[:]), Bre[:], Bim[:], u1[:], u2[:])
            for d in range(2):
                P.copy(W2[:, d, 0, :], pwr[:, 8, d, :], rd=[t_p], wr=[t_W])
                P.copy(W2[:, d, 1, :], pwr[:, 8, d, :], rd=[t_p], wr=[t_W])
                P.ts(W3[:, d, 0, :], pwi[:, 8, d, :], -1.0, None, ALU.mult, rd=[t_p], wr=[t_W])
                P.copy(W3[:, d, 1, :], pwi[:, 8, d, :], rd=[t_p], wr=[t_W])
            CP = sb([128, 2, 2, 16, 32], BF16, sp)
            hm4 = lambda sgn_ap: sgn_ap.unsqueeze(1).unsqueeze(3).broadcast_to([128, 16, 2, 16])
            nhmask = f([128, 2])
            P.ts(nhmask[:], hmask, -1.0, None, ALU.mult, rd=[t_sm], wr=[t_p])
            for d in range(2):
                P.tt(CP[:, 0, d].rearrange("p r (h c) -> p r h c", h=2), Cre[:, d].unsqueeze(2).broadcast_to([128, 16, 2, 16]),
                     hm4(hmask), ALU.mult, rd=[t_p, t_sm], wr=[t_p])
                P.tt(CP[:, 1, d].rearrange("p r (h c) -> p r h c", h=2), Cim[:, d].unsqueeze(2).broadcast_to([128, 16, 2, 16]),
                     hm4(nhmask[:]), ALU.mult, rd=[t_p, t_sm], wr=[t_p])
            Yr, Yi = f([128, 16, 16]), f([128, 16, 16])
            bc3 = lambda a: a.unsqueeze(2).broadcast_to([128, 16, 16])
            for d in range(2):
                for t in range(8):
                    k = t + 1 if d == 0 else 8 - t
                    cmul(Yr[:], Yi[:], Cre[:, d], Cim[:, d], bc3(pwr[:, k, d, :]), bc3(pwi[:, k, d, :]), u1[:, 0], u2[:, 0])
                    for ri_, (Y_, hm_) in enumerate(((Yr, hmask), (Yi, nhmask[:]))):
                        P.tt(Psi[:, d, ri_, :, t, :].rearrange("p r (h c) -> p r h c", h=2),
                             Y_[:].unsqueeze(2).broadcast_to([128, 16, 2, 16]), hm4(hm_), ALU.mult,
                             rd=[t_p, t_sm], wr=[t_Psi])
            if STAGE == 0.3:
                return
            Xr, Xi = Yr, Yi
            XP = [sb([128, 4, 2, 16], BF16, sp) for a_ in range(2)]
            Kst = sb([32, 4, 15, 32], BF16, sp)
            t_XP = [Tok(), Tok()]
            xi = 0
            for d in range(2):
                for k in range(8):
                    cmul(Xr[:], Xi[:], Bbr[:, d], Bbi[:, d], bc3(pwr[:, k, d, :]), bc3(pwi[:, k, d, :]), u1[:, 0], u2[:, 0])
                    s_idx = 7 - k if d == 0 else k
                    for i in range(4):
                        kb = 1 + (i % 2)
                        for ri, X in enumerate((Xr, Xi)):
                            xp = XP[xi % 2]
                            txp = t_XP[xi % 2]
                            xi += 1
                            P.tt(xp[:], X[:, 4 * i:4 * i + 4, :].unsqueeze(2).broadcast_to([128, 4, 2, 16]),
                                 hmask.unsqueeze(1).unsqueeze(3).broadcast_to([128, 4, 2, 16]), ALU.mult,
                                 rd=[t_p, t_sm], wr=[txp])
                            xpf = xp[:].rearrange("p q h c -> p (q h c)")
                            P.tr(C.pT[:, 0:128], xpf, C.ident[:], rd=[txp, C.t_ident], wr=[C.t_pT])
                            P.copy(Gam[:, d, i, s_idx, ri, :], C.pT[:, 0:128], rd=[C.t_pT], wr=[t_Gam])
                            for q in range(3):
                                P.mm(ps[kb][32 * q:32 * q + 32, 32 * q:32 * q + 32], xp[:, q].rearrange("p h c -> p (h c)"),
                                     CP[:, ri, d, 4 * i + q, :], start=(ri == 0), stop=(ri == 1), rd=[txp, t_p],
                                     wr=[t_ps[kb]])
                            P.mm(ps[kb + 2][0:32, 128:160], xp[:, 3].rearrange("p h c -> p (h c)"), CP[:, ri, d, 4 * i + 3, :],
                                 start=(ri == 0), stop=(ri == 1), rd=[txp, t_p], wr=[t_ps[kb + 2]])
                        li = 7 + k if d == 0 else 7 - k
                        for q in range(4):
                            blk = slice(32 * q, 32 * q + 32)
                            if q < 3:
                                dst_, src_, tkp = Kmat[blk, i, (7 if (d == 1 and k == 0) else li), blk], ps[kb][blk, blk], t_ps[kb]
                            else:
                                dst_, src_, tkp = Kst[:, i, (7 if (d == 1 and k == 0) else li), :], ps[kb + 2][0:32, 128:160], t_ps[kb + 2]
                            if d == 1 and k == 0:
                                P.tt(dst_, dst_, src_, ALU.add, rd=[tkp, t_K], wr=[t_K])
                            else:
                                P.copy(dst_, src_, rd=[tkp], wr=[t_K])
            for i in range(4):
                P.dma(Kmat[96:128, i, :, 96:128], Kst[:, i, :, :], rd=[t_K], wr=[t_K])
        if EV.dbg:
            for nm_, t_, tk_ in (("Gam", Gam, t_Gam), ("Psi", Psi, t_Psi), ("Kmat", Kmat, t_K)):
                n_el = 1
                for v_ in t_.shape[1:]:
                    n_el *= v_
                dd_ = nc.dram_tensor("dbg_%s_%d" % (nm_, idx), [128, n_el], BF16, kind="ExternalOutput").ap()
                names_ = "abcdefg"[:len(t_.shape) - 1]
                P.dma(dd_[:, :], t_[:].rearrange("p %s -> p (%s)" % (" ".join(names_), " ".join(names_))), rd=[tk_], wr=[Tok()])
            for nm_, t_ in (("W2", W2), ("W3", W3)):
                dd_ = nc.dram_tensor("dbg_%s_%d" % (nm_, idx), [128, 64], F32, kind="ExternalOutput").ap()
                P.dma(dd_[:, :], t_[:].rearrange("p a b c -> p (a b c)"), rd=[t_W], wr=[Tok()])
        P.barrier()
        if STAGE == 1:
            sa.close()
            return
        kdup = sb([128, 8, 2, 128], BF16, sa)
        uT3 = sb([128, 4, 256], BF16, sa)
        t_kdup = Tok()
        for hk in range(2):
            for cp in range(2):
                P.copy(kdup[:, :, hk, cp * 64:(cp + 1) * 64], w_in[:, :, KO + hk * 64:KO + (hk + 1) * 64], rd=[t_w],
                       wr=[t_kdup], eng='pool')
        kTt = sb([128, 2, 256], BF16, sa)
        t_kT = Tok()
        Vt = sb([128, 2, 65], BF16, sa)
        t_Vt = Tok()
        Gblk = sb([128, 64, 32], F32, sa)
        t_G = Tok()
        t_Gd = [[Tok() for _ in range(NTL)] for _ in range(NS)]
        t_Hd = [[Tok() for _ in range(NTL)] for _ in range(NS)]
        t_kd = [Tok() for _ in range(NS)]
        t_vd = [Tok() for _ in range(NS)]
        P.memset(Vt[:], 1.0, wr=[t_Vt], eng='pool')

        def gslot(s, tl, sub):
            return (2 * (s * NTL + tl) + sub) % 4

        def norm_tile(s, tl, which):
            for sub in range(2):
                n = 2 * tl + sub
                C.load_h(idx, s, n, gslot(s, tl, sub))
                C.norm_T(gslot(s, tl, sub), which, lambda k, sub=sub: aT[:, k, sub * 128:(sub + 1) * 128], t_aT)

        def rope(dst, raw, t_raw, tl, t_dst):
            P.mm(ps[6][:, 0:256], perm[:], raw, start=True, stop=True, rd=[t_raw, t_sm], wr=[t_ps[6]])
            P.tt(r1[:], raw, rcos[:], ALU.mult, rd=[t_raw, t_rope], wr=[t_r])
            P.tt(r2[:], ps[6][:, 0:256], rsin[:], ALU.mult, rd=[t_ps[6], t_rope], wr=[t_r])
            P.tt(dst, r1[:], r2[:], ALU.add, rd=[t_r], wr=[t_dst], eng='pool')

        def load_rope(tl):
            P.dma(rcos[:], EV.rope_cos[:, (tl - 1) * 256:tl * 256], wr=[t_rope])
            P.dma(rsin[:], EV.rope_sin[:, (tl - 1) * 256:tl * 256], wr=[t_rope])

        def proj_fm(m_cols, bank, ncols=256, extra=()):
            for k in range(8):
                P.mm(ps[bank][:, 0:256], m_cols(k), aT[:, k, :], start=(k == 0), stop=(k == 7),
                     rd=[t_aT, t_w] + list(extra), wr=[t_ps[bank]])

        tiles_all = [(s_, tl_) for s_ in range(NS) for tl_ in range(NTL)]

        def norm_next(s_, tl_, force=False):
            if (not PIPE) and not force:
                return
            ii = tiles_all.index((s_, tl_)) + 1
            if ii < len(tiles_all):
                s2, tl2 = tiles_all[ii]
                norm_tile(s2, tl2, 2 if tl2 == 0 else s2)
        norm_tile(0, 0, 2)
        for s in range(NS):
            for tl in range(NTL):
                which = 2 if tl == 0 else s
                if tl > 0:
                    load_rope(tl)
                for i in range(4):
                    b = i % 2
                    proj_fm(lambda k, i=i: w_in[:, k, UO + i * 128:UO + (i + 1) * 128], b)
                    P.copy(uT[:, i, :].rearrange("p (s j) -> p j s", s=8), ps[b][:, 0:256].rearrange("p (j s) -> p j s", s=8),
                           rd=[t_ps[b]], wr=[t_uT])
                if PA != 1:
                    P.ts(uT3[64:128], uT[64:128], hmask3[64:128, 2:3], None, ALU.mult, rd=[t_uT, t_sm], wr=[t_uT])
                for hk in range(2):
                    b = hk
                    proj_fm(lambda k, hk=hk: kdup[:, k, hk, :], b, extra=[t_kdup])
                    if tl == 0:
                        P.copy(kTt[:, hk, :], ps[b][:, 0:256], rd=[t_ps[b]], wr=[t_kT])
                    else:
                        P.copy(kraw[:], ps[b][:, 0:256], rd=[t_ps[b]], wr=[t_kraw])
                        rope(kTt[:, hk, :], kraw[:], t_kraw, tl, t_kT)
                P.dma(EV.kT_d[s, :, :, tl * 256:(tl + 1) * 256], kTt[:], rd=[t_kT], wr=[t_kd[s]], eng='pool')
                for sub in range(2):
                    for k in range(8):
                        P.mm(ps[2][:, 0:128], aT[:, k, sub * 128:(sub + 1) * 128], w_in[:, k, VO:VO + 128], start=(k == 0),
                             stop=(k == 7), rd=[t_aT, t_w], wr=[t_ps[2]])
                    P.copy(Vt[:, :, 0:64], ps[2][:, 0:128].rearrange("p (h e) -> p h e", h=2), rd=[t_ps[2]], wr=[t_Vt])
                    P.dma(EV.V_d[s, 2 * tl + sub], Vt[:].rearrange("p h e -> p (h e)"), rd=[t_Vt], wr=[t_vd[s]], eng='pool')
                norm_next(s, tl)
                def g_mm(q, d, ri, i, s8):
                    rows = slice(32 * q, 32 * q + 32) if q < 3 else slice(64, 128)
                    usrc = uT if q < 3 else uT3
                    col = ((d * 2 + ri) * 4 + i) * 32
                    if d == 0:
                        rhs = usrc[rows, i, s8 * 32:(s8 + 1) * 32]
                    else:
                        rhs = usrc[rows, i, s8 * 32 + 31:(s8 * 32 - 1 if s8 > 0 else None):-1]
                    P.mm(ps[3 + q][:, col:col + 32], Gam[rows, d, i, s8, ri, :], rhs, start=(s8 == 0),
                         stop=(s8 == 7), rd=[t_Gam, t_uT], wr=[t_ps[3 + q]])
                for qs_ in ((0, 1, 2), (3,)):
                    for d in range(2):
                        for ri in range(2):
                            for i in range(4):
                                for s8 in range(8):
                                    for q in qs_:
                                        g_mm(q, d, ri, i, s8)
                    for q in qs_:
                        P.copy(Gblk[:, q::4, :], ps[3 + q][:, :].rearrange("p (r k) -> p r k", r=16),
                               rd=[t_ps[3 + q]], wr=[t_G])
                P.dma(EV.G_d[s, tl], Gblk[:].rearrange("p a k -> p (a k)"), rd=[t_G], wr=[t_Gd[s][tl]], eng='pool')
                if not PIPE:
                    norm_next(s, tl, force=True)
        P.barrier()
        sa.close()
        if STAGE == 2:
            return
        ss = ExitStack()
        _sb0 = sb
        sb = lambda shape, dtype: _sb0(shape, dtype, ss)
        NCH = NS * 2
        Gs = [sb([128, NCH, 2, 16, 32], F32) for _ in range(2)]
        t_Gs = [Tok(), Tok()]
        Hx = sb([128, 33, NCH, 2, 16], F32)
        t_Hx = Tok()
        Hb = sb([128, NCH, 2, 16, 32], BF16)
        t_Hb = Tok()
        W2c = sb([128, NCH, 2, 16], F32)
        W3c = sb([128, NCH, 2, 16], F32)
        sc1 = sb([128, NCH, 2, 16], F32)
        sc2 = sb([128, NCH, 2, 16], F32)
        sc3 = sb([128, NCH, 2, 16], F32)
        t_sa, t_sb, t_sc = Tok(), Tok(), Tok()
        for s in range(NS):
            for d in range(2):
                P.copy(W2c[:, s * 2 + d], W2[:, d], rd=[t_W], wr=[t_W], eng='pool')
                P.copy(W3c[:, s * 2 + d], W3[:, d], rd=[t_W], wr=[t_W], eng='pool')
        order = ([i for i in range(NTL)], [0] + list(range(NTL - 1, 0, -1)))
        P.memset(Hx[:, 0], 0.0, wr=[t_Hx])

        def load_g(step):
            g, tg = Gs[step % 2], t_Gs[step % 2]
            for s in range(NS):
                for d in range(2):
                    tl_ = order[d][step]
                    P.dma(g[:, s * 2 + d].rearrange("p a r k -> p (a r k)"), EV.G_d[s, tl_, :, d * 1024:(d + 1) * 1024],
                          rd=[t_Gd[s][tl_]], wr=[tg])
        load_g(0)
        for step in range(NTL):
            if step + 1 < NTL:
                load_g(step + 1)
            g, tg = Gs[step % 2], t_Gs[step % 2]
            for k in range(32):
                cur = Hx[:, k]
                P.tt(sc1[:], W2c[:], cur, ALU.mult, rd=[t_Hx, t_W], wr=[t_sa])
                P.tt(sc2[:], W3c[:], cur[:, :, ::-1, :], ALU.mult, rd=[t_Hx, t_W], wr=[t_sb])
                P.tt(sc3[:], sc1[:], g[:, :, :, :, k], ALU.add, rd=[t_sa, tg], wr=[t_sc])
                P.tt(Hx[:, k + 1], sc2[:], sc3[:], ALU.add, rd=[t_sb, t_sc], wr=[t_Hx])
            for s in range(NS):
                for d in range(2):
                    c = s * 2 + d
                    src = Hx[:, 0:32, c].rearrange("p k a r -> p a r k")
                    if d == 0:
                        P.copy(Hb[:, c], src, rd=[t_Hx], wr=[t_Hb], eng='pool')
                    else:
                        P.copy(Hb[:, c, :, :, ::-1], src, rd=[t_Hx], wr=[t_Hb], eng='pool')
                    tl_ = order[d][step]
                    P.dma(EV.H_d[s, tl_, :, d * 1024:(d + 1) * 1024], Hb[:, c].rearrange("p a r k -> p (a r k)"),
                          rd=[t_Hb], wr=[t_Hd[s][tl_]], eng='sp')
            P.copy(Hx[:, 0], Hx[:, 32], rd=[t_Hx], wr=[t_Hx])
        P.barrier()
        ss.close()
        sb = _sb0
        if STAGE == 3:
            return
        w_out = sb([128, 8, 1024], BF16)
        gluw = sb([128, 4, 512], BF16)
        qraw = sb([128, 256], BF16)
        t_qraw = Tok()
        qT = sb([128, 4, 256], BF16)
        t_qT = Tok()
        sgA = sb([128, 4, 256], BF16)
        sgS = sb([128, 4, 256], BF16)
        t_sg = Tok()
        Hin = sb([128, 2, 2, 16, 32], BF16)
        t_Hin = Tok()
        Zs = [sb([32, 8, 4, 32], BF16) for _ in range(2)]
        t_Zs = [Tok(), Tok()]
        yT = sb([128, 4, 256], F32)
        t_yT = Tok()
        zT = sb([128, 4, 256], BF16)
        t_zT = Tok()
        sig = sb([128, 256], BF16)
        t_sig = Tok()
        mixT = sb([128, 8, 256], BF16)
        t_mix = Tok()
        kctx = sb([128, 2, 256], BF16)
        kwin = sb([128, 2, 512], BF16)
        t_kw = Tok()
        vctx = sb([128, 2, 130], BF16)
        vwin = sb([128, 4, 130], BF16)
        t_vw = Tok()
        Pt = [sb([128, 5, 128], BF16) for _ in range(2)]
        t_Pt = [Tok(), Tok()]
        o_tm = sb([128, 512], BF16)
        t_otm = Tok()
        rden = sb([128, 8], F32)
        t_rden = Tok()
        tmp = sb([128, 1024], F32)
        t_tmp = Tok()
        g1 = tmp[:].rearrange("p (a b) -> p a b", a=4)
        t_g = t_tmp
        load_weight_bf16(C, w_out, EV.w_out[j], 8, 1024, t_w, [tmp], [t_tmp])
        load_weight_bf16(C, gluw, EV.glu_w[j], 4, 512, t_w, [tmp], [t_tmp])
        hn = [sb([128, 1024], F32) for _ in range(2)]
        t_hn = [Tok(), Tok()]
        hcnt = [0]
        for s in range(NS):
            P.dma(kctx[:], EV.kT_d[s, :, :, 0:256], rd=[t_kd[s]], wr=[t_kw])
            P.dma(vctx[:].rearrange("p a e -> p a e"), EV.V_d[s, 0:2].rearrange("a p e -> p a e"), rd=[t_vd[s]], wr=[t_vw])
            if s == 0:
                norm_tile(0, 0, 2)
            for tl in range(NTL):
                which = 2 if tl == 0 else s
                if tl > 0:
                    load_rope(tl)
                    lo = max(256, tl * 256 - 128)
                    hi = min(TOK, tl * 256 + 384)
                    off = lo - (tl * 256 - 128)
                    P.dma(kwin[:, :, off:off + (hi - lo)], EV.kT_d[s, :, :, lo:hi], rd=[t_kd[s]], wr=[t_kw])
                    n0 = 2 * tl - 1
                    for a in range(4):
                        n = n0 + a
                        if 2 <= n < NSUB:
                            P.dma(vwin[:, a, :], EV.V_d[s, n], rd=[t_vd[s]], wr=[t_vw])
                P.dma(Hin[:].rearrange("p d a r k -> p (d a r k)"), EV.H_d[s, tl], rd=[t_Hd[s][tl]], wr=[t_Hin])
                for m in range(4):
                    b = m % 2
                    proj_fm(lambda k, m=m: w_in[:, k, QO + m * 128:QO + (m + 1) * 128], b)
                    if tl == 0:
                        P.copy(qT[:, m, :], ps[b][:, 0:256], rd=[t_ps[b]], wr=[t_qT])
                    else:
                        P.copy(qraw[:], ps[b][:, 0:256], rd=[t_ps[b]], wr=[t_qraw])
                        rope(qT[:, m, :], qraw[:], t_qraw, tl, t_qT)
                for m in range(4):
                    b = m % 2
                    proj_fm(lambda k, m=m: w_in[:, k, GAO + m * 128:GAO + (m + 1) * 128], b)
                    P.act(sgA[:, m, :], ps[b][:, 0:256], AF.Silu, rd=[t_ps[b]], wr=[t_sg])
                for m in range(4):
                    b = m % 2
                    proj_fm(lambda k, m=m: w_in[:, k, GSO + m * 128:GSO + (m + 1) * 128], b)
                    P.act(sgS[:, m, :], ps[b][:, 0:256], AF.Silu, rd=[t_ps[b]], wr=[t_sg])
                for i in range(4):
                    b = i % 2
                    proj_fm(lambda k, i=i: w_in[:, k, UO + i * 128:UO + (i + 1) * 128], b)
                    P.copy(uT[:, i, :].rearrange("p (s j) -> p j s", s=8), ps[b][:, 0:256].rearrange("p (j s) -> p j s", s=8),
                           rd=[t_ps[b]], wr=[t_uT])
                norm_next(s, tl)
                def z_part(i):
                    zb = (0, 1) if i % 2 == 0 else (4, 5)
                    for q in range(4):
                        pr = 4 * i + q
                        cq = 32 * q if q < 3 else 128
                        bk = zb[q // 2]
                        cnt = 0
                        for d in range(2):
                            for ri in range(2):
                                cnt += 1
                                P.mm(ps[bk][0:32, (q % 2) * 256:(q % 2) * 256 + 256], Hin[:, d, ri, pr, :],
                                     Psi[:, d, ri, pr].rearrange("p t c -> p (t c)"), start=(cnt == 1), stop=(cnt == 4),
                                     rd=[t_Psi, t_Hin], wr=[t_ps[bk]])
                    for hb_ in range(2):
                        P.act(Zs[i % 2][:, :, 2 * hb_:2 * hb_ + 2, :],
                              ps[zb[hb_]][0:32, :].rearrange("p (q t c) -> p t q c", q=2, t=8), AF.Identity,
                              rd=[t_ps[zb[hb_]]], wr=[t_Zs[i % 2]])
                z_part(0)
                for i in range(4):
                    bank = 2 + i % 2
                    if i + 1 < 4:
                        z_part(i + 1)
                    for li_, tau in enumerate([0] + [v for v in range(-7, 8) if v != 0]):
                        t0, t1 = max(0, tau), min(7, 7 + tau)
                        P.mm(ps[bank][:, t0 * 32:(t1 + 1) * 32], Kmat[:, i, 7 + tau, :],
                             uT[:, i, (t0 - tau) * 32:(t1 - tau + 1) * 32], start=(li_ == 0), stop=False,
                             rd=[t_K, t_uT], wr=[t_ps[bank]], sgc=True)
                    for t in range(8):
                        P.mm(ps[bank][:, t * 32:(t + 1) * 32], Zs[i % 2][:, t].rearrange("p q c -> p (q c)"), C.ident[0:32, 0:32],
                             start=False, stop=(t == 7), rd=[t_Zs[i % 2], C.t_ident], wr=[t_ps[bank]], sgc=True)
                    P.stt(yT[:, i, :].rearrange("p (k t) -> p t k", t=8), uT[:, i, :].rearrange("p (t k) -> p t k", t=8),
                          dvec[:, i:i + 1], ps[bank][:, 0:256].rearrange("p (t k) -> p t k", t=8), ALU.mult, ALU.add,
                          rd=[t_uT, t_sm, t_ps[bank]], wr=[t_yT])
                P.tt(g1, yT[:], yT[:], ALU.mult, rd=[t_yT], wr=[t_g])
                P.ts(g1, g1, 0.044715, 1.0, ALU.mult, ALU.add, rd=[t_g], wr=[t_g])
                P.tt(g1, g1, yT[:], ALU.mult, rd=[t_g, t_yT], wr=[t_g])
                P.act(g1, g1, AF.Sigmoid, rd=[t_g], wr=[t_g], scale=1.5957691216057308)
                P.tt(zT[:], g1, yT[:], ALU.mult, rd=[t_g, t_yT], wr=[t_zT])
                def key_blocks(qb):
                    n = 2 * tl + qb
                    kbs = []
                    if tl > 0:
                        for a in range(3):
                            nn = n - 1 + a
                            if 2 <= nn < NSUB:
                                wa = qb + a
                                kbs.append(('w', wa, [0, None, 1][a]))
                    kbs.append(('c', 0, None))
                    kbs.append(('c', 1, None))
                    return kbs

                def att_scores(qb, h):
                    kbs = key_blocks(qb)
                    nb = len(kbs)
                    qs = slice(qb * 128, (qb + 1) * 128)
                    hk, m, half = h // 4, h // 2, h % 2
                    hr = slice(64 * half, 64 * half + 64)
                    pt, tpt = Pt[h % 2], t_Pt[h % 2]
                    for bi in range(nb):
                        kind, wa, mk = kbs[bi]
                        ksrc = kwin[hr, hk, wa * 128:(wa + 1) * 128] if kind == 'w' else kctx[hr, hk, wa * 128:(wa + 1) * 128]
                        bnk = (4 + half) if bi < 4 else (6, 0)[half]
                        col = (bi % 4) * 128
                        P.mm(ps[bnk][:, col:col + 128], ksrc, qT[hr, m, qs], start=True, stop=True, rd=[t_kw, t_qT],
                             wr=[t_ps[bnk]])
                    n4 = min(nb, 4)
                    P.act(pt[:, 0:n4, :], ps[4 + half][:, 0:n4 * 128].rearrange("p (b q) -> p b q", b=n4), AF.Exp,
                          rd=[t_ps[4 + half]], wr=[tpt], scale=0.125)
                    if nb > 4:
                        b5 = (6, 0)[half]
                        P.act(pt[:, 4, :], ps[b5][:, 0:128], AF.Exp, rd=[t_ps[b5]], wr=[tpt], scale=0.125)
                    for bi, (kind, wa, mk) in enumerate(kbs):
                        if mk is not None:
                            P.tt(pt[:, bi, :], pt[:, bi, :], masks[:, mk, :], ALU.mult, rd=[tpt, t_sm], wr=[tpt])

                def att_pv(qb, h):
                    kbs = key_blocks(qb)
                    nb = len(kbs)
                    hk = h // 4
                    pt, tpt = Pt[h % 2], t_Pt[h % 2]
                    ob = 2 + h // 4
                    oc = (h % 4) * 65
                    for bi, (kind, wa, mk) in enumerate(kbs):
                        vsrc = vwin[:, wa, hk * 65:(hk + 1) * 65] if kind == 'w' else vctx[:, wa, hk * 65:(hk + 1) * 65]
                        P.mm(ps[ob][:, oc:oc + 65], pt[:, bi, :], vsrc, start=(bi == 0), stop=(bi == nb - 1),
                             rd=[tpt, t_vw], wr=[t_ps[ob]])

                def att_norm(qb):
                    for hb in range(2):
                        ob = 2 + hb
                        o3 = ps[ob][:, 0:260].rearrange("p (h e) -> p h e", h=4)
                        P.tt(rden[:, hb * 4:(hb + 1) * 4], o3[:, :, 64], esink[:, hb * 4:(hb + 1) * 4], ALU.add,
                             rd=[t_ps[ob], t_sm], wr=[t_rden])
                        P.recip(rden[:, hb * 4:(hb + 1) * 4], rden[:, hb * 4:(hb + 1) * 4], rd=[t_rden], wr=[t_rden])
                        P.tt(o_tm[:, hb * 256:(hb + 1) * 256].rearrange("p (h e) -> p h e", h=4), o3[:, :, 0:64],
                             rden[:, hb * 4:(hb + 1) * 4].unsqueeze(2).broadcast_to([128, 4, 64]), ALU.mult,
                             rd=[t_ps[ob], t_rden], wr=[t_otm])

                def att_finish(qb):
                    qs = slice(qb * 128, (qb + 1) * 128)
                    for m in range(4):
                        P.tr(C.pT[:, m * 128:(m + 1) * 128], o_tm[:, m * 128:(m + 1) * 128], C.ident[:], rd=[t_otm, C.t_ident],
                             wr=[C.t_pT])
                    for m in range(4):
                        P.tt(mixT[:, m, qs], C.pT[:, m * 128:(m + 1) * 128], sgA[:, m, qs], ALU.mult, rd=[C.t_pT, t_sg],
                             wr=[t_mix])

                def glu_all():
                    for m in range(4):
                        b = m % 2
                        for k in range(4):
                            P.mm(ps[b][:, 0:256], gluw[:, k, m * 128:(m + 1) * 128], zT[:, k, :], start=(k == 0), stop=(k == 3),
                                 rd=[t_zT, t_w], wr=[t_ps[b]])
                        P.act(sig[:], ps[b][:, 0:256], AF.Sigmoid, rd=[t_ps[b], t_sm], wr=[t_sig], bias=glub[:, m:m + 1])
                        P.tt(sig[:], sig[:], zT[:, m, :], ALU.mult, rd=[t_sig, t_zT], wr=[t_sig])
                        P.tt(mixT[:, 4 + m, :], sig[:], sgS[:, m, :], ALU.mult, rd=[t_sig, t_sg], wr=[t_mix])

                att_scores(0, 0)
                for h in range(8):
                    if h + 1 < 8:
                        att_scores(0, h + 1)
                    att_pv(0, h)
                att_norm(0)
                att_scores(1, 0)
                att_finish(0)
                for h in range(8):
                    if h + 1 < 8:
                        att_scores(1, h + 1)
                    att_pv(1, h)
                att_norm(1)
                glu_all()
                att_finish(1)
                for sub in range(2):
                    n = 2 * tl + sub
                    slot = gslot(s, tl, sub)
                    hi_ = hcnt[0] % 2
                    hcnt[0] += 1
                    for nb2 in range(2):
                        for k in range(8):
                            P.mm(ps[nb2][:, :], mixT[:, k, sub * 128:(sub + 1) * 128], w_out[:, k, nb2 * 512:(nb2 + 1) * 512],
                                 start=(k == 0), stop=(k == 7), rd=[t_mix, t_w], wr=[t_ps[nb2]])
                        sl = slice(nb2 * 512, (nb2 + 1) * 512)
                        P.tt(tmp[:, sl], ps[nb2][:, :], C.gate_bc[:, which, sl], ALU.mult, rd=[t_ps[nb2], C.t_mod], wr=[t_tmp])
                        P.tt(hn[hi_][:, sl], tmp[:, sl], C.hT[slot][:, sl], ALU.add, rd=[t_tmp, C.t_hT[slot]],
                             wr=[t_hn[hi_]], eng='pool')
                    C.store_h(idx, s, n, hn[hi_], t_hn[hi_], last)
                if not PIPE:
                    norm_next(s, tl, force=True)
        P.barrier()
```
